# Optimizing a Trainium2 kernel written in Bass

```python
import jax, jax.numpy as jnp
from jax import lax
import numpy as np

D_MODEL = 2048
BATCH = 4
SEQ = 8192
DEPTH = 1

HEAD_DIM = 64
N_HEADS_A = D_MODEL // (2 * HEAD_DIM)
N_KV_A = N_HEADS_A // 8
N_HEADS_B = D_MODEL // (2 * HEAD_DIM)
WINDOW_A = 128
DILATED_BRANCHES = ((128, 1), (512, 4), (2048, 16))
D_FF = 4 * D_MODEL
BLOCK = 128
EPS = 1e-5
NEG_INF = -1e30

Q_A = N_HEADS_A * HEAD_DIM
KV_A = N_KV_A * HEAD_DIM
Q_B = N_HEADS_B * HEAD_DIM
D_IN = Q_A + 2 * KV_A + 3 * Q_B
D_MIX = Q_A + Q_B

kernel_name = "hybrid_swa_sink_dilated_alibi_block"


def alibi_slopes(n):
    return jnp.asarray(2.0 ** (-8.0 * (np.arange(n) + 1) / n), dtype=jnp.float32)


def rmsnorm(x, g):
    x32 = x.astype(jnp.float32)
    y = x32 * lax.rsqrt(jnp.mean(x32 * x32, axis=-1, keepdims=True) + EPS)
    return y.astype(x.dtype) * g


def _with_prev_block(t, nb):
    b, L, G, Dh = t.shape
    tb = t.reshape(b, nb, BLOCK, G, Dh)
    prev = jnp.concatenate([jnp.zeros_like(tb[:, :1]), tb[:, :-1]], axis=1)
    return jnp.concatenate([prev, tb], axis=2)


def banded_attention(q, k, v, max_steps, step_dist, slopes, sinks):
    b, L, H, Dh = q.shape
    G = k.shape[2]
    R = H // G
    nb = L // BLOCK
    qb = q.reshape(b, nb, BLOCK, G, R, Dh)
    kb = _with_prev_block(k, nb)
    vb = _with_prev_block(v, nb)
    s = jnp.einsum('bnqgrd,bnkgd->bngrqk', qb, kb).astype(jnp.float32) * (Dh ** -0.5)
    qi = jnp.arange(BLOCK)[:, None]
    kj = jnp.arange(2 * BLOCK)[None, :]
    steps = qi + BLOCK - kj
    kpos = jnp.arange(nb)[:, None, None] * BLOCK + kj[None] - BLOCK
    valid = (steps >= 0) & (steps <= max_steps) & (kpos >= 0)
    alibi = slopes.reshape(G, R, 1, 1) * (step_dist * steps).astype(jnp.float32)
    s = jnp.where(valid[None, :, None, None], s - alibi[None, None], NEG_INF)
    m = jnp.max(s, axis=-1)
    if sinks is not None:
        sink = sinks.astype(jnp.float32).reshape(G, R, 1)
        m = jnp.maximum(m, sink)
    p = jnp.exp(s - m[..., None])
    denom = jnp.sum(p, axis=-1)
    if sinks is not None:
        denom = denom + jnp.exp(sink - m)
    o = jnp.einsum('bngrqk,bnkgd->bnqgrd', p, vb.astype(jnp.float32))
    o = o / jnp.moveaxis(denom, -1, 2)[..., None]
    lse = jnp.moveaxis(m + jnp.log(denom), -1, 2)
    return o.reshape(b, L, H, Dh), lse.reshape(b, L, H)


def _strided(t, dil, Lp):
    b, S, H, Dh = t.shape
    L = S // dil
    t = t.reshape(b, L, dil, H, Dh).transpose(0, 2, 1, 3, 4).reshape(b * dil, L, H, Dh)
    return jnp.pad(t, ((0, 0), (0, Lp - L), (0, 0), (0, 0)))


def dilated_mixture(q, k, v, slopes):
    b, S, H, Dh = q.shape
    outs, lses = [], []
    for window, dil in DILATED_BRANCHES:
        L = S // dil
        Lp = -(-L // BLOCK) * BLOCK
        o, lse = banded_attention(_strided(q, dil, Lp), _strided(k, dil, Lp),
                                  _strided(v, dil, Lp), window // dil, dil, slopes, None)
        outs.append(o[:, :L].reshape(b, dil, L, H, Dh).transpose(0, 2, 1, 3, 4).reshape(b, S, H, Dh))
        lses.append(lse[:, :L].reshape(b, dil, L, H).transpose(0, 2, 1, 3).reshape(b, S, H))
    w = jax.nn.softmax(jnp.stack(lses), axis=0)
    return jnp.einsum('nbsh,nbshd->bshd', w, jnp.stack(outs)).astype(q.dtype)


def setup_inputs(seed: int = 0) -> dict:
    key = jax.random.key(seed)
    ks = jax.random.split(key, 12)
    f32 = jnp.float32
    x = jax.random.normal(ks[0], (BATCH, SEQ, D_MODEL), f32)
    g_attn = 1.0 + 0.02 * jax.random.normal(ks[1], (DEPTH, D_MODEL), f32)
    w_in = jax.random.normal(ks[2], (DEPTH, D_MODEL, D_IN), f32) * D_MODEL ** -0.5
    b_in = 0.02 * jax.random.normal(ks[3], (DEPTH, D_IN), f32)
    sinks_a = jax.random.normal(ks[4], (DEPTH, N_HEADS_A), f32)
    g_out_a = 1.0 + 0.02 * jax.random.normal(ks[5], (DEPTH, Q_A), f32)
    g_out_b = 1.0 + 0.02 * jax.random.normal(ks[6], (DEPTH, Q_B), f32)
    w_out = jax.random.normal(ks[7], (DEPTH, D_MIX, D_MODEL), f32) * D_MIX ** -0.5
    g_mlp = 1.0 + 0.02 * jax.random.normal(ks[8], (DEPTH, D_MODEL), f32)
    w_1 = jax.random.normal(ks[9], (DEPTH, D_MODEL, D_FF), f32) * D_MODEL ** -0.5
    w_2 = jax.random.normal(ks[10], (DEPTH, D_FF, D_MODEL), f32) * D_FF ** -0.5
    g_final = 1.0 + 0.02 * jax.random.normal(ks[11], (D_MODEL,), f32)
    return {"x": x, "g_attn": g_attn, "w_in": w_in, "b_in": b_in, "sinks_a": sinks_a,
            "g_out_a": g_out_a, "g_out_b": g_out_b, "w_out": w_out, "g_mlp": g_mlp,
            "w_1": w_1, "w_2": w_2, "g_final": g_final}


def reference(x, g_attn, w_in, b_in, sinks_a, g_out_a, g_out_b, w_out, g_mlp, w_1, w_2, g_final):
    b, S, _ = x.shape
    slopes_a = alibi_slopes(N_HEADS_A)
    slopes_b = alibi_slopes(N_HEADS_B)
    for l in range(DEPTH):
        h = rmsnorm(x, g_attn[l])
        proj = jnp.einsum('bsd,de->bse', h, w_in[l]) + b_in[l]
        o1 = Q_A
        o2 = o1 + KV_A
        o3 = o2 + KV_A
        o4 = o3 + Q_B
        o5 = o4 + Q_B
        qa = proj[..., :o1].reshape(b, S, N_HEADS_A, HEAD_DIM)
        ka = proj[..., o1:o2].reshape(b, S, N_KV_A, HEAD_DIM)
        va = proj[..., o2:o3].reshape(b, S, N_KV_A, HEAD_DIM)
        qb = proj[..., o3:o4].reshape(b, S, N_HEADS_B, HEAD_DIM)
        kb = proj[..., o4:o5].reshape(b, S, N_HEADS_B, HEAD_DIM)
        vb = proj[..., o5:].reshape(b, S, N_HEADS_B, HEAD_DIM)
        oa, _ = banded_attention(qa, ka, va, WINDOW_A - 1, 1, slopes_a, sinks_a[l])
        ya = rmsnorm(oa.astype(x.dtype).reshape(b, S, Q_A), g_out_a[l])
        ob = dilated_mixture(qb, kb, vb, slopes_b)
        yb = rmsnorm(ob.reshape(b, S, Q_B), g_out_b[l])
        mix = jnp.concatenate([ya, yb], axis=-1)
        x = x + jnp.einsum('bse,ed->bsd', mix, w_out[l])
        h = rmsnorm(x, g_mlp[l])
        u = jax.nn.relu(jnp.einsum('bsd,df->bsf', h, w_1[l]))
        x = x + jnp.einsum('bsf,fd->bsd', u * u, w_2[l])
    return rmsnorm(x, g_final)
```

```python
import numpy as np
import concourse.bass as bass
import concourse.mybir as mybir
from concourse.bass_utils import run_bass_kernel_spmd

F32 = mybir.dt.float32
BF16 = mybir.dt.bfloat16
I32 = mybir.dt.int32
AF = mybir.ActivationFunctionType
ALU = mybir.AluOpType

D = 2048
NOWN = 4096
HALO = 2048
NTOK = 6144
DIN = 4352
DFF = 8192
EPS = 1e-5
NEG = -30000.0
SLOPES = [2.0 ** (-8.0 * (h + 1) / 16) for h in range(16)]
N_CORES = 8
P2_STAGE = 9


class Sched:
    ENGS = ("pe", "act", "dve", "pool", "sp")

    def __init__(self, nc):
        self.nc = nc
        self.ops = []
        self.last_w = {}
        self.readers = {}
        self.chan = {}
        self.chan_last = {}
        self.eng_last = {}
        self.pending = {}

    def _add(self, eng, fn, reads, writes, dma_chan=None):
        idx = len(self.ops)
        deps = set()
        for r in reads:
            if r in self.last_w:
                deps.add((self.last_w[r], "raw"))
        for w in writes:
            if w in self.last_w:
                deps.add((self.last_w[w], "waw"))
            for rd in self.readers.get(w, {}).values():
                deps.add((rd, "war"))
        if eng in self.pending:
            for j in self.pending.pop(eng):
                deps.add((j, "raw"))
        op = dict(eng=eng, fn=fn, deps=deps, dma=dma_chan, sig=False)
        if dma_chan is not None:
            self.chan[dma_chan] = self.chan.get(dma_chan, 0) + 16
            op["chan_val"] = self.chan[dma_chan]
            self.chan_last[dma_chan] = idx
        self.eng_last[eng] = idx
        self.ops.append(op)
        rkey = eng if dma_chan is None else ("dma", dma_chan)
        for r in reads:
            self.readers.setdefault(r, {})[rkey] = idx
        for w in writes:
            self.last_w[w] = idx
            self.readers[w] = {}
        return idx

    def op(self, eng, fn, reads=(), writes=()):
        return self._add(eng, fn, tuple(reads), tuple(writes))

    def dma(self, eng, chan, out, in_, reads=(), writes=(), **kw):
        def fn(e, out=out, in_=in_, kw=kw):
            return e.dma_start(out=out, in_=in_, **kw)
        return self._add(eng, fn, tuple(reads), tuple(writes), dma_chan=chan)

    def barrier(self):
        deps = list(self.eng_last.values()) + list(self.chan_last.values())
        for e in self.ENGS:
            self.pending[e] = list(set(self.pending.get(e, []) + deps))

    def emit(self, final_wait_chans=()):
        from contextlib import ExitStack
        nc = self.nc
        ops = self.ops
        for op in ops:
            waits = []
            for (j, kind) in op["deps"]:
                J = ops[j]
                if J["dma"] is not None:
                    waits.append(("chan", J["dma"], J["chan_val"]))
                elif J["eng"] == op["eng"]:
                    if op["eng"] == "pe" or kind == "war":
                        continue
                    J["sig"] = True
                    waits.append(("eng", J["eng"], j))
                else:
                    J["sig"] = True
                    waits.append(("eng", J["eng"], j))
            op["waits"] = waits
        cnt = {e: 0 for e in self.ENGS}
        for op in ops:
            if op["sig"] and op["dma"] is None:
                cnt[op["eng"]] += 1
                op["sigval"] = cnt[op["eng"]]
        per_eng = {e: [] for e in self.ENGS}
        waited = {e: {} for e in self.ENGS}
        for op in ops:
            w2 = {}
            for w in op["waits"]:
                if w[0] == "chan":
                    key, val = ("chan", w[1]), w[2]
                else:
                    key, val = ("eng", w[1]), ops[w[2]]["sigval"]
                if waited[op["eng"]].get(key, 0) >= val:
                    continue
                w2[key] = max(w2.get(key, 0), val)
            for k, v in w2.items():
                waited[op["eng"]][k] = v
            op["w2"] = w2
            per_eng[op["eng"]].append(op)
        self.stats = dict(n_ops=len(ops), n_sem=len(self.chan) + 5,
                          per_eng={e: len(v) for e, v in per_eng.items()}, sig=dict(cnt))
        with ExitStack() as st:
            sems = {}
            for e in self.ENGS:
                sems[("eng", e)] = st.enter_context(nc.semaphore("s_" + e))
            for c in self.chan:
                sems[("chan", c)] = st.enter_context(nc.semaphore("c_" + str(c)))
            block = st.enter_context(nc.Block())

            def run(engobj, lst):
                for op in lst:
                    for k, v in op["w2"].items():
                        engobj.wait_ge(sems[k], v)
                    ins = op["fn"](engobj)
                    if op["dma"] is not None:
                        ins.then_inc(sems[("chan", op["dma"])], 16)
                    elif op["sig"]:
                        ins.then_inc(sems[("eng", op["eng"])], 1)

            @block.tensor
            def _(e):
                run(e, per_eng["pe"])

            @block.scalar
            def _(e):
                run(e, per_eng["act"])

            @block.vector
            def _(e):
                run(e, per_eng["dve"])

            @block.gpsimd
            def _(e):
                run(e, per_eng["pool"])

            @block.sync
            def _(e):
                run(e, per_eng["sp"])
                for c in final_wait_chans:
                    e.wait_ge(sems[("chan", c)], self.chan[c])


class Arena:
    def __init__(self, nc):
        self.nc = nc
        self.lo = ((nc.SBUF_PARTITION_SIZE_BYTES - nc.sbuf_bytes_remaining + 63) // 64) * 64
        self.hi = nc.SBUF_PARTITION_SIZE_BYTES
        self.cur = self.lo
        self.n = 0

    def alloc(self, name, shape, dt):
        nbytes = int(np.prod(shape[1:])) * (4 if dt in (F32, I32) else 2)
        nbytes = ((nbytes + 63) // 64) * 64
        off = self.cur
        assert off + nbytes <= self.hi, f"SBUF overflow at {name}: {off + nbytes} > {self.hi}"
        self.cur += nbytes
        self.n += 1
        return self.nc.alloc_sbuf_tensor_at(f"{name}_{self.n}", list(shape), dt, offset=off)

    def mark(self):
        return self.cur

    def reset(self, m):
        self.cur = m


class Rot:
    def __init__(self, items):
        self.items = list(items)
        self.i = 0

    def next(self):
        v = self.items[self.i % len(self.items)]
        self.i += 1
        return v


def MM(o, l, r, start, stop, skip=False):
    return lambda e: e.matmul(o, lhsT=l, rhs=r, start=start, stop=stop, skip_group_check=skip)


def ACTF(o, i, func, bias=None, scale=1.0, accum=None):
    def f(e):
        kw = {}
        if bias is not None:
            kw["bias"] = bias
        if accum is not None:
            kw["accum_out"] = accum
        return e.activation(out=o, in_=i, func=func, scale=scale, **kw)
    return f


def TT(o, a, b, op):
    return lambda e: e.tensor_tensor(out=o, in0=a, in1=b, op=op)


def STT(o, a, s, b, op0, op1):
    return lambda e: e.scalar_tensor_tensor(out=o, in0=a, scalar=s, in1=b, op0=op0, op1=op1)


def TS(o, a, s1, s2, op0, op1=None):
    if op1 is None:
        return lambda e: e.tensor_scalar(out=o, in0=a, scalar1=s1, scalar2=None, op0=op0)
    return lambda e: e.tensor_scalar(out=o, in0=a, scalar1=s1, scalar2=s2, op0=op0, op1=op1)


def CP(o, i):
    return lambda e: e.tensor_copy(out=o, in_=i)


def ACP(o, i):
    return lambda e: e.copy(out=o, in_=i)


def RCP(o, i):
    return lambda e: e.reciprocal(out=o, in_=i)


def MS(ap, v):
    return lambda e: e.memset(ap, v)


def build(phases=(0, 1, 2, 3), dbg=False, n_chunks=8, p2_items=None):
    nc = bass.Bass("TRN2", target_bir_lowering=False)

    def dram(name, shape, dt, kind):
        return nc.dram_tensor(name, list(shape), dt, kind=kind).ap()

    IN, OUT, INT = "ExternalInput", "ExternalOutput", "Internal"
    SCR = OUT if dbg else INT
    P = set(phases)
    xc = dram("xc", [NTOK, D], F32, IN) if P & {1, 3} else None
    if 0 in P:
        w_in = dram("w_in", [D, DIN], F32, IN)
        w_out = dram("w_out", [D, D], F32, IN)
        w_1 = dram("w_1", [D, DFF], F32, IN)
        w_2 = dram("w_2", [DFF, D], F32, IN)
    if 1 in P:
        g_attn = dram("g_attn", [D], F32, IN)
        b_in = dram("b_in", [DIN], F32, IN)
    if 3 in P:
        sinks = dram("sinks", [16], F32, IN)
        g_oab = dram("g_oab", [2048], F32, IN)
        g_mlp = dram("g_mlp", [D], F32, IN)
        g_fin = dram("g_fin", [D], F32, IN)
    hb = dram("hb", [128, 1], F32, IN)
    out = dram("out", [NOWN, D], F32, OUT)

    win_s = dram("win_s", [D, DIN], BF16, INT)
    wout_s = dram("wout_s", [D, D], BF16, INT)
    w1_s = dram("w1_s", [D, DFF], BF16, INT)
    w2_s = dram("w2_s", [DFF, D], BF16, INT)
    qta = dram("qta", [8, 128, NOWN], BF16, SCR)
    qtb = dram("qtb", [8, 128, NOWN], BF16, SCR)
    ktb = dram("ktb", [8, 128, NTOK], BF16, SCR)
    kta = dram("kta", [2, 128, NTOK], BF16, SCR)
    va = dram("va", [NTOK, 128], BF16, SCR)
    vb = dram("vb", [NTOK, 1024], BF16, SCR)
    opart = dram("opart", [4, NOWN, 8, 130], F32, SCR)

    S = Sched(nc)
    A = Arena(nc)
    PBALL = nc.alloc_psum_tensor("pball", [128, 4096], F32)
    PB = [PBALL[:, 512 * i:512 * i + 512] for i in range(8)]
    PS2 = [PBALL[:, 1024 * i:1024 * i + 1024] for i in range(2)]

    def pbf(i):
        return PB[i].bitcast(BF16)

    ident = A.alloc("ident", [128, 128], BF16)
    zcol = A.alloc("zcol", [128, 1], F32)
    hbt = A.alloc("hbt", [128, 1], F32)
    epsc = A.alloc("epsc", [128, 1], F32)
    S.op("pool", MS(ident[:], 0.0), writes=["ident"])
    S.op("pool", lambda e: e.affine_select(out=ident[:], in_=ident[:], pattern=[[-1, 128]],
                                           compare_op=ALU.not_equal, fill=1.0, base=0,
                                           channel_multiplier=1),
         reads=["ident"], writes=["ident"])
    S.op("pool", MS(zcol[:], 0.0), writes=["zcol"])
    S.op("pool", MS(epsc[:], EPS), writes=["epsc"])
    S.dma("sp", "misc", hbt[:], hb, writes=["hbt", "misc"])
    base_mark = A.mark()

    CVW = 2176
    cv32 = [A.alloc(f"cv32_{i}", [128, CVW], F32) for i in range(2)]
    cvbf = [A.alloc(f"cvbf_{i}", [128, CVW], BF16) for i in range(2)]
    p12_mark = A.mark()
    pieces = []
    if 0 in P:
        for (nm, src, dst, R, C, pw) in (("win", w_in, win_s, D, DIN, 2176), ("wout", w_out, wout_s, D, D, 2048),
                                         ("w1", w_1, w1_s, D, DFF, 2048), ("w2", w_2, w2_s, DFF, D, 2048)):
            for rc in range(R // 128):
                for c0 in range(0, C, pw):
                    pieces.append((nm, src[rc * 128:(rc + 1) * 128, c0:c0 + pw],
                                   dst[rc * 128:(rc + 1) * 128, c0:c0 + pw], pw))
    n_win = sum(1 for p_ in pieces if p_[0] == "win")
    cvs = dict(idx=0, pending=None, engs=("pool", "dve", "pool", "act"))

    def CV(nm):
        return [f"cvst_{s}_{nm}" for s in range(2)]

    def pump(n=1):
        for _ in range(n):
            if cvs["pending"] is not None:
                k = cvs["pending"]
                nm, src, dst, w = pieces[k]
                s = k % 2
                eng = cvs["engs"][k % len(cvs["engs"])]
                if eng == "act":
                    S.op("act", ACP(cvbf[s][:, :w], cv32[s][:, :w]), reads=[f"cv32_{s}"], writes=[f"cvbf_{s}"])
                else:
                    S.op(eng, CP(cvbf[s][:, :w], cv32[s][:, :w]), reads=[f"cv32_{s}"], writes=[f"cvbf_{s}"])
                S.dma("pool", f"cvst_{s}", dst, cvbf[s][:, :w], reads=[f"cvbf_{s}"], writes=[f"cvst_{s}_{nm}"])
                cvs["pending"] = None
            if cvs["idx"] < len(pieces):
                k = cvs["idx"]
                nm, src, dst, w = pieces[k]
                s = k % 2
                S.dma("sp", f"cv32_{s}", cv32[s][:, :w], src, writes=[f"cv32_{s}"])
                cvs["pending"] = k
                cvs["idx"] += 1

    def conv_drain():
        while cvs["idx"] < len(pieces) or cvs["pending"] is not None:
            pump()

    cvs["engs"] = ("pool", "dve", "act")
    while cvs["idx"] < n_win:
        pump()
    pump()
    cvs["engs"] = ("pool", "dve", "pool", "act")

    def rstd_ops(ssap, rsap, n, reads, writes):
        S.op("act", ACTF(rsap, ssap, AF.Sqrt, bias=epsc[:, 0:1], scale=1.0 / n), reads=list(reads) + ["epsc"],
             writes=writes)
        S.op("dve", RCP(rsap, rsap), reads=writes, writes=writes)

    def transposes(src_bf, src_res, dstT, dst_res_fn, tcol, trot, evrot):
        for g in range(4):
            bk = trot.next()
            for j in range(4):
                kc = 4 * g + j
                o = pbf(bk)[:, j * 128:(j + 1) * 128]
                i_ = src_bf[:, kc * 128:(kc + 1) * 128]
                S.op("pe", (lambda o, i_: (lambda e: e.transpose(out=o, in_=i_, identity=ident[:])))(o, i_),
                     reads=[src_res, "ident"], writes=[f"ps{bk}"])
            ev = evrot.next()
            o = dstT[:, 4 * g:4 * g + 4, tcol:tcol + 128]
            i_ = pbf(bk)[:, 0:512].rearrange("p (a b) -> p a b", a=4)
            if ev == "dve":
                S.op("dve", CP(o, i_), reads=[f"ps{bk}"], writes=[dst_res_fn(g)])
            else:
                S.op("act", ACP(o, i_), reads=[f"ps{bk}"], writes=[dst_res_fn(g)])

    def load_wpiece(wsl, slot, scr, wname, r0, c0, ncols, dcol0=0):
        S.dma("sp", f"w{slot}", wsl[slot][:, :, dcol0:dcol0 + ncols],
              scr[r0:r0 + 2048, c0:c0 + ncols].rearrange("(k p) c -> p k c", p=128),
              reads=CV(wname), writes=[f"w{slot}"])

    def drain(gen):
        if gen is not None:
            for _ in gen:
                pass

    if 1 in phases:
        hT = [A.alloc(f"hT{i}", [128, 16, 1024], BF16) for i in range(2)]
        xs = [A.alloc(f"xs{i}", [128, 2048], F32) for i in range(2)]
        junk = A.alloc("junk", [128, 2048], BF16)
        hbf = [A.alloc(f"hbf{i}", [128, 2048], BF16) for i in range(2)]
        gat = A.alloc("gat", [128, 2048], F32)
        ss = A.alloc("ss", [128, 2], F32)
        rs = A.alloc("rs", [128, 2], F32)
        binT = A.alloc("binT", [128, 34], F32)
        binT8 = A.alloc("binT8", [128, 34], F32)
        bka = A.alloc("bka", [128, 2], F32)
        bv = A.alloc("bv", [128, 1152], F32)
        wsl = [A.alloc(f"wsl{i}", [128, 16, 512], BF16) for i in range(3)]
        qst = [A.alloc(f"qst{i}", [128, 1024], BF16) for i in range(3)]
        vst = [A.alloc(f"vst{i}", [128, 512], BF16) for i in range(4)]

        S.dma("sp", "misc", gat[:], g_attn.partition_broadcast(128), writes=["gat", "misc"])
        S.dma("sp", "misc", binT[:], b_in.rearrange("(c p) -> p c", p=128), writes=["binT", "misc"],
              allow_slow_non_contiguous=True)
        for g in range(2):
            for hh in range(2):
                S.dma("sp", "misc", bka[64 * hh:64 * hh + 64, g:g + 1],
                      b_in[1024 + 64 * g:1024 + 64 * g + 64].rearrange("(p o) -> p o", o=1),
                      writes=["bka", "misc"])
        S.dma("sp", "misc", bv[:, 0:1024], b_in[3328:4352].partition_broadcast(128), writes=["bv", "misc"])
        S.dma("sp", "misc", bv[:, 1024:1152], b_in[1152:1280].partition_broadcast(128), writes=["bv", "misc"])
        S.op("dve", TS(binT8[:], binT[:], 0.125, None, ALU.mult), reads=["binT", "misc"], writes=["binT8"])
        trot = Rot([0, 1])
        mrot = Rot([2, 3, 4, 5, 6, 7])
        evrot = Rot(["dve", "act"])
        wrot = Rot([0, 1, 2])
        qrot = Rot([0, 1, 2])
        vrot = Rot([0, 1, 2, 3])
        NCH1 = 6

        def p1_prologue(c):
            T0 = 1024 * c
            hb_ = c % 2
            for tt in range(8):
                s = tt % 2
                S.dma("sp", f"xs{s}", xs[s][:], xc[T0 + tt * 128:T0 + (tt + 1) * 128, :], writes=[f"xs{s}"])
                S.op("act", ACTF(junk[:], xs[s][:], AF.Square, accum=ss[:, s:s + 1]),
                     reads=[f"xs{s}"], writes=[f"ss{s}"])
                rstd_ops(ss[:, s:s + 1], rs[:, s:s + 1], D, [f"ss{s}"], [f"rs{s}"])
                S.op("dve", STT(hbf[s][:], xs[s][:], rs[:, s:s + 1], gat[:], ALU.mult, ALU.mult),
                     reads=[f"xs{s}", f"rs{s}", "gat", "misc"], writes=[f"hbf{s}"])
                transposes(hbf[s], f"hbf{s}", hT[hb_], lambda g, tt=tt: f"hT{hb_}_{tt}_{g}", tt * 128, trot, evrot)
                yield

        def fm_group(c, slot, wc0, bias_col, scale, dstap):
            hb_ = c % 2
            q = qrot.next()
            for tq in range(2):
                bk = mrot.next()
                for kc in range(16):
                    S.op("pe", MM(PB[bk][:, :], wsl[slot][:, kc, wc0:wc0 + 128],
                                  hT[hb_][:, kc, tq * 512:(tq + 1) * 512], kc == 0, kc == 15),
                         reads=[f"w{slot}"] + [f"hT{hb_}_{4 * tq + t}_{kc // 4}" for t in range(4)],
                         writes=[f"ps{bk}"])
                S.op("act", ACTF(qst[q][:, tq * 512:(tq + 1) * 512], PB[bk][:, :], AF.Identity,
                                 bias=bias_col, scale=scale),
                     reads=[f"ps{bk}", "binT8", "misc"], writes=[f"qst{q}"])
            S.dma("pool", f"qst{q}", dstap, qst[q][:], reads=[f"qst{q}"], writes=[f"qstd{q}"])

        def tm_piece(c, slot, wc0, n, dstap_fn, bcol0, tick):
            hb_ = c % 2
            for tt in range(8):
                bk = mrot.next()
                v = vrot.next()
                for kc in range(16):
                    S.op("pe", MM(PB[bk][:, 0:n], hT[hb_][:, kc, tt * 128:(tt + 1) * 128],
                                  wsl[slot][:, kc, wc0:wc0 + n], kc == 0, kc == 15),
                         reads=[f"w{slot}", f"hT{hb_}_{tt}_{kc // 4}"], writes=[f"ps{bk}"])
                S.op("dve", TT(vst[v][:, 0:n], PB[bk][:, 0:n], bv[:, bcol0:bcol0 + n], ALU.add),
                     reads=[f"ps{bk}", "bv", "misc"], writes=[f"vst{v}"])
                S.dma("pool", f"vst{v}", dstap_fn(tt), vst[v][:, 0:n], reads=[f"vst{v}"], writes=[f"vstd{v}"])
                if tt % 2 == 1:
                    tick()

        drain(p1_prologue(0))
        for c in range(NCH1):
            T0 = 1024 * c
            own = c >= 2
            nxt = p1_prologue(c + 1) if c + 1 < NCH1 else None
            tk = [0]

            def tick():
                pump(1)
                tk[0] += 1
                if nxt is not None and tk[0] % 3 == 0:
                    next(nxt, None)

            for p in range(2):
                slot = wrot.next()
                load_wpiece(wsl, slot, win_s, "win", 0, 2304 + 512 * p, 512)
                for j in range(4):
                    fc = 4 * p + j
                    fm_group(c, slot, 128 * j, binT[:, 18 + fc:18 + fc + 1], 1.0, ktb[fc, :, T0:T0 + 1024])
                    tick()
            for p in range(2):
                slot = wrot.next()
                load_wpiece(wsl, slot, win_s, "win", 0, 3328 + 512 * p, 512)
                tm_piece(c, slot, 0, 512,
                         lambda tt, T0=T0, p=p: vb[T0 + tt * 128:T0 + (tt + 1) * 128, 512 * p:512 * p + 512],
                         512 * p, tick)
            slot = wrot.next()
            for g in range(2):
                for hh in range(2):
                    load_wpiece(wsl, slot, win_s, "win", 0, 1024 + 64 * g, 64, dcol0=128 * g + 64 * hh)
            load_wpiece(wsl, slot, win_s, "win", 0, 1152, 128, dcol0=256)
            for g in range(2):
                fm_group(c, slot, 128 * g, bka[:, g:g + 1], 1.0, kta[g, :, T0:T0 + 1024])
                tick()
            tm_piece(c, slot, 256, 128, lambda tt, T0=T0: va[T0 + tt * 128:T0 + (tt + 1) * 128, :], 1024, tick)
            if own:
                for (c0, dst, b0) in ((0, qta, 0), (1280, qtb, 10)):
                    for p in range(2):
                        slot = wrot.next()
                        load_wpiece(wsl, slot, win_s, "win", 0, c0 + 512 * p, 512)
                        for j in range(4):
                            fc = 4 * p + j
                            fm_group(c, slot, 128 * j, binT8[:, b0 + fc:b0 + fc + 1], 0.125,
                                     dst[fc, :, T0 - HALO:T0 - HALO + 1024])
                            tick()
            drain(nxt)
        conv_drain()
        S.barrier()
    else:
        conv_drain()
        S.barrier()
    A.reset(p12_mark)

    if 2 in phases:
        qT = [A.alloc(f"qT{i}", [128, NOWN], BF16) for i in range(2)]
        kT = [A.alloc(f"kT{i}", [128, NTOK], BF16) for i in range(2)]
        vS = [A.alloc(f"vS{i}", [128, 48, 130], BF16) for i in range(2)]
        stp_i = A.alloc("stp_i", [128, 256], I32)
        stp = A.alloc("stp", [128, 256], F32)
        vbA = A.alloc("vbA", [128, 256], F32)
        vbB = A.alloc("vbB", [128, 256], F32)
        bias2 = [A.alloc(f"bias2_{i}", [128, 512], F32) for i in range(2)]
        s32 = [A.alloc(f"s32_{i}", [128, 512], F32) for i in range(2)]
        pT = [A.alloc(f"pT{i}", [128, 512], BF16) for i in range(3)]
        ost = [A.alloc(f"ost{i}", [128, 130], F32) for i in range(4)]

        S.op("pool", lambda e: e.iota(stp_i[:], pattern=[[1, 256]], base=0, channel_multiplier=-1),
             writes=["stp_i"])
        S.op("pool", CP(stp[:], stp_i[:]), reads=["stp_i"], writes=["stp"])
        for (t, ms, nm) in ((vbA, 127, "vbA"), (vbB, 128, "vbB")):
            S.op("pool", MS(t[:], 0.0), writes=[nm])
            S.op("pool", (lambda t: (lambda e: e.affine_select(
                out=t[:], in_=t[:], pattern=[[1, 256]], compare_op=ALU.is_ge, fill=NEG, base=0,
                channel_multiplier=-1)))(t), reads=[nm], writes=[nm])
            S.op("pool", (lambda t, ms: (lambda e: e.affine_select(
                out=t[:], in_=t[:], pattern=[[-1, 256]], compare_op=ALU.is_ge, fill=NEG, base=ms,
                channel_multiplier=1)))(t, ms), reads=[nm], writes=[nm])
        for i in range(2):
            S.op("pool", MS(vS[i][:, :, 0:1], 1.0), writes=[f"vS{i}"])
            S.op("pool", MS(vS[i][:, :, 129:130], 1.0), writes=[f"vS{i}"])

        passes = [("A", 1, 0), ("B", 1, 1), ("B", 4, 2), ("B", 16, 3)]
        items = [(pi, hp) for pi in range(4) for hp in range(8)]
        if p2_items is not None:
            items = p2_items
        srot = Rot([0, 1])
        s3rot = Rot([0, 1])
        prot = Rot([0, 1, 2])
        orot = Rot([0, 1, 2, 3])
        obank = Rot([4, 5, 6, 7])

        def p2_loads(it, pi, hp):
            kind, d, pidx = passes[pi]
            b = it % 2
            isA = kind == "A"
            nt = 48 // d
            S.dma("sp", f"qT{b}", qT[b][:], (qta if isA else qtb)[hp], writes=[f"qT{b}"])
            S.dma("sp", f"kT{b}", kT[b][:], kta[hp // 4] if isA else ktb[hp], writes=[f"kT{b}"])
            for r in range(d):
                if isA:
                    g = hp // 4
                    src = va[:, 64 * g:64 * g + 64].rearrange("(jt m) c -> m jt c", m=128)
                    S.dma("sp", f"vS{b}", vS[b][:, 0:48, 1:65], src, writes=[f"vS{b}"])
                else:
                    src = vb[r::d, 128 * hp:128 * hp + 128].rearrange("(jt m) c -> m jt c", m=128)
                    S.dma("sp", f"vS{b}", vS[b][:, r * nt:(r + 1) * nt, 1:129], src, writes=[f"vS{b}"])

        def p2_compute(it, pi, hp):
            kind, d, pidx = passes[pi]
            b = it % 2
            isA = kind == "A"
            nt = 48 // d
            jh = 16 // d
            vbt = vbA if isA else vbB
            for h in range(2):
                S.op("dve", STT(bias2[b][:, 256 * h:256 * h + 256], stp[:], -SLOPES[2 * hp + h] * d, vbt[:],
                                ALU.mult, ALU.add), reads=["stp", "vbA", "vbB"], writes=[f"bias2_{b}"])

            def score(r, jt):
                n0 = 128 if jt == jh - 1 else 0
                n1 = 128 if jt == nt - 1 else 256
                sb = srot.next()
                ks = r + 128 * d * jt
                qs = r + d * (128 * jt + n0) - HALO
                nq = n1 - n0
                for h in range(2):
                    S.op("pe", MM(PS2[sb][:, 512 * h + n0:512 * h + n0 + nq],
                                  kT[b][64 * h:64 * h + 64, ks:ks + 127 * d + 1:d],
                                  qT[b][64 * h:64 * h + 64, qs:qs + (nq - 1) * d + 1:d], True, True),
                         reads=[f"kT{b}", f"qT{b}"], writes=[f"ps2_{sb}"])
                s3 = s3rot.next()
                p = prot.next()
                pv = PS2[sb].rearrange("p (h n) -> p h n", h=2)[:, :, n0:n1]
                bvw = bias2[b][:, :].rearrange("p (h n) -> p h n", h=2)[:, :, n0:n1]
                sv = s32[s3][:, :].rearrange("p (h n) -> p h n", h=2)[:, :, n0:n1]
                ptv = pT[p][:, :].rearrange("p (h n) -> p h n", h=2)[:, :, n0:n1]
                S.op("dve", TT(sv, pv, bvw, ALU.add), reads=[f"ps2_{sb}", f"bias2_{b}"], writes=[f"s32_{s3}"])
                col = hbt if jt < jh else zcol
                S.op("act", ACTF(ptv, sv, AF.Exp, bias=col[:, 0:1], scale=1.0),
                     reads=[f"s32_{s3}", "hbt", "zcol"], writes=[f"pT{p}"])
                return p

            def pv_mm(p, r, jt, jq, first, ob):
                tile_i = r * nt + jt
                half = jq - jt
                for h in range(2):
                    rc0 = 0 if (isA or h == 0) else 65
                    S.op("pe", MM(PB[ob][:, 65 * h:65 * h + 65],
                                  pT[p][:, 256 * h + 128 * half:256 * h + 128 * half + 128],
                                  vS[b][:, tile_i, rc0:rc0 + 65],
                                  (first and h == 0), (not first), skip=True),
                         reads=[f"pT{p}", f"vS{b}"], writes=[f"ps{ob}"])

            for r in range(d):
                tiles = list(range(jh - 1, nt))
                pcur = score(r, tiles[0])
                obs = {}
                for ti, jt in enumerate(tiles):
                    pnext = score(r, tiles[ti + 1]) if ti + 1 < len(tiles) else None
                    if jt >= jh:
                        ob = obs.pop(jt)
                        pv_mm(pcur, r, jt, jt, False, ob)
                        o = orot.next()
                        S.op("dve", CP(ost[o][:, :], PB[ob][:, 0:130]), reads=[f"ps{ob}"], writes=[f"ost{o}"])
                        t0 = r + 128 * d * jt - HALO
                        S.dma("pool", f"ost{o}", opart[pidx, t0:t0 + 127 * d + 1:d, hp, :], ost[o][:, :],
                              reads=[f"ost{o}"], writes=[f"ostd{o}"])
                    if jt + 1 < nt:
                        ob = obank.next()
                        obs[jt + 1] = ob
                        pv_mm(pcur, r, jt, jt + 1, True, ob)
                    pcur = pnext

        if items:
            p2_loads(0, *items[0])
        for it, (pi, hp) in enumerate(items):
            if it + 1 < len(items):
                p2_loads(it + 1, *items[it + 1])
            p2_compute(it, pi, hp)
        S.barrier()
    A.reset(base_mark)

    if 3 in phases:
        x1 = A.alloc("x1", [128, 4, 2048], F32)
        opA = A.alloc("opA", [128, 8 * 130], F32)
        opB = A.alloc("opB", [128, 3, 8 * 130], F32)
        junk = A.alloc("junk3", [128, 2048], BF16)
        mixb = [A.alloc(f"mixb{i}", [128, 2048], BF16) for i in range(2)]
        h2b = [A.alloc(f"h2b{i}", [128, 2048], BF16) for i in range(2)]
        mixT = A.alloc("mixT", [128, 16, 512], BF16)
        h2T = A.alloc("h2T", [128, 16, 512], BF16)
        uT = A.alloc("uT", [128, 16, 512], BF16)
        r32 = [A.alloc(f"r32_{i}", [128, 512], F32) for i in range(2)]
        xres = [A.alloc(f"xres{i}", [128, 512], F32) for i in range(3)]
        wsl = [A.alloc(f"wsl3_{i}", [128, 16, 512], BF16) for i in range(3)]
        goab = A.alloc("goab", [128, 2048], F32)
        gml = A.alloc("gml", [128, 2048], F32)
        gfi = A.alloc("gfi", [128, 2048], F32)
        esink = A.alloc("esink", [128, 16], F32)
        dA = A.alloc("dA", [128, 16], F32)
        dB = A.alloc("dB", [128, 16], F32)
        ss3 = A.alloc("ss3", [128, 4], F32)
        rs3 = A.alloc("rs3", [128, 4], F32)

        S.dma("sp", "misc3", goab[:], g_oab.partition_broadcast(128), writes=["goab", "misc3"])
        S.dma("sp", "misc3", gml[:], g_mlp.partition_broadcast(128), writes=["gml", "misc3"])
        S.dma("sp", "misc3", gfi[:], g_fin.partition_broadcast(128), writes=["gfi", "misc3"])
        S.dma("sp", "misc3", esink[:], sinks.partition_broadcast(128), writes=["esink", "misc3"])
        S.op("act", ACTF(esink[:], esink[:], AF.Exp), reads=["esink", "misc3"], writes=["esink"])
        trot = Rot([0, 1])
        mrot = Rot([2, 3, 4, 5, 6, 7])
        evrot = Rot(["dve", "act"])
        wrot = Rot([0, 1, 2])
        rrot = Rot([0, 1])
        xrot = Rot([0, 1, 2])
        a3 = opA[:, :].rearrange("p (h c) -> p h c", h=16)
        b3 = opB[:, 0, :].rearrange("p (a c) -> p a c", a=8)
        dB3 = dB[:, :].rearrange("p (a t) -> p a t", a=8)
        b3o = b3[:, :, 1:129].rearrange("p a (t c) -> p a t c", t=2)

        def p3_prologue(c):
            tok0 = 512 * c
            for tt in range(4):
                r0 = tok0 + 128 * tt
                s = tt % 2
                S.dma("sp", "opA", opA[:, :], opart[0, r0:r0 + 128].rearrange("t h c -> t (h c)"), writes=["opA"])
                S.dma("sp", "opB", opB[:, :, :], opart[1:4, r0:r0 + 128].rearrange("p t h c -> t p (h c)"),
                      writes=["opB"])
                yield
                S.op("dve", TT(dA[:, :], a3[:, :, 0], esink[:, :], ALU.add), reads=["opA", "esink"], writes=["dA"])
                S.op("dve", RCP(dA[:, :], dA[:, :]), reads=["dA"], writes=["dA"])
                S.op("dve", TT(a3[:, :, 1:65], a3[:, :, 1:65],
                               dA[:, :].unsqueeze(2).to_broadcast([128, 16, 64]), ALU.mult),
                     reads=["opA", "dA"], writes=["opA"])
                S.op("act", ACTF(junk[:, 0:1024].rearrange("p (h c) -> p h c", h=16), a3[:, :, 1:65],
                                 AF.Square, accum=ss3[:, 0:1]), reads=["opA"], writes=["ss3a"])
                S.op("dve", TT(opB[:, 0, :], opB[:, 0, :], opB[:, 1, :], ALU.add), reads=["opB"], writes=["opB"])
                S.op("dve", TT(opB[:, 0, :], opB[:, 0, :], opB[:, 2, :], ALU.add), reads=["opB"], writes=["opB"])
                S.op("dve", RCP(dB3, b3[:, :, 0::129]), reads=["opB"], writes=["dB"])
                S.op("dve", TT(b3o, b3o, dB3.unsqueeze(3).to_broadcast([128, 8, 2, 64]), ALU.mult),
                     reads=["opB", "dB"], writes=["opB"])
                S.op("act", ACTF(junk[:, 1024:2048].rearrange("p (a c) -> p a c", a=8), b3[:, :, 1:129],
                                 AF.Square, accum=ss3[:, 1:2]), reads=["opB"], writes=["ss3b"])
                yield
                rstd_ops(ss3[:, 0:2], rs3[:, 0:2], 1024, ["ss3a", "ss3b"], ["rs3ab"])
                S.op("dve", STT(mixb[s][:, 0:1024].rearrange("p (h c) -> p h c", h=16), a3[:, :, 1:65],
                                rs3[:, 0:1], goab[:, 0:1024].rearrange("p (h c) -> p h c", h=16),
                                ALU.mult, ALU.mult), reads=["opA", "rs3ab", "goab", "misc3"], writes=[f"mixb{s}"])
                S.op("dve", STT(mixb[s][:, 1024:2048].rearrange("p (a c) -> p a c", a=8), b3[:, :, 1:129],
                                rs3[:, 1:2], goab[:, 1024:2048].rearrange("p (a c) -> p a c", a=8),
                                ALU.mult, ALU.mult), reads=["opB", "rs3ab", "goab", "misc3"], writes=[f"mixb{s}"])
                transposes(mixb[s], f"mixb{s}", mixT, lambda g, tt=tt: f"mixT{tt}_{g}", tt * 128, trot, evrot)
                yield

        drain(p3_prologue(0))
        for c in range(n_chunks):
            tok0 = 512 * c
            for cg in range(4):
                slot = wrot.next()
                load_wpiece(wsl, slot, wout_s, "wout", 0, 512 * cg, 512)
                for tt in range(4):
                    bk = mrot.next()
                    xr = xrot.next()
                    r0 = tok0 + 128 * tt
                    S.dma("sp", f"xres{xr}", xres[xr][:, :], xc[HALO + r0:HALO + r0 + 128, 512 * cg:512 * cg + 512],
                          writes=[f"xres{xr}"])
                    for kc in range(16):
                        S.op("pe", MM(PB[bk][:, :], mixT[:, kc, tt * 128:(tt + 1) * 128], wsl[slot][:, kc, :],
                                      kc == 0, kc == 15),
                             reads=[f"w{slot}", f"mixT{tt}_{kc // 4}"], writes=[f"ps{bk}"])
                    S.op("dve", TT(x1[:, tt, 512 * cg:512 * cg + 512], PB[bk][:, :], xres[xr][:, :], ALU.add),
                         reads=[f"ps{bk}", f"xres{xr}"], writes=[f"x1_{tt}"])
            for tt in range(4):
                s = tt % 2
                S.op("act", ACTF(junk[:], x1[:, tt, :], AF.Square, accum=ss3[:, 2:3]),
                     reads=[f"x1_{tt}"], writes=["ss3c"])
                rstd_ops(ss3[:, 2:3], rs3[:, 2:3], D, ["ss3c"], ["rs3c"])
                S.op("dve", STT(h2b[s][:], x1[:, tt, :], rs3[:, 2:3], gml[:], ALU.mult, ALU.mult),
                     reads=[f"x1_{tt}", "rs3c", "gml", "misc3"], writes=[f"h2b{s}"])
                transposes(h2b[s], f"h2b{s}", h2T, lambda g, tt=tt: f"h2T{tt}_{g}", tt * 128, trot, evrot)
            nxt = p3_prologue(c + 1) if c + 1 < n_chunks else None
            for q in range(4):
                for gp in range(4):
                    slot = wrot.next()
                    load_wpiece(wsl, slot, w1_s, "w1", 0, 2048 * q + 512 * gp, 512)
                    for j in range(4):
                        bk = mrot.next()
                        f = 4 * gp + j
                        for kc in range(16):
                            S.op("pe", MM(PB[bk][:, :], wsl[slot][:, kc, 128 * j:128 * j + 128], h2T[:, kc, :],
                                          kc == 0, kc == 15),
                                 reads=[f"w{slot}"] + [f"h2T{t}_{kc // 4}" for t in range(4)],
                                 writes=[f"ps{bk}"])
                        rr = rrot.next()
                        S.op("act", ACTF(r32[rr][:, :], PB[bk][:, :], AF.Relu), reads=[f"ps{bk}"],
                             writes=[f"r32_{rr}"])
                        S.op("pool", TT(uT[:, f, :], r32[rr][:, :], r32[rr][:, :], ALU.mult),
                             reads=[f"r32_{rr}"], writes=[f"uT{f}"])
                    if nxt is not None:
                        next(nxt, None)
                for cg in range(4):
                    slot = wrot.next()
                    load_wpiece(wsl, slot, w2_s, "w2", 2048 * q, 512 * cg, 512)
                    for tt in range(4):
                        bk = mrot.next()
                        for f in range(16):
                            S.op("pe", MM(PB[bk][:, :], uT[:, f, tt * 128:(tt + 1) * 128], wsl[slot][:, f, :],
                                          f == 0, f == 15),
                                 reads=[f"w{slot}", f"uT{f}"], writes=[f"ps{bk}"])
                        xv = x1[:, tt, 512 * cg:512 * cg + 512]
                        S.op("dve", TT(xv, PB[bk][:, :], xv, ALU.add), reads=[f"ps{bk}", f"x1_{tt}"],
                             writes=[f"x1_{tt}"])
                    if nxt is not None:
                        next(nxt, None)
            drain(nxt)
            for tt in range(4):
                S.op("act", ACTF(junk[:], x1[:, tt, :], AF.Square, accum=ss3[:, 3:4]),
                     reads=[f"x1_{tt}"], writes=["ss3d"])
                rstd_ops(ss3[:, 3:4], rs3[:, 3:4], D, ["ss3d"], ["rs3d"])
                S.op("dve", STT(x1[:, tt, :], x1[:, tt, :], rs3[:, 3:4], gfi[:], ALU.mult, ALU.mult),
                     reads=[f"x1_{tt}", "rs3d", "gfi", "misc3"], writes=[f"x1_{tt}"])
                r0 = tok0 + 128 * tt
                S.dma("pool", f"outst{tt}", out[r0:r0 + 128, :], x1[:, tt, :], reads=[f"x1_{tt}"],
                      writes=[f"outd{tt}"])
        S.barrier()

    finals = [c for c in S.chan if c.startswith(("outst", "ost", "qst", "vst", "cvst"))]
    S.emit(final_wait_chans=finals)
    return nc, S


_CACHE = {}


def _core_inputs(x, hf_first, b, hf):
    xc = np.zeros((NTOK, D), np.float32)
    if hf == 0:
        xc[HALO:] = x[b, 0:NOWN]
    else:
        xc[:] = x[b, NOWN - HALO:2 * NOWN]
    return xc


def kernel(x, g_attn, w_in, b_in, sinks_a, g_out_a, g_out_b, w_out, g_mlp, w_1, w_2, g_final):
    x = np.asarray(x, np.float32)
    if "nc" not in _CACHE:
        _CACHE["nc"] = build()[0]
    nc = _CACHE["nc"]
    shared = {
        "w_in": np.ascontiguousarray(np.asarray(w_in, np.float32)[0]),
        "w_out": np.ascontiguousarray(np.asarray(w_out, np.float32)[0]),
        "w_1": np.ascontiguousarray(np.asarray(w_1, np.float32)[0]),
        "w_2": np.ascontiguousarray(np.asarray(w_2, np.float32)[0]),
        "g_attn": np.ascontiguousarray(np.asarray(g_attn, np.float32)[0]),
        "b_in": np.ascontiguousarray(np.asarray(b_in, np.float32)[0]),
        "sinks": np.ascontiguousarray(np.asarray(sinks_a, np.float32)[0]),
        "g_oab": np.concatenate([np.asarray(g_out_a, np.float32)[0], np.asarray(g_out_b, np.float32)[0]]),
        "g_mlp": np.ascontiguousarray(np.asarray(g_mlp, np.float32)[0]),
        "g_fin": np.ascontiguousarray(np.asarray(g_final, np.float32)),
    }
    in_maps = []
    for core in range(N_CORES):
        b, hf = core // 2, core % 2
        m = dict(shared)
        m["xc"] = _core_inputs(x, None, b, hf)
        m["hb"] = np.full((128, 1), NEG if hf == 0 else 0.0, np.float32)
        in_maps.append(m)
    res = run_bass_kernel_spmd(nc, in_maps, core_ids=list(range(N_CORES)))
    outp = np.empty((4, 2 * NOWN, D), np.float32)
    for core in range(N_CORES):
        b, hf = core // 2, core % 2
        outp[b, hf * NOWN:(hf + 1) * NOWN] = res.results[core]["out"]
    return outp
```

```python
import numpy as np
import concourse.bass as bass
import concourse.mybir as mybir
from concourse.bass_utils import run_bass_kernel_spmd

F32 = mybir.dt.float32
BF16 = mybir.dt.bfloat16
I32 = mybir.dt.int32
AF = mybir.ActivationFunctionType
ALU = mybir.AluOpType

D = 2048
NOWN = 4096
HALO = 2048
NTOK = 6144
DIN = 4352
DFF = 8192
EPS = 1e-5
NEG = -30000.0
SLOPES = [2.0 ** (-8.0 * (h + 1) / 16) for h in range(16)]
N_CORES = 8
P2_STAGE = 9


class Sched:
    ENGS = ("pe", "act", "dve", "pool", "sp")

    def __init__(self, nc):
        self.nc = nc
        self.ops = []
        self.last_w = {}
        self.readers = {}
        self.chan = {}
        self.chan_last = {}
        self.eng_last = {}
        self.pending = {}

    def _add(self, eng, fn, reads, writes, dma_chan=None):
        idx = len(self.ops)
        deps = set()
        for r in reads:
            if r in self.last_w:
                deps.add((self.last_w[r], "raw"))
        for w in writes:
            if w in self.last_w:
                deps.add((self.last_w[w], "waw"))
            for rd in self.readers.get(w, {}).values():
                deps.add((rd, "war"))
        if eng in self.pending:
            for j in self.pending.pop(eng):
                deps.add((j, "raw"))
        op = dict(eng=eng, fn=fn, deps=deps, dma=dma_chan, sig=False)
        if dma_chan is not None:
            self.chan[dma_chan] = self.chan.get(dma_chan, 0) + 16
            op["chan_val"] = self.chan[dma_chan]
            self.chan_last[dma_chan] = idx
        self.eng_last[eng] = idx
        self.ops.append(op)
        rkey = eng if dma_chan is None else ("dma", dma_chan)
        for r in reads:
            self.readers.setdefault(r, {})[rkey] = idx
        for w in writes:
            self.last_w[w] = idx
            self.readers[w] = {}
        return idx

    def op(self, eng, fn, reads=(), writes=()):
        return self._add(eng, fn, tuple(reads), tuple(writes))

    def dma(self, eng, chan, out, in_, reads=(), writes=(), **kw):
        def fn(e, out=out, in_=in_, kw=kw):
            return e.dma_start(out=out, in_=in_, **kw)
        return self._add(eng, fn, tuple(reads), tuple(writes), dma_chan=chan)

    def barrier(self, skip_engs=(), skip_chan_prefix=None):
        deps = [v for k, v in self.eng_last.items() if k not in skip_engs]
        deps += [v for k, v in self.chan_last.items()
                 if not (skip_chan_prefix and k.startswith(skip_chan_prefix))]
        for e in self.ENGS:
            if e in skip_engs:
                continue
            self.pending[e] = list(set(self.pending.get(e, []) + deps))

    def emit(self, final_wait_chans=()):
        from contextlib import ExitStack
        nc = self.nc
        ops = self.ops
        for op in ops:
            waits = []
            for (j, kind) in op["deps"]:
                J = ops[j]
                if J["dma"] is not None:
                    waits.append(("chan", J["dma"], J["chan_val"]))
                elif J["eng"] == op["eng"]:
                    if op["eng"] == "pe" or kind == "war":
                        continue
                    J["sig"] = True
                    waits.append(("eng", J["eng"], j))
                else:
                    J["sig"] = True
                    waits.append(("eng", J["eng"], j))
            op["waits"] = waits
        cnt = {e: 0 for e in self.ENGS}
        for op in ops:
            if op["sig"] and op["dma"] is None:
                cnt[op["eng"]] += 1
                op["sigval"] = cnt[op["eng"]]
        per_eng = {e: [] for e in self.ENGS}
        waited = {e: {} for e in self.ENGS}
        for op in ops:
            w2 = {}
            for w in op["waits"]:
                if w[0] == "chan":
                    key, val = ("chan", w[1]), w[2]
                else:
                    key, val = ("eng", w[1]), ops[w[2]]["sigval"]
                if waited[op["eng"]].get(key, 0) >= val:
                    continue
                w2[key] = max(w2.get(key, 0), val)
            for k, v in w2.items():
                waited[op["eng"]][k] = v
            op["w2"] = w2
            per_eng[op["eng"]].append(op)
        self.stats = dict(n_ops=len(ops), n_sem=len(self.chan) + 5,
                          per_eng={e: len(v) for e, v in per_eng.items()}, sig=dict(cnt))
        with ExitStack() as st:
            sems = {}
            for e in self.ENGS:
                sems[("eng", e)] = st.enter_context(nc.semaphore("s_" + e))
            for c in self.chan:
                sems[("chan", c)] = st.enter_context(nc.semaphore("c_" + str(c)))
            block = st.enter_context(nc.Block())

            def run(engobj, lst):
                for op in lst:
                    for k, v in op["w2"].items():
                        engobj.wait_ge(sems[k], v)
                    ins = op["fn"](engobj)
                    if op["dma"] is not None:
                        ins.then_inc(sems[("chan", op["dma"])], 16)
                    elif op["sig"]:
                        ins.then_inc(sems[("eng", op["eng"])], 1)

            @block.tensor
            def _(e):
                run(e, per_eng["pe"])

            @block.scalar
            def _(e):
                run(e, per_eng["act"])

            @block.vector
            def _(e):
                run(e, per_eng["dve"])

            @block.gpsimd
            def _(e):
                run(e, per_eng["pool"])

            @block.sync
            def _(e):
                run(e, per_eng["sp"])
                for c in final_wait_chans:
                    e.wait_ge(sems[("chan", c)], self.chan[c])


class Arena:
    def __init__(self, nc):
        self.nc = nc
        self.lo = ((nc.SBUF_PARTITION_SIZE_BYTES - nc.sbuf_bytes_remaining + 63) // 64) * 64
        self.hi = nc.SBUF_PARTITION_SIZE_BYTES
        self.cur = self.lo
        self.n = 0

    def alloc(self, name, shape, dt):
        nbytes = int(np.prod(shape[1:])) * (4 if dt in (F32, I32) else 2)
        nbytes = ((nbytes + 63) // 64) * 64
        off = self.cur
        assert off + nbytes <= self.hi, f"SBUF overflow at {name}: {off + nbytes} > {self.hi}"
        self.cur += nbytes
        self.n += 1
        return self.nc.alloc_sbuf_tensor_at(f"{name}_{self.n}", list(shape), dt, offset=off)

    def mark(self):
        return self.cur

    def reset(self, m):
        self.cur = m


class Rot:
    def __init__(self, items):
        self.items = list(items)
        self.i = 0

    def next(self):
        v = self.items[self.i % len(self.items)]
        self.i += 1
        return v


def MM(o, l, r, start, stop, skip=False):
    return lambda e: e.matmul(o, lhsT=l, rhs=r, start=start, stop=stop, skip_group_check=skip)


def ACTF(o, i, func, bias=None, scale=1.0, accum=None):
    def f(e):
        kw = {}
        if bias is not None:
            kw["bias"] = bias
        if accum is not None:
            kw["accum_out"] = accum
        return e.activation(out=o, in_=i, func=func, scale=scale, **kw)
    return f


def TT(o, a, b, op):
    return lambda e: e.tensor_tensor(out=o, in0=a, in1=b, op=op)


def STT(o, a, s, b, op0, op1):
    return lambda e: e.scalar_tensor_tensor(out=o, in0=a, scalar=s, in1=b, op0=op0, op1=op1)


def TS(o, a, s1, s2, op0, op1=None):
    if op1 is None:
        return lambda e: e.tensor_scalar(out=o, in0=a, scalar1=s1, scalar2=None, op0=op0)
    return lambda e: e.tensor_scalar(out=o, in0=a, scalar1=s1, scalar2=s2, op0=op0, op1=op1)


def CP(o, i):
    return lambda e: e.tensor_copy(out=o, in_=i)


def ACP(o, i):
    return lambda e: e.copy(out=o, in_=i)


def RCP(o, i):
    return lambda e: e.reciprocal(out=o, in_=i)


def MS(ap, v):
    return lambda e: e.memset(ap, v)


def build(phases=(0, 1, 2, 3), dbg=False, n_chunks=8, p2_items=None):
    nc = bass.Bass("TRN2", target_bir_lowering=False)

    def dram(name, shape, dt, kind):
        return nc.dram_tensor(name, list(shape), dt, kind=kind).ap()

    IN, OUT, INT = "ExternalInput", "ExternalOutput", "Internal"
    SCR = OUT if dbg else INT
    P = set(phases)
    xc = dram("xc", [NTOK, D], F32, IN) if P & {1, 3} else None
    if 0 in P:
        w_in = dram("w_in", [D, DIN], F32, IN)
        w_out = dram("w_out", [D, D], F32, IN)
        w_1 = dram("w_1", [D, DFF], F32, IN)
        w_2 = dram("w_2", [DFF, D], F32, IN)
    if 1 in P:
        g_attn = dram("g_attn", [D], F32, IN)
        b_in = dram("b_in", [DIN], F32, IN)
    if 3 in P:
        sinks = dram("sinks", [16], F32, IN)
        g_oab = dram("g_oab", [2048], F32, IN)
        g_mlp = dram("g_mlp", [D], F32, IN)
        g_fin = dram("g_fin", [D], F32, IN)
    hb = dram("hb", [128, 1], F32, IN)
    out = dram("out", [NOWN, D], F32, OUT)

    win_s = dram("win_s", [D, DIN], BF16, INT)
    wout_s = dram("wout_s", [D, D], BF16, INT)
    w1_s = dram("w1_s", [D, DFF], BF16, INT)
    w2_s = dram("w2_s", [DFF, D], BF16, INT)
    qta = dram("qta", [8, 128, NOWN], BF16, SCR)
    qtb = dram("qtb", [8, 128, NOWN], BF16, SCR)
    ktb = dram("ktb", [8, 128, NTOK], BF16, SCR)
    kta = dram("kta", [2, 128, NTOK], BF16, SCR)
    va = dram("va", [NTOK, 128], BF16, SCR)
    vb = dram("vb", [NTOK, 1024], BF16, SCR)
    opart = dram("opart", [4, NOWN, 8, 130], F32, SCR)

    S = Sched(nc)
    A = Arena(nc)
    PBALL = nc.alloc_psum_tensor("pball", [128, 4096], F32)
    PB = [PBALL[:, 512 * i:512 * i + 512] for i in range(8)]
    PS2 = [PBALL[:, 1024 * i:1024 * i + 1024] for i in range(3)]

    def pbf(i):
        return PB[i].bitcast(BF16)

    ident = A.alloc("ident", [128, 128], BF16)
    zcol = A.alloc("zcol", [128, 1], F32)
    hbt = A.alloc("hbt", [128, 1], F32)
    epsc = A.alloc("epsc", [128, 1], F32)
    S.op("pool", MS(ident[:], 0.0), writes=["ident"])
    S.op("pool", lambda e: e.affine_select(out=ident[:], in_=ident[:], pattern=[[-1, 128]],
                                           compare_op=ALU.not_equal, fill=1.0, base=0,
                                           channel_multiplier=1),
         reads=["ident"], writes=["ident"])
    S.op("pool", MS(zcol[:], 0.0), writes=["zcol"])
    S.op("pool", MS(epsc[:], EPS), writes=["epsc"])
    S.dma("sp", "misc", hbt[:], hb, writes=["hbt", "misc"])
    base_mark = A.mark()

    CVW = 2176
    cv32 = [A.alloc(f"cv32_{i}", [128, CVW], F32) for i in range(2)]
    cvbf = [A.alloc(f"cvbf_{i}", [128, CVW], BF16) for i in range(2)]
    pieces = []
    if 0 in P:
        for (nm, src, dst, R, C, pw) in (("win", w_in, win_s, D, DIN, 2176), ("wout", w_out, wout_s, D, D, 2048),
                                         ("w1", w_1, w1_s, D, DFF, 2048), ("w2", w_2, w2_s, DFF, D, 2048)):
            for rc in range(R // 128):
                for c0 in range(0, C, pw):
                    pieces.append((nm, src[rc * 128:(rc + 1) * 128, c0:c0 + pw],
                                   dst[rc * 128:(rc + 1) * 128, c0:c0 + pw], pw))
    n_win = sum(1 for p_ in pieces if p_[0] == "win")
    def CV(nm):
        return [f"cvst_{s}_{nm}" for s in range(2)]

    stp_i = A.alloc("stp_i", [128, 256], I32)
    stp = A.alloc("stp", [128, 256], F32)
    vbA = A.alloc("vbA", [128, 256], F32)
    vbB = A.alloc("vbB", [128, 256], F32)
    S.op("pool", lambda e: e.iota(stp_i[:], pattern=[[1, 256]], base=0, channel_multiplier=-1),
         writes=["stp_i"])
    S.op("pool", CP(stp[:], stp_i[:]), reads=["stp_i"], writes=["stp"])
    for (t, ms, nm) in ((vbA, 127, "vbA"), (vbB, 128, "vbB")):
        S.op("pool", MS(t[:], 0.0), writes=[nm])
        S.op("pool", (lambda t: (lambda e: e.affine_select(
            out=t[:], in_=t[:], pattern=[[1, 256]], compare_op=ALU.is_ge, fill=NEG, base=0,
            channel_multiplier=-1)))(t), reads=[nm], writes=[nm])
        S.op("pool", (lambda t, ms: (lambda e: e.affine_select(
            out=t[:], in_=t[:], pattern=[[-1, 256]], compare_op=ALU.is_ge, fill=NEG, base=ms,
            channel_multiplier=1)))(t, ms), reads=[nm], writes=[nm])
    p12_mark = A.mark()

    for k in range(n_win):
        nm, src, dst, w = pieces[k]
        s_ = k % 2
        S.dma("sp", f"cv32_{s_}", cv32[s_][:, :w], src, writes=[f"cv32_{s_}"])
        eng = ("pool", "dve", "act")[k % 3]
        if eng == "act":
            S.op("act", ACP(cvbf[s_][:, :w], cv32[s_][:, :w]), reads=[f"cv32_{s_}"], writes=[f"cvbf_{s_}"])
        else:
            S.op(eng, CP(cvbf[s_][:, :w], cv32[s_][:, :w]), reads=[f"cv32_{s_}"], writes=[f"cvbf_{s_}"])
        S.dma("pool", f"cvst_{s_}", dst, cvbf[s_][:, :w], reads=[f"cvbf_{s_}"], writes=[f"cvst_{s_}_{nm}"])
    for k in range(n_win, len(pieces)):
        nm, src, dst, w = pieces[k]
        s_ = k % 2
        S.dma("pool", f"cv32_{s_}", cv32[s_][:, :w], src, writes=[f"cv32_{s_}"])
        if k - 1 >= n_win:
            nm1, src1, dst1, w1 = pieces[k - 1]
            s1 = (k - 1) % 2
            S.op("pool", CP(cvbf[s1][:, :w1], cv32[s1][:, :w1]), reads=[f"cv32_{s1}"], writes=[f"cvbf_{s1}"])
            S.dma("pool", f"cvst_{s1}", dst1, cvbf[s1][:, :w1], reads=[f"cvbf_{s1}"], writes=[f"cvst_{s1}_{nm1}"])
    if len(pieces) > n_win:
        k = len(pieces) - 1
        nm1, src1, dst1, w1 = pieces[k]
        s1 = k % 2
        S.op("pool", CP(cvbf[s1][:, :w1], cv32[s1][:, :w1]), reads=[f"cv32_{s1}"], writes=[f"cvbf_{s1}"])
        S.dma("pool", f"cvst_{s1}", dst1, cvbf[s1][:, :w1], reads=[f"cvbf_{s1}"], writes=[f"cvst_{s1}_{nm1}"])

    def pump(n=1):
        pass

    def conv_drain():
        pass

    def rstd_ops(ssap, rsap, n, reads, writes):
        S.op("act", ACTF(rsap, ssap, AF.Sqrt, bias=epsc[:, 0:1], scale=1.0 / n), reads=list(reads) + ["epsc"],
             writes=writes)
        S.op("dve", RCP(rsap, rsap), reads=writes, writes=writes)

    def transposes(src_bf, src_res, dstT, dst_res_fn, tcol, trot, evrot):
        for g in range(4):
            bk = trot.next()
            for j in range(4):
                kc = 4 * g + j
                o = pbf(bk)[:, j * 128:(j + 1) * 128]
                i_ = src_bf[:, kc * 128:(kc + 1) * 128]
                S.op("pe", (lambda o, i_: (lambda e: e.transpose(out=o, in_=i_, identity=ident[:])))(o, i_),
                     reads=[src_res, "ident"], writes=[f"ps{bk}"])
            ev = evrot.next()
            o = dstT[:, 4 * g:4 * g + 4, tcol:tcol + 128]
            i_ = pbf(bk)[:, 0:512].rearrange("p (a b) -> p a b", a=4)
            if ev == "dve":
                S.op("dve", CP(o, i_), reads=[f"ps{bk}"], writes=[dst_res_fn(g)])
            else:
                S.op("act", ACP(o, i_), reads=[f"ps{bk}"], writes=[dst_res_fn(g)])

    def load_wpiece(wsl, slot, scr, wname, r0, c0, ncols, dcol0=0):
        S.dma("sp", f"w{slot}", wsl[slot][:, :, dcol0:dcol0 + ncols],
              scr[r0:r0 + 2048, c0:c0 + ncols].rearrange("(k p) c -> p k c", p=128),
              reads=CV(wname), writes=[f"w{slot}"])

    def drain(gen):
        if gen is not None:
            for _ in gen:
                pass

    if 1 in phases:
        hT = [A.alloc(f"hT{i}", [128, 16, 1024], BF16) for i in range(2)]
        xs = [A.alloc(f"xs{i}", [128, 2048], F32) for i in range(3)]
        hbf = [A.alloc(f"hbf{i}", [128, 2048], BF16) for i in range(3)]
        gat = A.alloc("gat", [128, 2048], F32)
        ss = A.alloc("ss", [128, 3], F32)
        rs = A.alloc("rs", [128, 3], F32)
        binT = A.alloc("binT", [128, 34], F32)
        binT8 = A.alloc("binT8", [128, 34], F32)
        bka = A.alloc("bka", [128, 2], F32)
        bv = A.alloc("bv", [128, 1152], F32)
        wsl = [A.alloc(f"wsl{i}", [128, 16, 512], BF16) for i in range(3)]
        qst = [A.alloc(f"qst{i}", [128, 1024], BF16) for i in range(3)]
        vst = [A.alloc(f"vst{i}", [128, 512], BF16) for i in range(4)]

        S.dma("sp", "misc", gat[:], g_attn.partition_broadcast(128), writes=["gat", "misc"])
        S.dma("sp", "misc", binT[:], b_in.rearrange("(c p) -> p c", p=128), writes=["binT", "misc"],
              allow_slow_non_contiguous=True)
        for g in range(2):
            for hh in range(2):
                S.dma("sp", "misc", bka[64 * hh:64 * hh + 64, g:g + 1],
                      b_in[1024 + 64 * g:1024 + 64 * g + 64].rearrange("(p o) -> p o", o=1),
                      writes=["bka", "misc"])
        S.dma("sp", "misc", bv[:, 0:1024], b_in[3328:4352].partition_broadcast(128), writes=["bv", "misc"])
        S.dma("sp", "misc", bv[:, 1024:1152], b_in[1152:1280].partition_broadcast(128), writes=["bv", "misc"])
        S.op("dve", TS(binT8[:], binT[:], 0.125, None, ALU.mult), reads=["binT", "misc"], writes=["binT8"])
        trot = Rot([0, 1])
        mrot = Rot([2, 3, 4, 5, 6, 7])
        evrot = Rot(["dve", "act"])
        wrot = Rot([0, 1, 2])
        qrot = Rot([0, 1, 2])
        vrot = Rot([0, 1, 2, 3])
        NCH1 = 6

        def p1_prologue(c):
            T0 = 1024 * c
            hb_ = c % 2

            def L(tt):
                s = tt % 3
                S.dma("sp", f"xs{s}", xs[s][:], xc[T0 + tt * 128:T0 + (tt + 1) * 128, :], writes=[f"xs{s}"])

            def N(tt):
                s = tt % 3
                S.op("act", ACTF(hbf[s][:], xs[s][:], AF.Square, accum=ss[:, s:s + 1]),
                     reads=[f"xs{s}"], writes=[f"ss{s}", f"hbf{s}"])
                rstd_ops(ss[:, s:s + 1], rs[:, s:s + 1], D, [f"ss{s}"], [f"rs{s}"])
                S.op("dve", STT(hbf[s][:], xs[s][:], rs[:, s:s + 1], gat[:], ALU.mult, ALU.mult),
                     reads=[f"xs{s}", f"rs{s}", "gat", "misc"], writes=[f"hbf{s}"])

            def T(tt):
                s = tt % 3
                transposes(hbf[s], f"hbf{s}", hT[hb_], lambda g, tt=tt: f"hT{hb_}_{tt}_{g}", tt * 128, trot, evrot)

            for step in range(8 + 4):
                if step < 8:
                    L(step)
                    yield
                if 0 <= step - 2 < 8:
                    N(step - 2)
                    yield
                if 0 <= step - 4 < 8:
                    T(step - 4)
                    yield

        def fm_group(c, slot, wc0, bias_col, scale, dstap):
            hb_ = c % 2
            q = qrot.next()
            for tq in range(2):
                bk = mrot.next()
                for kc in range(16):
                    S.op("pe", MM(PB[bk][:, :], wsl[slot][:, kc, wc0:wc0 + 128],
                                  hT[hb_][:, kc, tq * 512:(tq + 1) * 512], kc == 0, kc == 15),
                         reads=[f"w{slot}"] + [f"hT{hb_}_{4 * tq + t}_{kc // 4}" for t in range(4)],
                         writes=[f"ps{bk}"])
                S.op("act", ACTF(qst[q][:, tq * 512:(tq + 1) * 512], PB[bk][:, :], AF.Identity,
                                 bias=bias_col, scale=scale),
                     reads=[f"ps{bk}", "binT8", "misc"], writes=[f"qst{q}"])
            S.dma("sp", f"qst{q}", dstap, qst[q][:], reads=[f"qst{q}"], writes=[f"qstd{q}"])

        def tm_piece(c, slot, wc0, n, dstap_fn, bcol0, tick):
            hb_ = c % 2
            for tt in range(8):
                bk = mrot.next()
                v = vrot.next()
                for kc in range(16):
                    S.op("pe", MM(PB[bk][:, 0:n], hT[hb_][:, kc, tt * 128:(tt + 1) * 128],
                                  wsl[slot][:, kc, wc0:wc0 + n], kc == 0, kc == 15),
                         reads=[f"w{slot}", f"hT{hb_}_{tt}_{kc // 4}"], writes=[f"ps{bk}"])
                S.op("dve", TT(vst[v][:, 0:n], PB[bk][:, 0:n], bv[:, bcol0:bcol0 + n], ALU.add),
                     reads=[f"ps{bk}", "bv", "misc"], writes=[f"vst{v}"])
                S.dma("sp", f"vst{v}", dstap_fn(tt), vst[v][:, 0:n], reads=[f"vst{v}"], writes=[f"vstd{v}"])
                if tt % 2 == 1:
                    tick()

        plan = []

        def mk_simple(c0):
            return lambda slot: load_wpiece(wsl, slot, win_s, "win", 0, c0, 512)

        def mk_kava():
            def f(slot):
                for g in range(2):
                    for hh in range(2):
                        load_wpiece(wsl, slot, win_s, "win", 0, 1024 + 64 * g, 64, dcol0=128 * g + 64 * hh)
                load_wpiece(wsl, slot, win_s, "win", 0, 1152, 128, dcol0=256)
            return f

        for c in range(NCH1):
            plan += [mk_simple(2304), mk_simple(2816), mk_simple(3328), mk_simple(3840), mk_kava()]
            if c >= 2:
                plan += [mk_simple(0), mk_simple(512), mk_simple(1280), mk_simple(1792)]
        pi_ = [0]

        def wl(i):
            if i < len(plan):
                plan[i](i % 3)

        def next_piece():
            i = pi_[0]
            pi_[0] += 1
            wl(i + 2)
            return i % 3

        wl(0)
        wl(1)
        drain(p1_prologue(0))
        for c in range(NCH1):
            T0 = 1024 * c
            own = c >= 2
            nxt = p1_prologue(c + 1) if c + 1 < NCH1 else None
            tk = [0]

            def tick():
                tk[0] += 1
                if nxt is not None:
                    next(nxt, None)

            for p in range(2):
                slot = next_piece()
                for j in range(4):
                    fc = 4 * p + j
                    fm_group(c, slot, 128 * j, binT[:, 18 + fc:18 + fc + 1], 1.0, ktb[fc, :, T0:T0 + 1024])
                    tick()
            for p in range(2):
                slot = next_piece()
                tm_piece(c, slot, 0, 512,
                         lambda tt, T0=T0, p=p: vb[T0 + tt * 128:T0 + (tt + 1) * 128, 512 * p:512 * p + 512],
                         512 * p, tick)
            slot = next_piece()
            for g in range(2):
                fm_group(c, slot, 128 * g, bka[:, g:g + 1], 1.0, kta[g, :, T0:T0 + 1024])
                tick()
            tm_piece(c, slot, 256, 128, lambda tt, T0=T0: va[T0 + tt * 128:T0 + (tt + 1) * 128, :], 1024, tick)
            if own:
                for (c0, dst, b0) in ((0, qta, 0), (1280, qtb, 10)):
                    for p in range(2):
                        slot = next_piece()
                        for j in range(4):
                            fc = 4 * p + j
                            fm_group(c, slot, 128 * j, binT8[:, b0 + fc:b0 + fc + 1], 0.125,
                                     dst[fc, :, T0 - HALO:T0 - HALO + 1024])
                            tick()
            drain(nxt)
        S.barrier(skip_engs=("pool",), skip_chan_prefix="cv")
    A.reset(p12_mark)

    if 2 in phases:
        qT = [[A.alloc(f"qT{i}_{h}", [128, NOWN], BF16) for h in range(2)] for i in range(2)]
        kT = [A.alloc(f"kT{i}", [128, NTOK], BF16) for i in range(2)]
        vS = [A.alloc(f"vS{i}", [128, 48, 130], BF16) for i in range(2)]
        bias2 = [A.alloc(f"bias2_{i}", [128, 512], F32) for i in range(2)]
        s32 = [A.alloc(f"s32_{i}", [128, 512], F32) for i in range(3)]
        pT = [A.alloc(f"pT{i}", [128, 512], BF16) for i in range(5)]
        ost = [A.alloc(f"ost{i}", [128, 130], F32) for i in range(4)]

        for i in range(2):
            S.op("dve", MS(vS[i][:, :, 0:1], 1.0), writes=[f"vS{i}"])
            S.op("dve", MS(vS[i][:, :, 129:130], 1.0), writes=[f"vS{i}"])
            S.op("dve", MS(qT[i][0][64:128, :], 0.0), writes=[f"qT{i}"])
            S.op("dve", MS(qT[i][1][0:64, :], 0.0), writes=[f"qT{i}"])

        passes = [("A", 1, 0), ("B", 1, 1), ("B", 4, 2), ("B", 16, 3)]
        items = [(pi, hp) for pi in range(4) for hp in range(8)]
        if p2_items is not None:
            items = p2_items
        srot = Rot([0, 1, 2])
        s3rot = Rot([0, 1, 2])
        prot = Rot([0, 1, 2, 3, 4])
        orot = Rot([0, 1, 2, 3])
        obank = Rot([6, 7])
        oev = Rot(["dve", "act"])

        def p2_loads(it, pi, hp):
            kind, d, pidx = passes[pi]
            b = it % 2
            isA = kind == "A"
            nt = 48 // d
            qsrc = (qta if isA else qtb)[hp]
            for h in range(2):
                S.dma("sp", f"qT{b}", qT[b][h][64 * h:64 * h + 64, :], qsrc[64 * h:64 * h + 64, :], writes=[f"qT{b}"])
            S.dma("sp", f"kT{b}", kT[b][:], kta[hp // 4] if isA else ktb[hp], writes=[f"kT{b}"])
            for r in range(d):
                if isA:
                    g = hp // 4
                    src = va[:, 64 * g:64 * g + 64].rearrange("(jt m) c -> m jt c", m=128)
                    S.dma("sp", f"vS{b}", vS[b][:, 0:48, 1:65], src, writes=[f"vS{b}"])
                else:
                    src = vb[r::d, 128 * hp:128 * hp + 128].rearrange("(jt m) c -> m jt c", m=128)
                    S.dma("sp", f"vS{b}", vS[b][:, r * nt:(r + 1) * nt, 1:129], src, writes=[f"vS{b}"])

        def p2_compute(it, pi, hp):
            kind, d, pidx = passes[pi]
            b = it % 2
            isA = kind == "A"
            nt = 48 // d
            jh = 16 // d
            vbt = vbA if isA else vbB
            for h in range(2):
                S.op("dve", STT(bias2[b][:, 256 * h:256 * h + 256], stp[:], -SLOPES[2 * hp + h] * d, vbt[:],
                                ALU.mult, ALU.add), reads=["stp", "vbA", "vbB"], writes=[f"bias2_{b}"])

            def score(r, jt):
                n0 = 128 if jt == jh - 1 else 0
                n1 = 128 if jt == nt - 1 else 256
                sb = srot.next()
                ks = r + 128 * d * jt
                qs = r + d * (128 * jt + n0) - HALO
                nq = n1 - n0
                for h in range(2):
                    S.op("pe", MM(PS2[sb][:, 512 * h + n0:512 * h + n0 + nq],
                                  kT[b][:, ks:ks + 127 * d + 1:d],
                                  qT[b][h][:, qs:qs + (nq - 1) * d + 1:d], True, True),
                         reads=[f"kT{b}", f"qT{b}"], writes=[f"ps2_{sb}"])
                s3 = s3rot.next()
                p = prot.next()
                pv = PS2[sb].rearrange("p (h n) -> p h n", h=2)[:, :, n0:n1]
                bvw = bias2[b][:, :].rearrange("p (h n) -> p h n", h=2)[:, :, n0:n1]
                sv = s32[s3][:, :].rearrange("p (h n) -> p h n", h=2)[:, :, n0:n1]
                ptv = pT[p][:, :].rearrange("p (h n) -> p h n", h=2)[:, :, n0:n1]
                S.op("dve", TT(sv, pv, bvw, ALU.add), reads=[f"ps2_{sb}", f"bias2_{b}"], writes=[f"s32_{s3}"])
                col = hbt if jt < jh else zcol
                S.op("act", ACTF(ptv, sv, AF.Exp, bias=col[:, 0:1], scale=1.0),
                     reads=[f"s32_{s3}", "hbt", "zcol"], writes=[f"pT{p}"])
                return p

            def pv_q(r, jq, p_prev, p_cur):
                ob = obank.next()
                for h in range(2):
                    rc0 = 0 if (isA or h == 0) else 65
                    S.op("pe", MM(PB[ob][:, 65 * h:65 * h + 65],
                                  pT[p_prev][:, 256 * h + 128:256 * h + 256],
                                  vS[b][:, r * nt + jq - 1, rc0:rc0 + 65], True, False, skip=True),
                         reads=[f"pT{p_prev}", f"vS{b}"], writes=[f"ps{ob}"])
                    S.op("pe", MM(PB[ob][:, 65 * h:65 * h + 65],
                                  pT[p_cur][:, 256 * h:256 * h + 128],
                                  vS[b][:, r * nt + jq, rc0:rc0 + 65], False, True, skip=True),
                         reads=[f"pT{p_cur}", f"vS{b}"], writes=[f"ps{ob}"])
                o = orot.next()
                if oev.next() == "dve":
                    S.op("dve", CP(ost[o][:, :], PB[ob][:, 0:130]), reads=[f"ps{ob}"], writes=[f"ost{o}"])
                else:
                    S.op("act", ACP(ost[o][:, :], PB[ob][:, 0:130]), reads=[f"ps{ob}"], writes=[f"ost{o}"])
                t0 = r + 128 * d * jq - HALO
                S.dma("sp", f"ost{o}", opart[pidx, t0:t0 + 127 * d + 1:d, hp, :], ost[o][:, :],
                      reads=[f"ost{o}"], writes=[f"ostd{o}"])

            for r in range(d):
                tiles = list(range(jh - 1, nt))
                ps_ = {}
                for ti in range(min(2, len(tiles))):
                    ps_[ti] = score(r, tiles[ti])
                for ti, jt in enumerate(tiles):
                    if ti + 2 < len(tiles):
                        ps_[ti + 2] = score(r, tiles[ti + 2])
                    if jt >= jh:
                        pv_q(r, jt, ps_[ti - 1], ps_[ti])
                        ps_.pop(ti - 1)

        if items:
            p2_loads(0, *items[0])
        for it, (pi, hp) in enumerate(items):
            if it + 1 < len(items):
                p2_loads(it + 1, *items[it + 1])
            p2_compute(it, pi, hp)
        S.barrier()
    A.reset(base_mark)

    if 3 in phases:
        x1 = A.alloc("x1", [128, 4, 2048], F32)
        opA = A.alloc("opA", [128, 8 * 130], F32)
        opB = A.alloc("opB", [128, 3, 8 * 130], F32)
        junk = A.alloc("junk3", [128, 2048], BF16)
        mixb = [A.alloc(f"mixb{i}", [128, 2048], BF16) for i in range(2)]
        h2b = [A.alloc(f"h2b{i}", [128, 2048], BF16) for i in range(2)]
        mixT = A.alloc("mixT", [128, 16, 512], BF16)
        h2T = A.alloc("h2T", [128, 16, 512], BF16)
        uT = A.alloc("uT", [128, 16, 512], BF16)
        r32 = [A.alloc(f"r32_{i}", [128, 512], F32) for i in range(2)]
        xres = [A.alloc(f"xres{i}", [128, 512], F32) for i in range(3)]
        wsl = [A.alloc(f"wsl3_{i}", [128, 16, 512], BF16) for i in range(3)]
        goab = A.alloc("goab", [128, 2048], F32)
        gml = A.alloc("gml", [128, 2048], F32)
        gfi = A.alloc("gfi", [128, 2048], F32)
        esink = A.alloc("esink", [128, 16], F32)
        dA = A.alloc("dA", [128, 16], F32)
        dB = A.alloc("dB", [128, 16], F32)
        ss3 = A.alloc("ss3", [128, 4], F32)
        rs3 = A.alloc("rs3", [128, 4], F32)

        S.dma("sp", "misc3", goab[:], g_oab.partition_broadcast(128), writes=["goab", "misc3"])
        S.dma("sp", "misc3", gml[:], g_mlp.partition_broadcast(128), writes=["gml", "misc3"])
        S.dma("sp", "misc3", gfi[:], g_fin.partition_broadcast(128), writes=["gfi", "misc3"])
        S.dma("sp", "misc3", esink[:], sinks.partition_broadcast(128), writes=["esink", "misc3"])
        S.op("act", ACTF(esink[:], esink[:], AF.Exp), reads=["esink", "misc3"], writes=["esink"])
        trot = Rot([0, 1])
        mrot = Rot([2, 3, 4, 5, 6, 7])
        evrot = Rot(["dve", "act"])
        wrot = Rot([0, 1, 2])
        rrot = Rot([0, 1])
        xrot = Rot([0, 1, 2])
        a3 = opA[:, :].rearrange("p (h c) -> p h c", h=16)
        b3 = opB[:, 0, :].rearrange("p (a c) -> p a c", a=8)
        dB3 = dB[:, :].rearrange("p (a t) -> p a t", a=8)
        b3o = b3[:, :, 1:129].rearrange("p a (t c) -> p a t c", t=2)

        def p3_prologue(c):
            tok0 = 512 * c
            for tt in range(4):
                r0 = tok0 + 128 * tt
                s = tt % 2
                S.dma("sp", "opA", opA[:, :], opart[0, r0:r0 + 128].rearrange("t h c -> t (h c)"), writes=["opA"])
                S.dma("sp", "opB", opB[:, :, :], opart[1:4, r0:r0 + 128].rearrange("p t h c -> t p (h c)"),
                      writes=["opB"])
                yield
                S.op("dve", TT(dA[:, :], a3[:, :, 0], esink[:, :], ALU.add), reads=["opA", "esink"], writes=["dA"])
                S.op("dve", RCP(dA[:, :], dA[:, :]), reads=["dA"], writes=["dA"])
                S.op("dve", TT(a3[:, :, 1:65], a3[:, :, 1:65],
                               dA[:, :].unsqueeze(2).to_broadcast([128, 16, 64]), ALU.mult),
                     reads=["opA", "dA"], writes=["opA"])
                S.op("act", ACTF(junk[:, 0:1024].rearrange("p (h c) -> p h c", h=16), a3[:, :, 1:65],
                                 AF.Square, accum=ss3[:, 0:1]), reads=["opA"], writes=["ss3a"])
                S.op("dve", TT(opB[:, 0, :], opB[:, 0, :], opB[:, 1, :], ALU.add), reads=["opB"], writes=["opB"])
                S.op("dve", TT(opB[:, 0, :], opB[:, 0, :], opB[:, 2, :], ALU.add), reads=["opB"], writes=["opB"])
                S.op("dve", RCP(dB3, b3[:, :, 0::129]), reads=["opB"], writes=["dB"])
                S.op("dve", TT(b3o, b3o, dB3.unsqueeze(3).to_broadcast([128, 8, 2, 64]), ALU.mult),
                     reads=["opB", "dB"], writes=["opB"])
                S.op("act", ACTF(junk[:, 1024:2048].rearrange("p (a c) -> p a c", a=8), b3[:, :, 1:129],
                                 AF.Square, accum=ss3[:, 1:2]), reads=["opB"], writes=["ss3b"])
                yield
                rstd_ops(ss3[:, 0:2], rs3[:, 0:2], 1024, ["ss3a", "ss3b"], ["rs3ab"])
                S.op("dve", STT(mixb[s][:, 0:1024].rearrange("p (h c) -> p h c", h=16), a3[:, :, 1:65],
                                rs3[:, 0:1], goab[:, 0:1024].rearrange("p (h c) -> p h c", h=16),
                                ALU.mult, ALU.mult), reads=["opA", "rs3ab", "goab", "misc3"], writes=[f"mixb{s}"])
                S.op("dve", STT(mixb[s][:, 1024:2048].rearrange("p (a c) -> p a c", a=8), b3[:, :, 1:129],
                                rs3[:, 1:2], goab[:, 1024:2048].rearrange("p (a c) -> p a c", a=8),
                                ALU.mult, ALU.mult), reads=["opB", "rs3ab", "goab", "misc3"], writes=[f"mixb{s}"])
                yield
                transposes(mixb[s], f"mixb{s}", mixT, lambda g, tt=tt: f"mixT{tt}_{g}", tt * 128, trot, evrot)
                yield

        plan3 = []
        for c in range(n_chunks):
            for cg in range(4):
                plan3.append((wout_s, "wout", 0, 512 * cg))
            for q in range(4):
                for gp in range(4):
                    plan3.append((w1_s, "w1", 0, 2048 * q + 512 * gp))
                for cg in range(4):
                    plan3.append((w2_s, "w2", 2048 * q, 512 * cg))
        pi3 = [0]

        def wl3(i):
            if i < len(plan3):
                scr, nm, r0, c0 = plan3[i]
                load_wpiece(wsl, i % 3, scr, nm, r0, c0, 512)

        def next_piece3():
            i = pi3[0]
            pi3[0] += 1
            wl3(i + 2)
            return i % 3

        wl3(0)
        wl3(1)
        drain(p3_prologue(0))
        for c in range(n_chunks):
            tok0 = 512 * c
            for cg in range(4):
                slot = next_piece3()
                for tt in range(4):
                    bk = mrot.next()
                    xr = xrot.next()
                    r0 = tok0 + 128 * tt
                    S.dma("sp", f"xres{xr}", xres[xr][:, :], xc[HALO + r0:HALO + r0 + 128, 512 * cg:512 * cg + 512],
                          writes=[f"xres{xr}"])
                    for kc in range(16):
                        S.op("pe", MM(PB[bk][:, :], mixT[:, kc, tt * 128:(tt + 1) * 128], wsl[slot][:, kc, :],
                                      kc == 0, kc == 15),
                             reads=[f"w{slot}", f"mixT{tt}_{kc // 4}"], writes=[f"ps{bk}"])
                    S.op("dve", TT(x1[:, tt, 512 * cg:512 * cg + 512], PB[bk][:, :], xres[xr][:, :], ALU.add),
                         reads=[f"ps{bk}", f"xres{xr}"], writes=[f"x1_{tt}"])
            for tt in range(4):
                s = tt % 2
                S.op("act", ACTF(junk[:], x1[:, tt, :], AF.Square, accum=ss3[:, 2:3]),
                     reads=[f"x1_{tt}"], writes=["ss3c"])
                rstd_ops(ss3[:, 2:3], rs3[:, 2:3], D, ["ss3c"], ["rs3c"])
                S.op("dve", STT(h2b[s][:], x1[:, tt, :], rs3[:, 2:3], gml[:], ALU.mult, ALU.mult),
                     reads=[f"x1_{tt}", "rs3c", "gml", "misc3"], writes=[f"h2b{s}"])
                transposes(h2b[s], f"h2b{s}", h2T, lambda g, tt=tt: f"h2T{tt}_{g}", tt * 128, trot, evrot)
            nxt = p3_prologue(c + 1) if c + 1 < n_chunks else None
            for q in range(4):
                for gp in range(4):
                    slot = next_piece3()
                    for j in range(4):
                        bk = mrot.next()
                        f = 4 * gp + j
                        for kc in range(16):
                            S.op("pe", MM(PB[bk][:, :], wsl[slot][:, kc, 128 * j:128 * j + 128], h2T[:, kc, :],
                                          kc == 0, kc == 15),
                                 reads=[f"w{slot}"] + [f"h2T{t}_{kc // 4}" for t in range(4)],
                                 writes=[f"ps{bk}"])
                        rr = rrot.next()
                        S.op("act", ACTF(r32[rr][:, :], PB[bk][:, :], AF.Relu), reads=[f"ps{bk}"],
                             writes=[f"r32_{rr}"])
                        S.op("pool", TT(uT[:, f, :], r32[rr][:, :], r32[rr][:, :], ALU.mult),
                             reads=[f"r32_{rr}"], writes=[f"uT{f}"])
                    if nxt is not None:
                        next(nxt, None)
                for cg in range(4):
                    slot = next_piece3()
                    for tt in range(4):
                        bk = mrot.next()
                        for f in range(16):
                            S.op("pe", MM(PB[bk][:, :], uT[:, f, tt * 128:(tt + 1) * 128], wsl[slot][:, f, :],
                                          f == 0, f == 15),
                                 reads=[f"w{slot}", f"uT{f}"], writes=[f"ps{bk}"])
                        xv = x1[:, tt, 512 * cg:512 * cg + 512]
                        S.op("dve", TT(xv, PB[bk][:, :], xv, ALU.add), reads=[f"ps{bk}", f"x1_{tt}"],
                             writes=[f"x1_{tt}"])
                    if nxt is not None:
                        next(nxt, None)
            drain(nxt)
            for tt in range(4):
                S.op("act", ACTF(junk[:], x1[:, tt, :], AF.Square, accum=ss3[:, 3:4]),
                     reads=[f"x1_{tt}"], writes=["ss3d"])
                rstd_ops(ss3[:, 3:4], rs3[:, 3:4], D, ["ss3d"], ["rs3d"])
                S.op("dve", STT(x1[:, tt, :], x1[:, tt, :], rs3[:, 3:4], gfi[:], ALU.mult, ALU.mult),
                     reads=[f"x1_{tt}", "rs3d", "gfi", "misc3"], writes=[f"x1_{tt}"])
                r0 = tok0 + 128 * tt
                S.dma("pool", f"outst{tt}", out[r0:r0 + 128, :], x1[:, tt, :], reads=[f"x1_{tt}"],
                      writes=[f"outd{tt}"])
        S.barrier()

    finals = [c for c in S.chan if c.startswith(("outst", "ost", "qst", "vst", "cvst"))]
    S.emit(final_wait_chans=finals)
    return nc, S


_CACHE = {}


def _core_inputs(x, hf_first, b, hf):
    xc = np.zeros((NTOK, D), np.float32)
    if hf == 0:
        xc[HALO:] = x[b, 0:NOWN]
    else:
        xc[:] = x[b, NOWN - HALO:2 * NOWN]
    return xc


def kernel(x, g_attn, w_in, b_in, sinks_a, g_out_a, g_out_b, w_out, g_mlp, w_1, w_2, g_final):
    x = np.asarray(x, np.float32)
    if "nc" not in _CACHE:
        _CACHE["nc"] = build()[0]
    nc = _CACHE["nc"]
    shared = {
        "w_in": np.ascontiguousarray(np.asarray(w_in, np.float32)[0]),
        "w_out": np.ascontiguousarray(np.asarray(w_out, np.float32)[0]),
        "w_1": np.ascontiguousarray(np.asarray(w_1, np.float32)[0]),
        "w_2": np.ascontiguousarray(np.asarray(w_2, np.float32)[0]),
        "g_attn": np.ascontiguousarray(np.asarray(g_attn, np.float32)[0]),
        "b_in": np.ascontiguousarray(np.asarray(b_in, np.float32)[0]),
        "sinks": np.ascontiguousarray(np.asarray(sinks_a, np.float32)[0]),
        "g_oab": np.concatenate([np.asarray(g_out_a, np.float32)[0], np.asarray(g_out_b, np.float32)[0]]),
        "g_mlp": np.ascontiguousarray(np.asarray(g_mlp, np.float32)[0]),
        "g_fin": np.ascontiguousarray(np.asarray(g_final, np.float32)),
    }
    in_maps = []
    for core in range(N_CORES):
        b, hf = core // 2, core % 2
        m = dict(shared)
        m["xc"] = _core_inputs(x, None, b, hf)
        m["hb"] = np.full((128, 1), NEG if hf == 0 else 0.0, np.float32)
        in_maps.append(m)
    res = run_bass_kernel_spmd(nc, in_maps, core_ids=list(range(N_CORES)))
    outp = np.empty((4, 2 * NOWN, D), np.float32)
    for core in range(N_CORES):
        b, hf = core // 2, core % 2
        outp[b, hf * NOWN:(hf + 1) * NOWN] = res.results[core]["out"]
    return outp
```

```python
import numpy as np
import concourse.bass as bass
import concourse.mybir as mybir
from concourse.bass_utils import run_bass_kernel_spmd

F32 = mybir.dt.float32
BF16 = mybir.dt.bfloat16
I32 = mybir.dt.int32
AF = mybir.ActivationFunctionType
ALU = mybir.AluOpType

D = 2048
NOWN = 4096
HALO = 2048
NTOK = 6144
DIN = 4352
DFF = 8192
EPS = 1e-5
NEG = -30000.0
SLOPES = [2.0 ** (-8.0 * (h + 1) / 16) for h in range(16)]
N_CORES = 8
P2_STAGE = 9


class Sched:
    ENGS = ("pe", "act", "dve", "pool", "sp")

    def __init__(self, nc):
        self.nc = nc
        self.ops = []
        self.last_w = {}
        self.readers = {}
        self.chan = {}
        self.chan_last = {}
        self.eng_last = {}
        self.pending = {}

    def _add(self, eng, fn, reads, writes, dma_chan=None):
        idx = len(self.ops)
        deps = set()
        for r in reads:
            if r in self.last_w:
                deps.add((self.last_w[r], "raw"))
        for w in writes:
            if w in self.last_w:
                deps.add((self.last_w[w], "waw"))
            for rd in self.readers.get(w, {}).values():
                deps.add((rd, "war"))
        if eng in self.pending:
            for j in self.pending.pop(eng):
                deps.add((j, "raw"))
        op = dict(eng=eng, fn=fn, deps=deps, dma=dma_chan, sig=False)
        if dma_chan is not None:
            self.chan[dma_chan] = self.chan.get(dma_chan, 0) + 16
            op["chan_val"] = self.chan[dma_chan]
            self.chan_last[dma_chan] = idx
        self.eng_last[eng] = idx
        self.ops.append(op)
        rkey = eng if dma_chan is None else ("dma", dma_chan)
        for r in reads:
            self.readers.setdefault(r, {})[rkey] = idx
        for w in writes:
            self.last_w[w] = idx
            self.readers[w] = {}
        return idx

    def op(self, eng, fn, reads=(), writes=()):
        return self._add(eng, fn, tuple(reads), tuple(writes))

    def dma(self, eng, chan, out, in_, reads=(), writes=(), **kw):
        def fn(e, out=out, in_=in_, kw=kw):
            return e.dma_start(out=out, in_=in_, **kw)
        return self._add(eng, fn, tuple(reads), tuple(writes), dma_chan=chan)

    def barrier(self, skip_engs=(), skip_chan_prefix=None):
        deps = [v for k, v in self.eng_last.items() if k not in skip_engs]
        deps += [v for k, v in self.chan_last.items()
                 if not (skip_chan_prefix and k.startswith(skip_chan_prefix))]
        for e in self.ENGS:
            if e in skip_engs:
                continue
            self.pending[e] = list(set(self.pending.get(e, []) + deps))

    def emit(self, final_wait_chans=()):
        from contextlib import ExitStack
        nc = self.nc
        ops = self.ops
        for op in ops:
            waits = []
            for (j, kind) in op["deps"]:
                J = ops[j]
                if J["dma"] is not None:
                    waits.append(("chan", J["dma"], J["chan_val"]))
                elif J["eng"] == op["eng"]:
                    if op["eng"] == "pe":
                        continue
                    J["sig"] = True
                    waits.append(("eng", J["eng"], j))
                else:
                    J["sig"] = True
                    waits.append(("eng", J["eng"], j))
            op["waits"] = waits
        cnt = {e: 0 for e in self.ENGS}
        for op in ops:
            if op["sig"] and op["dma"] is None:
                cnt[op["eng"]] += 1
                op["sigval"] = cnt[op["eng"]]
        per_eng = {e: [] for e in self.ENGS}
        waited = {e: {} for e in self.ENGS}
        for op in ops:
            w2 = {}
            for w in op["waits"]:
                if w[0] == "chan":
                    key, val = ("chan", w[1]), w[2]
                else:
                    key, val = ("eng", w[1]), ops[w[2]]["sigval"]
                if waited[op["eng"]].get(key, 0) >= val:
                    continue
                w2[key] = max(w2.get(key, 0), val)
            for k, v in w2.items():
                waited[op["eng"]][k] = v
            op["w2"] = w2
            per_eng[op["eng"]].append(op)
        self.stats = dict(n_ops=len(ops), n_sem=len(self.chan) + 5,
                          per_eng={e: len(v) for e, v in per_eng.items()}, sig=dict(cnt))
        with ExitStack() as st:
            sems = {}
            for e in self.ENGS:
                sems[("eng", e)] = st.enter_context(nc.semaphore("s_" + e))
            for c in self.chan:
                sems[("chan", c)] = st.enter_context(nc.semaphore("c_" + str(c)))
            block = st.enter_context(nc.Block())

            def run(engobj, lst):
                for op in lst:
                    for k, v in op["w2"].items():
                        engobj.wait_ge(sems[k], v)
                    ins = op["fn"](engobj)
                    if op["dma"] is not None:
                        ins.then_inc(sems[("chan", op["dma"])], 16)
                    elif op["sig"]:
                        ins.then_inc(sems[("eng", op["eng"])], 1)

            @block.tensor
            def _(e):
                run(e, per_eng["pe"])

            @block.scalar
            def _(e):
                run(e, per_eng["act"])

            @block.vector
            def _(e):
                run(e, per_eng["dve"])

            @block.gpsimd
            def _(e):
                run(e, per_eng["pool"])

            @block.sync
            def _(e):
                run(e, per_eng["sp"])
                for c in final_wait_chans:
                    e.wait_ge(sems[("chan", c)], self.chan[c])


class Arena:
    def __init__(self, nc):
        self.nc = nc
        self.lo = ((nc.SBUF_PARTITION_SIZE_BYTES - nc.sbuf_bytes_remaining + 63) // 64) * 64
        self.hi = nc.SBUF_PARTITION_SIZE_BYTES
        self.cur = self.lo
        self.n = 0

    def alloc(self, name, shape, dt):
        nbytes = int(np.prod(shape[1:])) * (4 if dt in (F32, I32) else 2)
        nbytes = ((nbytes + 63) // 64) * 64
        off = self.cur
        assert off + nbytes <= self.hi, f"SBUF overflow at {name}: {off + nbytes} > {self.hi}"
        self.cur += nbytes
        self.n += 1
        return self.nc.alloc_sbuf_tensor_at(f"{name}_{self.n}", list(shape), dt, offset=off)

    def mark(self):
        return self.cur

    def reset(self, m):
        self.cur = m


class Rot:
    def __init__(self, items):
        self.items = list(items)
        self.i = 0

    def next(self):
        v = self.items[self.i % len(self.items)]
        self.i += 1
        return v


def MM(o, l, r, start, stop, skip=False):
    return lambda e: e.matmul(o, lhsT=l, rhs=r, start=start, stop=stop, skip_group_check=skip)


def ACTF(o, i, func, bias=None, scale=1.0, accum=None):
    def f(e):
        kw = {}
        if bias is not None:
            kw["bias"] = bias
        if accum is not None:
            kw["accum_out"] = accum
        return e.activation(out=o, in_=i, func=func, scale=scale, **kw)
    return f


def TT(o, a, b, op):
    return lambda e: e.tensor_tensor(out=o, in0=a, in1=b, op=op)


def STT(o, a, s, b, op0, op1):
    return lambda e: e.scalar_tensor_tensor(out=o, in0=a, scalar=s, in1=b, op0=op0, op1=op1)


def TS(o, a, s1, s2, op0, op1=None):
    if op1 is None:
        return lambda e: e.tensor_scalar(out=o, in0=a, scalar1=s1, scalar2=None, op0=op0)
    return lambda e: e.tensor_scalar(out=o, in0=a, scalar1=s1, scalar2=s2, op0=op0, op1=op1)


def CP(o, i):
    return lambda e: e.tensor_copy(out=o, in_=i)


def ACP(o, i):
    return lambda e: e.copy(out=o, in_=i)


def RCP(o, i):
    return lambda e: e.reciprocal(out=o, in_=i)


def MS(ap, v):
    return lambda e: e.memset(ap, v)


def build(phases=(0, 1, 2, 3), dbg=False, n_chunks=8, p2_items=None):
    nc = bass.Bass("TRN2", target_bir_lowering=False)

    def dram(name, shape, dt, kind):
        return nc.dram_tensor(name, list(shape), dt, kind=kind).ap()

    IN, OUT, INT = "ExternalInput", "ExternalOutput", "Internal"
    SCR = OUT if dbg else INT
    P = set(phases)
    xc = dram("xc", [NTOK, D], F32, IN) if P & {1, 3} else None
    if 0 in P:
        w_in = dram("w_in", [D, DIN], F32, IN)
        w_out = dram("w_out", [D, D], F32, IN)
        w_1 = dram("w_1", [D, DFF], F32, IN)
        w_2 = dram("w_2", [DFF, D], F32, IN)
    if 1 in P:
        g_attn = dram("g_attn", [D], F32, IN)
        b_in = dram("b_in", [DIN], F32, IN)
    if 3 in P:
        sinks = dram("sinks", [16], F32, IN)
        g_oab = dram("g_oab", [2048], F32, IN)
        g_mlp = dram("g_mlp", [D], F32, IN)
        g_fin = dram("g_fin", [D], F32, IN)
    hb = dram("hb", [128, 1], F32, IN)
    out = dram("out", [NOWN, D], F32, OUT)

    win_s = dram("win_s", [D, DIN], BF16, INT)
    wout_s = dram("wout_s", [D, D], BF16, INT)
    w1_s = dram("w1_s", [D, DFF], BF16, INT)
    w2_s = dram("w2_s", [DFF, D], BF16, INT)
    qta = dram("qta", [8, 128, NOWN], BF16, SCR)
    qtb = dram("qtb", [8, 128, NOWN], BF16, SCR)
    ktb = dram("ktb", [8, 128, NTOK], BF16, SCR)
    kta = dram("kta", [2, 128, NTOK], BF16, SCR)
    va = dram("va", [NTOK, 128], BF16, SCR)
    vb = dram("vb", [NTOK, 1024], BF16, SCR)
    opart = dram("opart", [4, NOWN, 8, 130], F32, SCR)

    S = Sched(nc)
    A = Arena(nc)
    PBALL = nc.alloc_psum_tensor("pball", [128, 4096], F32)
    PB = [PBALL[:, 512 * i:512 * i + 512] for i in range(8)]
    PS2 = [PBALL[:, 1024 * i:1024 * i + 1024] for i in range(3)]

    def pbf(i):
        return PB[i].bitcast(BF16)

    ident = A.alloc("ident", [128, 128], BF16)
    zcol = A.alloc("zcol", [128, 1], F32)
    hbt = A.alloc("hbt", [128, 1], F32)
    epsc = A.alloc("epsc", [128, 1], F32)
    S.op("pool", MS(ident[:], 0.0), writes=["ident"])
    S.op("pool", lambda e: e.affine_select(out=ident[:], in_=ident[:], pattern=[[-1, 128]],
                                           compare_op=ALU.not_equal, fill=1.0, base=0,
                                           channel_multiplier=1),
         reads=["ident"], writes=["ident"])
    S.op("pool", MS(zcol[:], 0.0), writes=["zcol"])
    S.op("pool", MS(epsc[:], EPS), writes=["epsc"])
    S.dma("sp", "misc", hbt[:], hb, writes=["hbt", "misc"])
    base_mark = A.mark()

    CVW = 2176
    cv32 = [A.alloc(f"cv32_{i}", [128, CVW], F32) for i in range(2)]
    cvbf = [A.alloc(f"cvbf_{i}", [128, CVW], BF16) for i in range(2)]
    pieces = []
    if 0 in P:
        for (nm, src, dst, R, C, pw) in (("win", w_in, win_s, D, DIN, 2176), ("wout", w_out, wout_s, D, D, 2048),
                                         ("w1", w_1, w1_s, D, DFF, 2048), ("w2", w_2, w2_s, DFF, D, 2048)):
            for rc in range(R // 128):
                for c0 in range(0, C, pw):
                    pieces.append((nm, src[rc * 128:(rc + 1) * 128, c0:c0 + pw],
                                   dst[rc * 128:(rc + 1) * 128, c0:c0 + pw], pw))
    n_win = sum(1 for p_ in pieces if p_[0] == "win")
    def CV(nm):
        return [f"cvst_{s}_{nm}" for s in range(2)]

    stp_i = A.alloc("stp_i", [128, 256], I32)
    stp = A.alloc("stp", [128, 256], F32)
    vbA = A.alloc("vbA", [128, 256], F32)
    vbB = A.alloc("vbB", [128, 256], F32)
    S.op("pool", lambda e: e.iota(stp_i[:], pattern=[[1, 256]], base=0, channel_multiplier=-1),
         writes=["stp_i"])
    S.op("pool", CP(stp[:], stp_i[:]), reads=["stp_i"], writes=["stp"])
    for (t, ms, nm) in ((vbA, 127, "vbA"), (vbB, 128, "vbB")):
        S.op("pool", MS(t[:], 0.0), writes=[nm])
        S.op("pool", (lambda t: (lambda e: e.affine_select(
            out=t[:], in_=t[:], pattern=[[1, 256]], compare_op=ALU.is_ge, fill=NEG, base=0,
            channel_multiplier=-1)))(t), reads=[nm], writes=[nm])
        S.op("pool", (lambda t, ms: (lambda e: e.affine_select(
            out=t[:], in_=t[:], pattern=[[-1, 256]], compare_op=ALU.is_ge, fill=NEG, base=ms,
            channel_multiplier=1)))(t, ms), reads=[nm], writes=[nm])
    p12_mark = A.mark()

    for k in range(n_win):
        nm, src, dst, w = pieces[k]
        s_ = k % 2
        S.dma("sp", f"cv32_{s_}", cv32[s_][:, :w], src, writes=[f"cv32_{s_}"])
        eng = ("pool", "dve", "act")[k % 3]
        if eng == "act":
            S.op("act", ACP(cvbf[s_][:, :w], cv32[s_][:, :w]), reads=[f"cv32_{s_}"], writes=[f"cvbf_{s_}"])
        else:
            S.op(eng, CP(cvbf[s_][:, :w], cv32[s_][:, :w]), reads=[f"cv32_{s_}"], writes=[f"cvbf_{s_}"])
        S.dma("pool", f"cvst_{s_}", dst, cvbf[s_][:, :w], reads=[f"cvbf_{s_}"], writes=[f"cvst_{s_}_{nm}"])
    cvs = dict(idx=n_win, pending=None)

    def pump(n=1):
        for _ in range(n):
            if cvs["pending"] is not None:
                k = cvs["pending"]
                nm, src, dst, w = pieces[k]
                s_ = k % 2
                if k % 2 == 0:
                    S.op("dve", CP(cvbf[s_][:, :w], cv32[s_][:, :w]), reads=[f"cv32_{s_}"], writes=[f"cvbf_{s_}"])
                else:
                    S.op("act", ACP(cvbf[s_][:, :w], cv32[s_][:, :w]), reads=[f"cv32_{s_}"], writes=[f"cvbf_{s_}"])
                S.dma("pool", f"cvst_{s_}", dst, cvbf[s_][:, :w], reads=[f"cvbf_{s_}"], writes=[f"cvst_{s_}_{nm}"])
                cvs["pending"] = None
            if cvs["idx"] < len(pieces):
                k = cvs["idx"]
                nm, src, dst, w = pieces[k]
                s_ = k % 2
                S.dma("pool", f"cv32_{s_}", cv32[s_][:, :w], src, writes=[f"cv32_{s_}"])
                cvs["pending"] = k
                cvs["idx"] += 1

    def conv_drain():
        while cvs["idx"] < len(pieces) or cvs["pending"] is not None:
            pump()

    def rstd_ops(ssap, rsap, n, reads, writes):
        S.op("act", ACTF(rsap, ssap, AF.Sqrt, bias=epsc[:, 0:1], scale=1.0 / n), reads=list(reads) + ["epsc"],
             writes=writes)
        S.op("dve", RCP(rsap, rsap), reads=writes, writes=writes)

    def transposes(src_bf, src_res, dstT, dst_res_fn, tcol, trot, evrot):
        for g in range(4):
            bk = trot.next()
            for j in range(4):
                kc = 4 * g + j
                o = pbf(bk)[:, j * 128:(j + 1) * 128]
                i_ = src_bf[:, kc * 128:(kc + 1) * 128]
                S.op("pe", (lambda o, i_: (lambda e: e.transpose(out=o, in_=i_, identity=ident[:])))(o, i_),
                     reads=[src_res, "ident"], writes=[f"ps{bk}"])
            ev = evrot.next()
            o = dstT[:, 4 * g:4 * g + 4, tcol:tcol + 128]
            i_ = pbf(bk)[:, 0:512].rearrange("p (a b) -> p a b", a=4)
            if ev == "dve":
                S.op("dve", CP(o, i_), reads=[f"ps{bk}"], writes=[dst_res_fn(g)])
            else:
                S.op("act", ACP(o, i_), reads=[f"ps{bk}"], writes=[dst_res_fn(g)])

    def load_wpiece(wsl, slot, scr, wname, r0, c0, ncols, dcol0=0):
        S.dma("sp", f"w{slot}", wsl[slot][:, :, dcol0:dcol0 + ncols],
              scr[r0:r0 + 2048, c0:c0 + ncols].rearrange("(k p) c -> p k c", p=128),
              reads=CV(wname), writes=[f"w{slot}"])

    def drain(gen):
        if gen is not None:
            for _ in gen:
                pass

    if 1 in phases:
        hT = [A.alloc(f"hT{i}", [128, 16, 1024], BF16) for i in range(2)]
        xs = [A.alloc(f"xs{i}", [128, 2048], F32) for i in range(3)]
        hbf = [A.alloc(f"hbf{i}", [128, 2048], BF16) for i in range(3)]
        gat = A.alloc("gat", [128, 2048], F32)
        ss = A.alloc("ss", [128, 3], F32)
        rs = A.alloc("rs", [128, 3], F32)
        binT = A.alloc("binT", [128, 34], F32)
        binT8 = A.alloc("binT8", [128, 34], F32)
        bka = A.alloc("bka", [128, 2], F32)
        bv = A.alloc("bv", [128, 1152], F32)
        wsl = [A.alloc(f"wsl{i}", [128, 16, 512], BF16) for i in range(3)]
        qst = [A.alloc(f"qst{i}", [128, 1024], BF16) for i in range(3)]
        vst = [A.alloc(f"vst{i}", [128, 512], BF16) for i in range(4)]

        S.dma("sp", "misc", gat[:], g_attn.partition_broadcast(128), writes=["gat", "misc"])
        for c0 in range(0, 34, 6):
            c1 = min(34, c0 + 6)
            S.dma("sp", "misc", binT[:, c0:c1], b_in.rearrange("(c p) -> p c", p=128)[:, c0:c1],
                  writes=["binT", "misc"], allow_slow_non_contiguous=True)
        for g in range(2):
            for hh in range(2):
                S.dma("sp", "misc", bka[64 * hh:64 * hh + 64, g:g + 1],
                      b_in[1024 + 64 * g:1024 + 64 * g + 64].rearrange("(p o) -> p o", o=1),
                      writes=["bka", "misc"])
        S.dma("sp", "misc", bv[:, 0:1024], b_in[3328:4352].partition_broadcast(128), writes=["bv", "misc"])
        S.dma("sp", "misc", bv[:, 1024:1152], b_in[1152:1280].partition_broadcast(128), writes=["bv", "misc"])
        S.op("dve", TS(binT8[:], binT[:], 0.125, None, ALU.mult), reads=["binT", "misc"], writes=["binT8"])
        trot = Rot([0, 1])
        mrot = Rot([2, 3, 4, 5, 6, 7])
        evrot = Rot(["dve", "act"])
        wrot = Rot([0, 1, 2])
        qrot = Rot([0, 1, 2])
        vrot = Rot([0, 1, 2, 3])
        NCH1 = 6

        def p1_prologue(c):
            T0 = 1024 * c
            hb_ = c % 2

            def L(tt):
                s = tt % 3
                S.dma("sp", f"xs{s}", xs[s][:], xc[T0 + tt * 128:T0 + (tt + 1) * 128, :], writes=[f"xs{s}"])

            def N(tt):
                s = tt % 3
                S.op("act", ACTF(hbf[s][:], xs[s][:], AF.Square, accum=ss[:, s:s + 1]),
                     reads=[f"xs{s}"], writes=[f"ss{s}", f"hbf{s}"])
                rstd_ops(ss[:, s:s + 1], rs[:, s:s + 1], D, [f"ss{s}"], [f"rs{s}"])
                S.op("dve", STT(hbf[s][:], xs[s][:], rs[:, s:s + 1], gat[:], ALU.mult, ALU.mult),
                     reads=[f"xs{s}", f"rs{s}", "gat", "misc"], writes=[f"hbf{s}"])

            def T(tt):
                s = tt % 3
                transposes(hbf[s], f"hbf{s}", hT[hb_], lambda g, tt=tt: f"hT{hb_}_{tt}_{g}", tt * 128, trot, evrot)

            for step in range(8 + 4):
                if step < 8:
                    L(step)
                    yield
                if 0 <= step - 2 < 8:
                    N(step - 2)
                    yield
                if 0 <= step - 4 < 8:
                    T(step - 4)
                    yield

        def fm_group(c, slot, wc0, bias_col, scale, dstap):
            hb_ = c % 2
            q = qrot.next()
            for tq in range(2):
                bk = mrot.next()
                for kc in range(16):
                    S.op("pe", MM(PB[bk][:, :], wsl[slot][:, kc, wc0:wc0 + 128],
                                  hT[hb_][:, kc, tq * 512:(tq + 1) * 512], kc == 0, kc == 15),
                         reads=[f"w{slot}"] + [f"hT{hb_}_{4 * tq + t}_{kc // 4}" for t in range(4)],
                         writes=[f"ps{bk}"])
                S.op("act", ACTF(qst[q][:, tq * 512:(tq + 1) * 512], PB[bk][:, :], AF.Identity,
                                 bias=bias_col, scale=scale),
                     reads=[f"ps{bk}", "binT8", "misc"], writes=[f"qst{q}"])
            S.dma("sp", f"qst{q}", dstap, qst[q][:], reads=[f"qst{q}"], writes=[f"qstd{q}"])

        def tm_piece(c, slot, wc0, n, dstap_fn, bcol0, tick):
            hb_ = c % 2
            for tt in range(8):
                bk = mrot.next()
                v = vrot.next()
                for kc in range(16):
                    S.op("pe", MM(PB[bk][:, 0:n], hT[hb_][:, kc, tt * 128:(tt + 1) * 128],
                                  wsl[slot][:, kc, wc0:wc0 + n], kc == 0, kc == 15),
                         reads=[f"w{slot}", f"hT{hb_}_{tt}_{kc // 4}"], writes=[f"ps{bk}"])
                S.op("dve", TT(vst[v][:, 0:n], PB[bk][:, 0:n], bv[:, bcol0:bcol0 + n], ALU.add),
                     reads=[f"ps{bk}", "bv", "misc"], writes=[f"vst{v}"])
                S.dma("sp", f"vst{v}", dstap_fn(tt), vst[v][:, 0:n], reads=[f"vst{v}"], writes=[f"vstd{v}"])
                if tt % 2 == 1:
                    tick()

        plan = []

        def mk_simple(c0):
            return lambda slot: load_wpiece(wsl, slot, win_s, "win", 0, c0, 512)

        def mk_kava():
            def f(slot):
                for g in range(2):
                    for hh in range(2):
                        load_wpiece(wsl, slot, win_s, "win", 0, 1024 + 64 * g, 64, dcol0=128 * g + 64 * hh)
                load_wpiece(wsl, slot, win_s, "win", 0, 1152, 128, dcol0=256)
            return f

        for c in range(NCH1):
            plan += [mk_simple(2304), mk_simple(2816), mk_simple(3328), mk_simple(3840), mk_kava()]
            if c >= 2:
                plan += [mk_simple(0), mk_simple(512), mk_simple(1280), mk_simple(1792)]
        pi_ = [0]

        def wl(i):
            if i < len(plan):
                plan[i](i % 3)

        def next_piece():
            i = pi_[0]
            pi_[0] += 1
            wl(i + 2)
            return i % 3

        wl(0)
        wl(1)
        drain(p1_prologue(0))
        for c in range(NCH1):
            T0 = 1024 * c
            own = c >= 2
            nxt = p1_prologue(c + 1) if c + 1 < NCH1 else None
            tk = [0]

            def tick():
                tk[0] += 1
                pump(1)
                if nxt is not None:
                    next(nxt, None)

            for p in range(2):
                slot = next_piece()
                for j in range(4):
                    fc = 4 * p + j
                    fm_group(c, slot, 128 * j, binT[:, 18 + fc:18 + fc + 1], 1.0, ktb[fc, :, T0:T0 + 1024])
                    tick()
            for p in range(2):
                slot = next_piece()
                tm_piece(c, slot, 0, 512,
                         lambda tt, T0=T0, p=p: vb[T0 + tt * 128:T0 + (tt + 1) * 128, 512 * p:512 * p + 512],
                         512 * p, tick)
            slot = next_piece()
            for g in range(2):
                fm_group(c, slot, 128 * g, bka[:, g:g + 1], 1.0, kta[g, :, T0:T0 + 1024])
                tick()
            tm_piece(c, slot, 256, 128, lambda tt, T0=T0: va[T0 + tt * 128:T0 + (tt + 1) * 128, :], 1024, tick)
            if own:
                for (c0, dst, b0) in ((0, qta, 0), (1280, qtb, 10)):
                    for p in range(2):
                        slot = next_piece()
                        for j in range(4):
                            fc = 4 * p + j
                            fm_group(c, slot, 128 * j, binT8[:, b0 + fc:b0 + fc + 1], 0.125,
                                     dst[fc, :, T0 - HALO:T0 - HALO + 1024])
                            tick()
            drain(nxt)
        conv_drain()
        S.barrier()
    else:
        conv_drain()
        S.barrier()
    A.reset(p12_mark)

    if 2 in phases:
        qT = [[A.alloc(f"qT{i}_{h}", [128, NOWN], BF16) for h in range(2)] for i in range(2)]
        kT = [A.alloc(f"kT{i}", [128, NTOK], BF16) for i in range(2)]
        vS = [A.alloc(f"vS{i}", [128, 48, 130], BF16) for i in range(2)]
        bias2 = [A.alloc(f"bias2_{i}", [128, 512], F32) for i in range(2)]
        s32 = [A.alloc(f"s32_{i}", [128, 512], F32) for i in range(3)]
        pT = [A.alloc(f"pT{i}", [128, 512], BF16) for i in range(5)]
        ost = [A.alloc(f"ost{i}", [128, 130], F32) for i in range(8)]

        for i in range(2):
            S.op("dve", MS(vS[i][:, :, 0:1], 1.0), writes=[f"vS{i}"])
            S.op("dve", MS(vS[i][:, :, 129:130], 1.0), writes=[f"vS{i}"])
            S.op("dve", MS(qT[i][0][64:128, :], 0.0), writes=[f"qT{i}"])
            S.op("dve", MS(qT[i][1][0:64, :], 0.0), writes=[f"qT{i}"])

        passes = [("A", 1, 0), ("B", 1, 1), ("B", 4, 2), ("B", 16, 3)]
        items = [(pi, hp) for pi in range(4) for hp in range(8)]
        if p2_items is not None:
            items = p2_items
        srot = Rot([0, 1, 2])
        s3rot = Rot([0, 1, 2])
        prot = Rot([0, 1, 2, 3, 4])
        orot = Rot(list(range(8)))
        obank = Rot([6, 7])
        oev = Rot(["dve", "act"])

        def p2_loads(it, pi, hp):
            kind, d, pidx = passes[pi]
            b = it % 2
            isA = kind == "A"
            nt = 48 // d
            qsrc = (qta if isA else qtb)[hp]
            for h in range(2):
                S.dma("sp", f"qT{b}", qT[b][h][64 * h:64 * h + 64, :], qsrc[64 * h:64 * h + 64, :], writes=[f"qT{b}"])
            S.dma("sp", f"kT{b}", kT[b][:], kta[hp // 4] if isA else ktb[hp], writes=[f"kT{b}"])
            for r in range(d):
                for j0 in range(0, nt, 8):
                    j1 = min(nt, j0 + 8)
                    if isA:
                        g = hp // 4
                        src = va[:, 64 * g:64 * g + 64].rearrange("(jt m) c -> m jt c", m=128)[:, j0:j1, :]
                        S.dma("sp", f"vS{b}", vS[b][:, j0:j1, 1:65], src, writes=[f"vS{b}"])
                    else:
                        src = vb[r::d, 128 * hp:128 * hp + 128].rearrange("(jt m) c -> m jt c", m=128)[:, j0:j1, :]
                        S.dma("sp", f"vS{b}", vS[b][:, r * nt + j0:r * nt + j1, 1:129], src, writes=[f"vS{b}"])

        def p2_compute(it, pi, hp):
            kind, d, pidx = passes[pi]
            b = it % 2
            isA = kind == "A"
            nt = 48 // d
            jh = 16 // d
            vbt = vbA if isA else vbB
            for h in range(2):
                S.op("dve", STT(bias2[b][:, 256 * h:256 * h + 256], stp[:], -SLOPES[2 * hp + h] * d, vbt[:],
                                ALU.mult, ALU.add), reads=["stp", "vbA", "vbB"], writes=[f"bias2_{b}"])

            def score(r, jt):
                n0 = 128 if jt == jh - 1 else 0
                n1 = 128 if jt == nt - 1 else 256
                sb = srot.next()
                ks = r + 128 * d * jt
                qs = r + d * (128 * jt + n0) - HALO
                nq = n1 - n0
                for h in range(2):
                    S.op("pe", MM(PS2[sb][:, 512 * h + n0:512 * h + n0 + nq],
                                  kT[b][:, ks:ks + 127 * d + 1:d],
                                  qT[b][h][:, qs:qs + (nq - 1) * d + 1:d], True, True),
                         reads=[f"kT{b}", f"qT{b}"], writes=[f"ps2_{sb}"])
                s3 = s3rot.next()
                p = prot.next()
                pv = PS2[sb].rearrange("p (h n) -> p h n", h=2)[:, :, n0:n1]
                bvw = bias2[b][:, :].rearrange("p (h n) -> p h n", h=2)[:, :, n0:n1]
                sv = s32[s3][:, :].rearrange("p (h n) -> p h n", h=2)[:, :, n0:n1]
                ptv = pT[p][:, :].rearrange("p (h n) -> p h n", h=2)[:, :, n0:n1]
                S.op("dve", TT(sv, pv, bvw, ALU.add), reads=[f"ps2_{sb}", f"bias2_{b}"], writes=[f"s32_{s3}"])
                col = hbt if jt < jh else zcol
                S.op("act", ACTF(ptv, sv, AF.Exp, bias=col[:, 0:1], scale=1.0),
                     reads=[f"s32_{s3}", "hbt", "zcol"], writes=[f"pT{p}"])
                return p

            def pv_q(r, jq, p_prev, p_cur):
                ob = obank.next()
                for h in range(2):
                    rc0 = 0 if (isA or h == 0) else 65
                    S.op("pe", MM(PB[ob][:, 65 * h:65 * h + 65],
                                  pT[p_prev][:, 256 * h + 128:256 * h + 256],
                                  vS[b][:, r * nt + jq - 1, rc0:rc0 + 65], True, False, skip=True),
                         reads=[f"pT{p_prev}", f"vS{b}"], writes=[f"ps{ob}"])
                    S.op("pe", MM(PB[ob][:, 65 * h:65 * h + 65],
                                  pT[p_cur][:, 256 * h:256 * h + 128],
                                  vS[b][:, r * nt + jq, rc0:rc0 + 65], False, True, skip=True),
                         reads=[f"pT{p_cur}", f"vS{b}"], writes=[f"ps{ob}"])
                o = orot.next()
                if oev.next() == "dve":
                    S.op("dve", CP(ost[o][:, :], PB[ob][:, 0:130]), reads=[f"ps{ob}"], writes=[f"ost{o}"])
                else:
                    S.op("act", ACP(ost[o][:, :], PB[ob][:, 0:130]), reads=[f"ps{ob}"], writes=[f"ost{o}"])
                t0 = r + 128 * d * jq - HALO
                S.dma("pool", f"ost{o}", opart[pidx, t0:t0 + 127 * d + 1:d, hp, :], ost[o][:, :],
                      reads=[f"ost{o}"], writes=[f"ostd{o}"])

            sc_list, pv_list = [], []
            for r in range(d):
                for jt in range(jh - 1, nt):
                    sc_list.append((r, jt))
                    if jt >= jh:
                        pv_list.append((r, jt, len(sc_list) - 2, len(sc_list) - 1))
            LA = 2
            slot_of = {}
            si = 0
            for (r, jq, ip, ic) in pv_list:
                while si <= min(ic + LA, len(sc_list) - 1):
                    slot_of[si] = score(*sc_list[si])
                    si += 1
                pv_q(r, jq, slot_of[ip], slot_of[ic])

        if items:
            p2_loads(0, *items[0])
        for it, (pi, hp) in enumerate(items):
            if it + 1 < len(items):
                p2_loads(it + 1, *items[it + 1])
            p2_compute(it, pi, hp)
        S.barrier()
    A.reset(base_mark)

    if 3 in phases:
        x1 = A.alloc("x1", [128, 4, 2048], F32)
        opA = A.alloc("opA", [128, 8 * 130], F32)
        opB = A.alloc("opB", [128, 3, 8 * 130], F32)
        junk = A.alloc("junk3", [128, 2048], BF16)
        mixb = [A.alloc(f"mixb{i}", [128, 2048], BF16) for i in range(2)]
        h2b = [A.alloc(f"h2b{i}", [128, 2048], BF16) for i in range(2)]
        mixT = A.alloc("mixT", [128, 16, 512], BF16)
        h2T = A.alloc("h2T", [128, 16, 512], BF16)
        uT = A.alloc("uT", [128, 16, 512], BF16)
        r32 = [A.alloc(f"r32_{i}", [128, 512], F32) for i in range(2)]
        xres = [A.alloc(f"xres{i}", [128, 512], F32) for i in range(3)]
        wsl = [A.alloc(f"wsl3_{i}", [128, 16, 512], BF16) for i in range(3)]
        goab = A.alloc("goab", [128, 2048], F32)
        gml = A.alloc("gml", [128, 2048], F32)
        gfi = A.alloc("gfi", [128, 2048], F32)
        esink = A.alloc("esink", [128, 16], F32)
        dA = A.alloc("dA", [128, 16], F32)
        dB = A.alloc("dB", [128, 16], F32)
        ss3 = A.alloc("ss3", [128, 4], F32)
        rs3 = A.alloc("rs3", [128, 4], F32)

        S.dma("sp", "misc3", goab[:], g_oab.partition_broadcast(128), writes=["goab", "misc3"])
        S.dma("sp", "misc3", gml[:], g_mlp.partition_broadcast(128), writes=["gml", "misc3"])
        S.dma("sp", "misc3", gfi[:], g_fin.partition_broadcast(128), writes=["gfi", "misc3"])
        S.dma("sp", "misc3", esink[:], sinks.partition_broadcast(128), writes=["esink", "misc3"])
        S.op("act", ACTF(esink[:], esink[:], AF.Exp), reads=["esink", "misc3"], writes=["esink"])
        trot = Rot([0, 1])
        mrot = Rot([2, 3, 4, 5, 6, 7])
        evrot = Rot(["dve", "act"])
        wrot = Rot([0, 1, 2])
        rrot = Rot([0, 1])
        xrot = Rot([0, 1, 2])
        a3 = opA[:, :].rearrange("p (h c) -> p h c", h=16)
        b3 = opB[:, 0, :].rearrange("p (a c) -> p a c", a=8)
        dB3 = dB[:, :].rearrange("p (a t) -> p a t", a=8)
        b3o = b3[:, :, 1:129].rearrange("p a (t c) -> p a t c", t=2)

        def p3_prologue(c):
            tok0 = 512 * c
            for tt in range(4):
                r0 = tok0 + 128 * tt
                s = tt % 2
                S.dma("sp", "opA", opA[:, :], opart[0, r0:r0 + 128].rearrange("t h c -> t (h c)"), writes=["opA"])
                S.dma("sp", "opB", opB[:, :, :], opart[1:4, r0:r0 + 128].rearrange("p t h c -> t p (h c)"),
                      writes=["opB"])
                yield
                S.op("dve", TT(dA[:, :], a3[:, :, 0], esink[:, :], ALU.add), reads=["opA", "esink"], writes=["dA"])
                S.op("dve", RCP(dA[:, :], dA[:, :]), reads=["dA"], writes=["dA"])
                S.op("dve", TT(a3[:, :, 1:65], a3[:, :, 1:65],
                               dA[:, :].unsqueeze(2).to_broadcast([128, 16, 64]), ALU.mult),
                     reads=["opA", "dA"], writes=["opA"])
                S.op("act", ACTF(junk[:, 0:1024].rearrange("p (h c) -> p h c", h=16), a3[:, :, 1:65],
                                 AF.Square, accum=ss3[:, 0:1]), reads=["opA"], writes=["ss3a"])
                S.op("dve", TT(opB[:, 0, :], opB[:, 0, :], opB[:, 1, :], ALU.add), reads=["opB"], writes=["opB"])
                S.op("dve", TT(opB[:, 0, :], opB[:, 0, :], opB[:, 2, :], ALU.add), reads=["opB"], writes=["opB"])
                S.op("dve", RCP(dB3, b3[:, :, 0::129]), reads=["opB"], writes=["dB"])
                S.op("dve", TT(b3o, b3o, dB3.unsqueeze(3).to_broadcast([128, 8, 2, 64]), ALU.mult),
                     reads=["opB", "dB"], writes=["opB"])
                S.op("act", ACTF(junk[:, 1024:2048].rearrange("p (a c) -> p a c", a=8), b3[:, :, 1:129],
                                 AF.Square, accum=ss3[:, 1:2]), reads=["opB"], writes=["ss3b"])
                yield
                rstd_ops(ss3[:, 0:2], rs3[:, 0:2], 1024, ["ss3a", "ss3b"], ["rs3ab"])
                S.op("dve", STT(mixb[s][:, 0:1024].rearrange("p (h c) -> p h c", h=16), a3[:, :, 1:65],
                                rs3[:, 0:1], goab[:, 0:1024].rearrange("p (h c) -> p h c", h=16),
                                ALU.mult, ALU.mult), reads=["opA", "rs3ab", "goab", "misc3"], writes=[f"mixb{s}"])
                S.op("dve", STT(mixb[s][:, 1024:2048].rearrange("p (a c) -> p a c", a=8), b3[:, :, 1:129],
                                rs3[:, 1:2], goab[:, 1024:2048].rearrange("p (a c) -> p a c", a=8),
                                ALU.mult, ALU.mult), reads=["opB", "rs3ab", "goab", "misc3"], writes=[f"mixb{s}"])
                yield
                transposes(mixb[s], f"mixb{s}", mixT, lambda g, tt=tt: f"mixT{tt}_{g}", tt * 128, trot, evrot)
                yield

        plan3 = []
        for c in range(n_chunks):
            for cg in range(4):
                plan3.append((wout_s, "wout", 0, 512 * cg))
            for q in range(4):
                for gp in range(4):
                    plan3.append((w1_s, "w1", 0, 2048 * q + 512 * gp))
                for cg in range(4):
                    plan3.append((w2_s, "w2", 2048 * q, 512 * cg))
        pi3 = [0]

        def wl3(i):
            if i < len(plan3):
                scr, nm, r0, c0 = plan3[i]
                load_wpiece(wsl, i % 3, scr, nm, r0, c0, 512)

        def next_piece3():
            i = pi3[0]
            pi3[0] += 1
            wl3(i + 2)
            return i % 3

        wl3(0)
        wl3(1)
        drain(p3_prologue(0))
        for c in range(n_chunks):
            tok0 = 512 * c
            for cg in range(4):
                slot = next_piece3()
                for tt in range(4):
                    bk = mrot.next()
                    xr = xrot.next()
                    r0 = tok0 + 128 * tt
                    S.dma("sp", f"xres{xr}", xres[xr][:, :], xc[HALO + r0:HALO + r0 + 128, 512 * cg:512 * cg + 512],
                          writes=[f"xres{xr}"])
                    for kc in range(16):
                        S.op("pe", MM(PB[bk][:, :], mixT[:, kc, tt * 128:(tt + 1) * 128], wsl[slot][:, kc, :],
                                      kc == 0, kc == 15),
                             reads=[f"w{slot}", f"mixT{tt}_{kc // 4}"], writes=[f"ps{bk}"])
                    S.op("dve", TT(x1[:, tt, 512 * cg:512 * cg + 512], PB[bk][:, :], xres[xr][:, :], ALU.add),
                         reads=[f"ps{bk}", f"xres{xr}"], writes=[f"x1_{tt}"])
            for tt in range(4):
                s = tt % 2
                S.op("act", ACTF(junk[:], x1[:, tt, :], AF.Square, accum=ss3[:, 2:3]),
                     reads=[f"x1_{tt}"], writes=["ss3c"])
                rstd_ops(ss3[:, 2:3], rs3[:, 2:3], D, ["ss3c"], ["rs3c"])
                S.op("dve", STT(h2b[s][:], x1[:, tt, :], rs3[:, 2:3], gml[:], ALU.mult, ALU.mult),
                     reads=[f"x1_{tt}", "rs3c", "gml", "misc3"], writes=[f"h2b{s}"])
                transposes(h2b[s], f"h2b{s}", h2T, lambda g, tt=tt: f"h2T{tt}_{g}", tt * 128, trot, evrot)
            nxt = p3_prologue(c + 1) if c + 1 < n_chunks else None
            for q in range(4):
                for gp in range(4):
                    slot = next_piece3()
                    for j in range(4):
                        bk = mrot.next()
                        f = 4 * gp + j
                        for kc in range(16):
                            S.op("pe", MM(PB[bk][:, :], wsl[slot][:, kc, 128 * j:128 * j + 128], h2T[:, kc, :],
                                          kc == 0, kc == 15),
                                 reads=[f"w{slot}"] + [f"h2T{t}_{kc // 4}" for t in range(4)],
                                 writes=[f"ps{bk}"])
                        rr = rrot.next()
                        S.op("act", ACTF(r32[rr][:, :], PB[bk][:, :], AF.Relu), reads=[f"ps{bk}"],
                             writes=[f"r32_{rr}"])
                        S.op("pool", TT(uT[:, f, :], r32[rr][:, :], r32[rr][:, :], ALU.mult),
                             reads=[f"r32_{rr}"], writes=[f"uT{f}"])
                    if nxt is not None:
                        next(nxt, None)
                for cg in range(4):
                    slot = next_piece3()
                    for tt in range(4):
                        bk = mrot.next()
                        for f in range(16):
                            S.op("pe", MM(PB[bk][:, :], uT[:, f, tt * 128:(tt + 1) * 128], wsl[slot][:, f, :],
                                          f == 0, f == 15),
                                 reads=[f"w{slot}", f"uT{f}"], writes=[f"ps{bk}"])
                        xv = x1[:, tt, 512 * cg:512 * cg + 512]
                        S.op("dve", TT(xv, PB[bk][:, :], xv, ALU.add), reads=[f"ps{bk}", f"x1_{tt}"],
                             writes=[f"x1_{tt}"])
                    if nxt is not None:
                        next(nxt, None)
            drain(nxt)
            for tt in range(4):
                S.op("act", ACTF(junk[:], x1[:, tt, :], AF.Square, accum=ss3[:, 3:4]),
                     reads=[f"x1_{tt}"], writes=["ss3d"])
                rstd_ops(ss3[:, 3:4], rs3[:, 3:4], D, ["ss3d"], ["rs3d"])
                S.op("dve", STT(x1[:, tt, :], x1[:, tt, :], rs3[:, 3:4], gfi[:], ALU.mult, ALU.mult),
                     reads=[f"x1_{tt}", "rs3d", "gfi", "misc3"], writes=[f"x1_{tt}"])
                r0 = tok0 + 128 * tt
                S.dma("pool", f"outst{tt}", out[r0:r0 + 128, :], x1[:, tt, :], reads=[f"x1_{tt}"],
                      writes=[f"outd{tt}"])
        S.barrier()

    finals = [c for c in S.chan if c.startswith(("outst", "ost", "qst", "vst", "cvst"))]
    S.emit(final_wait_chans=finals)
    return nc, S


_CACHE = {}


def _core_inputs(x, hf_first, b, hf):
    xc = np.zeros((NTOK, D), np.float32)
    if hf == 0:
        xc[HALO:] = x[b, 0:NOWN]
    else:
        xc[:] = x[b, NOWN - HALO:2 * NOWN]
    return xc


def kernel(x, g_attn, w_in, b_in, sinks_a, g_out_a, g_out_b, w_out, g_mlp, w_1, w_2, g_final):
    x = np.asarray(x, np.float32)
    if "nc" not in _CACHE:
        _CACHE["nc"] = build()[0]
    nc = _CACHE["nc"]
    shared = {
        "w_in": np.ascontiguousarray(np.asarray(w_in, np.float32)[0]),
        "w_out": np.ascontiguousarray(np.asarray(w_out, np.float32)[0]),
        "w_1": np.ascontiguousarray(np.asarray(w_1, np.float32)[0]),
        "w_2": np.ascontiguousarray(np.asarray(w_2, np.float32)[0]),
        "g_attn": np.ascontiguousarray(np.asarray(g_attn, np.float32)[0]),
        "b_in": np.ascontiguousarray(np.asarray(b_in, np.float32)[0]),
        "sinks": np.ascontiguousarray(np.asarray(sinks_a, np.float32)[0]),
        "g_oab": np.concatenate([np.asarray(g_out_a, np.float32)[0], np.asarray(g_out_b, np.float32)[0]]),
        "g_mlp": np.ascontiguousarray(np.asarray(g_mlp, np.float32)[0]),
        "g_fin": np.ascontiguousarray(np.asarray(g_final, np.float32)),
    }
    in_maps = []
    for core in range(N_CORES):
        b, hf = core // 2, core % 2
        m = dict(shared)
        m["xc"] = _core_inputs(x, None, b, hf)
        m["hb"] = np.full((128, 1), NEG if hf == 0 else 0.0, np.float32)
        in_maps.append(m)
    res = run_bass_kernel_spmd(nc, in_maps, core_ids=list(range(N_CORES)))
    outp = np.empty((4, 2 * NOWN, D), np.float32)
    for core in range(N_CORES):
        b, hf = core // 2, core % 2
        outp[b, hf * NOWN:(hf + 1) * NOWN] = res.results[core]["out"]
    return outp
```

```python
import numpy as np
import concourse.bass as bass
import concourse.mybir as mybir
from concourse.bass_utils import run_bass_kernel_spmd

F32 = mybir.dt.float32
BF16 = mybir.dt.bfloat16
I32 = mybir.dt.int32
AF = mybir.ActivationFunctionType
ALU = mybir.AluOpType

D = 2048
NOWN = 4096
HALO = 2048
NTOK = 6144
DIN = 4352
DFF = 8192
EPS = 1e-5
NEG = -30000.0
SLOPES = [2.0 ** (-8.0 * (h + 1) / 16) for h in range(16)]
N_CORES = 8
P2_STAGE = 9


class Sched:
    ENGS = ("pe", "act", "dve", "pool", "sp")

    def __init__(self, nc):
        self.nc = nc
        self.ops = []
        self.last_w = {}
        self.readers = {}
        self.chan = {}
        self.chan_last = {}
        self.eng_last = {}
        self.pending = {}

    def _add(self, eng, fn, reads, writes, dma_chan=None):
        idx = len(self.ops)
        deps = set()
        for r in reads:
            if r in self.last_w:
                deps.add((self.last_w[r], "raw"))
        for w in writes:
            if w in self.last_w:
                deps.add((self.last_w[w], "waw"))
            for rd in self.readers.get(w, {}).values():
                deps.add((rd, "war"))
        if eng in self.pending:
            for j in self.pending.pop(eng):
                deps.add((j, "raw"))
        op = dict(eng=eng, fn=fn, deps=deps, dma=dma_chan, sig=False)
        if dma_chan is not None:
            self.chan[dma_chan] = self.chan.get(dma_chan, 0) + 16
            op["chan_val"] = self.chan[dma_chan]
            self.chan_last[dma_chan] = idx
        self.eng_last[eng] = idx
        self.ops.append(op)
        rkey = eng if dma_chan is None else ("dma", dma_chan)
        for r in reads:
            self.readers.setdefault(r, {})[rkey] = idx
        for w in writes:
            self.last_w[w] = idx
            self.readers[w] = {}
        return idx

    def op(self, eng, fn, reads=(), writes=()):
        return self._add(eng, fn, tuple(reads), tuple(writes))

    def dma(self, eng, chan, out, in_, reads=(), writes=(), **kw):
        def fn(e, out=out, in_=in_, kw=kw):
            return e.dma_start(out=out, in_=in_, **kw)
        return self._add(eng, fn, tuple(reads), tuple(writes), dma_chan=chan)

    def barrier(self, skip_engs=(), skip_chan_prefix=None):
        deps = [v for k, v in self.eng_last.items() if k not in skip_engs]
        deps += [v for k, v in self.chan_last.items()
                 if not (skip_chan_prefix and k.startswith(skip_chan_prefix))]
        for e in self.ENGS:
            if e in skip_engs:
                continue
            self.pending[e] = list(set(self.pending.get(e, []) + deps))

    def emit(self, final_wait_chans=()):
        from contextlib import ExitStack
        nc = self.nc
        ops = self.ops
        for op in ops:
            waits = []
            for (j, kind) in op["deps"]:
                J = ops[j]
                if J["dma"] is not None:
                    waits.append(("chan", J["dma"], J["chan_val"]))
                elif J["eng"] == op["eng"]:
                    if op["eng"] == "pe":
                        continue
                    J["sig"] = True
                    waits.append(("eng", J["eng"], j))
                else:
                    J["sig"] = True
                    waits.append(("eng", J["eng"], j))
            op["waits"] = waits
        cnt = {e: 0 for e in self.ENGS}
        for op in ops:
            if op["sig"] and op["dma"] is None:
                cnt[op["eng"]] += 1
                op["sigval"] = cnt[op["eng"]]
        per_eng = {e: [] for e in self.ENGS}
        waited = {e: {} for e in self.ENGS}
        for op in ops:
            w2 = {}
            for w in op["waits"]:
                if w[0] == "chan":
                    key, val = ("chan", w[1]), w[2]
                else:
                    key, val = ("eng", w[1]), ops[w[2]]["sigval"]
                if waited[op["eng"]].get(key, 0) >= val:
                    continue
                w2[key] = max(w2.get(key, 0), val)
            for k, v in w2.items():
                waited[op["eng"]][k] = v
            op["w2"] = w2
            per_eng[op["eng"]].append(op)
        self.stats = dict(n_ops=len(ops), n_sem=len(self.chan) + 5,
                          per_eng={e: len(v) for e, v in per_eng.items()}, sig=dict(cnt))
        with ExitStack() as st:
            sems = {}
            for e in self.ENGS:
                sems[("eng", e)] = st.enter_context(nc.semaphore("s_" + e))
            for c in self.chan:
                sems[("chan", c)] = st.enter_context(nc.semaphore("c_" + str(c)))
            block = st.enter_context(nc.Block())

            def run(engobj, lst):
                for op in lst:
                    for k, v in op["w2"].items():
                        engobj.wait_ge(sems[k], v)
                    ins = op["fn"](engobj)
                    if op["dma"] is not None:
                        ins.then_inc(sems[("chan", op["dma"])], 16)
                    elif op["sig"]:
                        ins.then_inc(sems[("eng", op["eng"])], 1)

            @block.tensor
            def _(e):
                run(e, per_eng["pe"])

            @block.scalar
            def _(e):
                run(e, per_eng["act"])

            @block.vector
            def _(e):
                run(e, per_eng["dve"])

            @block.gpsimd
            def _(e):
                run(e, per_eng["pool"])

            @block.sync
            def _(e):
                run(e, per_eng["sp"])
                for c in final_wait_chans:
                    e.wait_ge(sems[("chan", c)], self.chan[c])


class Arena:
    def __init__(self, nc):
        self.nc = nc
        self.lo = ((nc.SBUF_PARTITION_SIZE_BYTES - nc.sbuf_bytes_remaining + 63) // 64) * 64
        self.hi = nc.SBUF_PARTITION_SIZE_BYTES
        self.cur = self.lo
        self.n = 0

    def alloc(self, name, shape, dt):
        nbytes = int(np.prod(shape[1:])) * (4 if dt in (F32, I32) else 2)
        nbytes = ((nbytes + 63) // 64) * 64
        off = self.cur
        assert off + nbytes <= self.hi, f"SBUF overflow at {name}: {off + nbytes} > {self.hi}"
        self.cur += nbytes
        self.n += 1
        return self.nc.alloc_sbuf_tensor_at(f"{name}_{self.n}", list(shape), dt, offset=off)

    def mark(self):
        return self.cur

    def reset(self, m):
        self.cur = m


class Rot:
    def __init__(self, items):
        self.items = list(items)
        self.i = 0

    def next(self):
        v = self.items[self.i % len(self.items)]
        self.i += 1
        return v


def MM(o, l, r, start, stop, skip=False):
    return lambda e: e.matmul(o, lhsT=l, rhs=r, start=start, stop=stop, skip_group_check=skip)


def ACTF(o, i, func, bias=None, scale=1.0, accum=None):
    def f(e):
        kw = {}
        if bias is not None:
            kw["bias"] = bias
        if accum is not None:
            kw["accum_out"] = accum
        return e.activation(out=o, in_=i, func=func, scale=scale, **kw)
    return f


def TT(o, a, b, op):
    return lambda e: e.tensor_tensor(out=o, in0=a, in1=b, op=op)


def STT(o, a, s, b, op0, op1):
    return lambda e: e.scalar_tensor_tensor(out=o, in0=a, scalar=s, in1=b, op0=op0, op1=op1)


def TS(o, a, s1, s2, op0, op1=None):
    if op1 is None:
        return lambda e: e.tensor_scalar(out=o, in0=a, scalar1=s1, scalar2=None, op0=op0)
    return lambda e: e.tensor_scalar(out=o, in0=a, scalar1=s1, scalar2=s2, op0=op0, op1=op1)


def CP(o, i):
    return lambda e: e.tensor_copy(out=o, in_=i)


def ACP(o, i):
    return lambda e: e.copy(out=o, in_=i)


def RCP(o, i):
    return lambda e: e.reciprocal(out=o, in_=i)


def MS(ap, v):
    return lambda e: e.memset(ap, v)


def build(phases=(0, 1, 2, 3), dbg=False, n_chunks=8, p2_items=None):
    nc = bass.Bass("TRN2", target_bir_lowering=False)

    def dram(name, shape, dt, kind):
        return nc.dram_tensor(name, list(shape), dt, kind=kind).ap()

    IN, OUT, INT = "ExternalInput", "ExternalOutput", "Internal"
    SCR = OUT if dbg else INT
    P = set(phases)
    xc = dram("xc", [NTOK, D], F32, IN) if P & {1, 3} else None
    if 0 in P:
        w_in = dram("w_in", [D, DIN], F32, IN)
        w_out = dram("w_out", [D, D], F32, IN)
        w_1 = dram("w_1", [D, DFF], F32, IN)
        w_2 = dram("w_2", [DFF, D], F32, IN)
    if 1 in P:
        g_attn = dram("g_attn", [D], F32, IN)
        b_in = dram("b_in", [DIN], F32, IN)
    if 3 in P:
        sinks = dram("sinks", [16], F32, IN)
        g_oab = dram("g_oab", [2048], F32, IN)
        g_mlp = dram("g_mlp", [D], F32, IN)
        g_fin = dram("g_fin", [D], F32, IN)
    hb = dram("hb", [128, 1], F32, IN)
    out = dram("out", [NOWN, D], F32, OUT)

    win_s = dram("win_s", [D, DIN], BF16, INT)
    wout_s = dram("wout_s", [D, D], BF16, INT)
    w1_s = dram("w1_s", [D, DFF], BF16, INT)
    w2_s = dram("w2_s", [DFF, D], BF16, INT)
    qta = dram("qta", [8, 128, NOWN], BF16, SCR)
    qtb = dram("qtb", [8, 128, NOWN], BF16, SCR)
    ktb = dram("ktb", [8, 128, NTOK], BF16, SCR)
    kta = dram("kta", [2, 128, NTOK], BF16, SCR)
    va = dram("va", [NTOK, 128], BF16, SCR)
    vb = dram("vb", [NTOK, 1024], BF16, SCR)
    opart = dram("opart", [4, NOWN, 8, 130], F32, SCR)

    S = Sched(nc)
    A = Arena(nc)
    PBALL = nc.alloc_psum_tensor("pball", [128, 4096], F32)
    PB = [PBALL[:, 512 * i:512 * i + 512] for i in range(8)]
    PS2 = [PBALL[:, 1024 * i:1024 * i + 1024] for i in range(3)]

    def pbf(i):
        return PB[i].bitcast(BF16)

    ident = A.alloc("ident", [128, 128], BF16)
    zcol = A.alloc("zcol", [128, 1], F32)
    hbt = A.alloc("hbt", [128, 1], F32)
    epsc = A.alloc("epsc", [128, 1], F32)
    S.op("pool", MS(ident[:], 0.0), writes=["ident"])
    S.op("pool", lambda e: e.affine_select(out=ident[:], in_=ident[:], pattern=[[-1, 128]],
                                           compare_op=ALU.not_equal, fill=1.0, base=0,
                                           channel_multiplier=1),
         reads=["ident"], writes=["ident"])
    S.op("pool", MS(zcol[:], 0.0), writes=["zcol"])
    S.op("pool", MS(epsc[:], EPS), writes=["epsc"])
    S.dma("sp", "misc", hbt[:], hb, writes=["hbt", "misc"])
    base_mark = A.mark()

    CVW = 2176
    cv32 = [A.alloc(f"cv32_{i}", [128, CVW], F32) for i in range(2)]
    cvbf = [A.alloc(f"cvbf_{i}", [128, CVW], BF16) for i in range(2)]
    pieces = []
    if 0 in P:
        for (nm, src, dst, R, C, pw) in (("win", w_in, win_s, D, DIN, 2176), ("wout", w_out, wout_s, D, D, 2048),
                                         ("w1", w_1, w1_s, D, DFF, 2048), ("w2", w_2, w2_s, DFF, D, 2048)):
            for rc in range(R // 128):
                for c0 in range(0, C, pw):
                    pieces.append((nm, src[rc * 128:(rc + 1) * 128, c0:c0 + pw],
                                   dst[rc * 128:(rc + 1) * 128, c0:c0 + pw], pw))
    n_win = sum(1 for p_ in pieces if p_[0] == "win")
    def CV(nm):
        return [f"cvst_{s}_{nm}" for s in range(2)]

    stp_i = A.alloc("stp_i", [128, 256], I32)
    stp = A.alloc("stp", [128, 256], F32)
    vbA = A.alloc("vbA", [128, 256], F32)
    vbB = A.alloc("vbB", [128, 256], F32)
    S.op("pool", lambda e: e.iota(stp_i[:], pattern=[[1, 256]], base=0, channel_multiplier=-1),
         writes=["stp_i"])
    S.op("pool", CP(stp[:], stp_i[:]), reads=["stp_i"], writes=["stp"])
    for (t, ms, nm) in ((vbA, 127, "vbA"), (vbB, 128, "vbB")):
        S.op("pool", MS(t[:], 0.0), writes=[nm])
        S.op("pool", (lambda t: (lambda e: e.affine_select(
            out=t[:], in_=t[:], pattern=[[1, 256]], compare_op=ALU.is_ge, fill=NEG, base=0,
            channel_multiplier=-1)))(t), reads=[nm], writes=[nm])
        S.op("pool", (lambda t, ms: (lambda e: e.affine_select(
            out=t[:], in_=t[:], pattern=[[-1, 256]], compare_op=ALU.is_ge, fill=NEG, base=ms,
            channel_multiplier=1)))(t, ms), reads=[nm], writes=[nm])
    p12_mark = A.mark()

    for k in range(n_win):
        nm, src, dst, w = pieces[k]
        s_ = k % 2
        S.dma("sp", f"cv32_{s_}", cv32[s_][:, :w], src, writes=[f"cv32_{s_}"])
        eng = ("pool", "dve", "act")[k % 3]
        if eng == "act":
            S.op("act", ACP(cvbf[s_][:, :w], cv32[s_][:, :w]), reads=[f"cv32_{s_}"], writes=[f"cvbf_{s_}"])
        else:
            S.op(eng, CP(cvbf[s_][:, :w], cv32[s_][:, :w]), reads=[f"cv32_{s_}"], writes=[f"cvbf_{s_}"])
        S.dma("pool", f"cvst_{s_}", dst, cvbf[s_][:, :w], reads=[f"cvbf_{s_}"], writes=[f"cvst_{s_}_{nm}"])
    for k in range(n_win, len(pieces)):
        nm, src, dst, w = pieces[k]
        s_ = k % 2
        S.dma("pool", f"cv32_{s_}", cv32[s_][:, :w], src, writes=[f"cv32_{s_}"])
        if k - 1 >= n_win:
            nm1, src1, dst1, w1 = pieces[k - 1]
            s1 = (k - 1) % 2
            S.op("pool", CP(cvbf[s1][:, :w1], cv32[s1][:, :w1]), reads=[f"cv32_{s1}"], writes=[f"cvbf_{s1}"])
            S.dma("pool", f"cvst_{s1}", dst1, cvbf[s1][:, :w1], reads=[f"cvbf_{s1}"], writes=[f"cvst_{s1}_{nm1}"])
    if len(pieces) > n_win:
        k = len(pieces) - 1
        nm1, src1, dst1, w1 = pieces[k]
        s1 = k % 2
        S.op("pool", CP(cvbf[s1][:, :w1], cv32[s1][:, :w1]), reads=[f"cv32_{s1}"], writes=[f"cvbf_{s1}"])
        S.dma("pool", f"cvst_{s1}", dst1, cvbf[s1][:, :w1], reads=[f"cvbf_{s1}"], writes=[f"cvst_{s1}_{nm1}"])

    def pump(n=1):
        pass

    def conv_drain():
        pass

    def rstd_ops(ssap, rsap, n, reads, writes):
        S.op("act", ACTF(rsap, ssap, AF.Sqrt, bias=epsc[:, 0:1], scale=1.0 / n), reads=list(reads) + ["epsc"],
             writes=writes)
        S.op("dve", RCP(rsap, rsap), reads=writes, writes=writes)

    def transposes(src_bf, src_res, dstT, dst_res_fn, tcol, trot, evrot):
        for g in range(4):
            bk = trot.next()
            for j in range(4):
                kc = 4 * g + j
                o = pbf(bk)[:, j * 128:(j + 1) * 128]
                i_ = src_bf[:, kc * 128:(kc + 1) * 128]
                S.op("pe", (lambda o, i_: (lambda e: e.transpose(out=o, in_=i_, identity=ident[:])))(o, i_),
                     reads=[src_res, "ident"], writes=[f"ps{bk}"])
            ev = evrot.next()
            o = dstT[:, 4 * g:4 * g + 4, tcol:tcol + 128]
            i_ = pbf(bk)[:, 0:512].rearrange("p (a b) -> p a b", a=4)
            if ev == "dve":
                S.op("dve", CP(o, i_), reads=[f"ps{bk}"], writes=[dst_res_fn(g)])
            else:
                S.op("act", ACP(o, i_), reads=[f"ps{bk}"], writes=[dst_res_fn(g)])

    def load_wpiece(wsl, slot, scr, wname, r0, c0, ncols, dcol0=0):
        S.dma("sp", f"w{slot}", wsl[slot][:, :, dcol0:dcol0 + ncols],
              scr[r0:r0 + 2048, c0:c0 + ncols].rearrange("(k p) c -> p k c", p=128),
              reads=CV(wname), writes=[f"w{slot}"])

    def drain(gen):
        if gen is not None:
            for _ in gen:
                pass

    if 1 in phases:
        hT = [A.alloc(f"hT{i}", [128, 16, 1024], BF16) for i in range(2)]
        xs = [A.alloc(f"xs{i}", [128, 2048], F32) for i in range(3)]
        hbf = [A.alloc(f"hbf{i}", [128, 2048], BF16) for i in range(3)]
        gat = A.alloc("gat", [128, 2048], F32)
        ss = A.alloc("ss", [128, 3], F32)
        rs = A.alloc("rs", [128, 3], F32)
        binT = A.alloc("binT", [128, 34], F32)
        binT8 = A.alloc("binT8", [128, 34], F32)
        bka = A.alloc("bka", [128, 2], F32)
        bv = A.alloc("bv", [128, 1152], F32)
        wsl = [A.alloc(f"wsl{i}", [128, 16, 512], BF16) for i in range(3)]
        qst = [A.alloc(f"qst{i}", [128, 1024], BF16) for i in range(3)]
        vst = [A.alloc(f"vst{i}", [128, 512], BF16) for i in range(4)]

        S.dma("sp", "misc", gat[:], g_attn.partition_broadcast(128), writes=["gat", "misc"])
        for c0 in range(0, 34, 6):
            c1 = min(34, c0 + 6)
            S.dma("sp", "misc", binT[:, c0:c1], b_in.rearrange("(c p) -> p c", p=128)[:, c0:c1],
                  writes=["binT", "misc"], allow_slow_non_contiguous=True)
        for g in range(2):
            for hh in range(2):
                S.dma("sp", "misc", bka[64 * hh:64 * hh + 64, g:g + 1],
                      b_in[1024 + 64 * g:1024 + 64 * g + 64].rearrange("(p o) -> p o", o=1),
                      writes=["bka", "misc"])
        S.dma("sp", "misc", bv[:, 0:1024], b_in[3328:4352].partition_broadcast(128), writes=["bv", "misc"])
        S.dma("sp", "misc", bv[:, 1024:1152], b_in[1152:1280].partition_broadcast(128), writes=["bv", "misc"])
        S.op("dve", TS(binT8[:], binT[:], 0.125, None, ALU.mult), reads=["binT", "misc"], writes=["binT8"])
        trot = Rot([0, 1])
        mrot = Rot([2, 3, 4, 5, 6, 7])
        evrot = Rot(["dve", "act"])
        wrot = Rot([0, 1, 2])
        qrot = Rot([0, 1, 2])
        vrot = Rot([0, 1, 2, 3])
        NCH1 = 6

        def p1_prologue(c):
            T0 = 1024 * c
            hb_ = c % 2

            def L(tt):
                s = tt % 3
                S.dma("sp", f"xs{s}", xs[s][:], xc[T0 + tt * 128:T0 + (tt + 1) * 128, :], writes=[f"xs{s}"])

            def N(tt):
                s = tt % 3
                S.op("act", ACTF(hbf[s][:], xs[s][:], AF.Square, accum=ss[:, s:s + 1]),
                     reads=[f"xs{s}"], writes=[f"ss{s}", f"hbf{s}"])
                rstd_ops(ss[:, s:s + 1], rs[:, s:s + 1], D, [f"ss{s}"], [f"rs{s}"])
                S.op("dve", STT(hbf[s][:], xs[s][:], rs[:, s:s + 1], gat[:], ALU.mult, ALU.mult),
                     reads=[f"xs{s}", f"rs{s}", "gat", "misc"], writes=[f"hbf{s}"])

            def T(tt):
                s = tt % 3
                transposes(hbf[s], f"hbf{s}", hT[hb_], lambda g, tt=tt: f"hT{hb_}_{tt}_{g}", tt * 128, trot, evrot)

            for step in range(8 + 4):
                if step < 8:
                    L(step)
                    yield
                if 0 <= step - 2 < 8:
                    N(step - 2)
                    yield
                if 0 <= step - 4 < 8:
                    T(step - 4)
                    yield

        def fm_group(c, slot, wc0, bias_col, scale, dstap):
            hb_ = c % 2
            q = qrot.next()
            for tq in range(2):
                bk = mrot.next()
                for kc in range(16):
                    S.op("pe", MM(PB[bk][:, :], wsl[slot][:, kc, wc0:wc0 + 128],
                                  hT[hb_][:, kc, tq * 512:(tq + 1) * 512], kc == 0, kc == 15),
                         reads=[f"w{slot}"] + [f"hT{hb_}_{4 * tq + t}_{kc // 4}" for t in range(4)],
                         writes=[f"ps{bk}"])
                S.op("act", ACTF(qst[q][:, tq * 512:(tq + 1) * 512], PB[bk][:, :], AF.Identity,
                                 bias=bias_col, scale=scale),
                     reads=[f"ps{bk}", "binT8", "misc"], writes=[f"qst{q}"])
            S.dma("sp", f"qst{q}", dstap, qst[q][:], reads=[f"qst{q}"], writes=[f"qstd{q}"])

        def tm_piece(c, slot, wc0, n, dstap_fn, bcol0, tick):
            hb_ = c % 2
            for tt in range(8):
                bk = mrot.next()
                v = vrot.next()
                for kc in range(16):
                    S.op("pe", MM(PB[bk][:, 0:n], hT[hb_][:, kc, tt * 128:(tt + 1) * 128],
                                  wsl[slot][:, kc, wc0:wc0 + n], kc == 0, kc == 15),
                         reads=[f"w{slot}", f"hT{hb_}_{tt}_{kc // 4}"], writes=[f"ps{bk}"])
                S.op("dve", TT(vst[v][:, 0:n], PB[bk][:, 0:n], bv[:, bcol0:bcol0 + n], ALU.add),
                     reads=[f"ps{bk}", "bv", "misc"], writes=[f"vst{v}"])
                S.dma("sp", f"vst{v}", dstap_fn(tt), vst[v][:, 0:n], reads=[f"vst{v}"], writes=[f"vstd{v}"])
                if tt % 2 == 1:
                    tick()

        plan = []

        def mk_simple(c0):
            return lambda slot: load_wpiece(wsl, slot, win_s, "win", 0, c0, 512)

        def mk_kava():
            def f(slot):
                for g in range(2):
                    for hh in range(2):
                        load_wpiece(wsl, slot, win_s, "win", 0, 1024 + 64 * g, 64, dcol0=128 * g + 64 * hh)
                load_wpiece(wsl, slot, win_s, "win", 0, 1152, 128, dcol0=256)
            return f

        for c in range(NCH1):
            plan += [mk_simple(2304), mk_simple(2816), mk_simple(3328), mk_simple(3840), mk_kava()]
            if c >= 2:
                plan += [mk_simple(0), mk_simple(512), mk_simple(1280), mk_simple(1792)]
        pi_ = [0]

        def wl(i):
            if i < len(plan):
                plan[i](i % 3)

        def next_piece():
            i = pi_[0]
            pi_[0] += 1
            wl(i + 2)
            return i % 3

        wl(0)
        wl(1)
        drain(p1_prologue(0))
        for c in range(NCH1):
            T0 = 1024 * c
            own = c >= 2
            nxt = p1_prologue(c + 1) if c + 1 < NCH1 else None
            tk = [0]

            def tick():
                tk[0] += 1
                if nxt is not None:
                    next(nxt, None)

            for p in range(2):
                slot = next_piece()
                for j in range(4):
                    fc = 4 * p + j
                    fm_group(c, slot, 128 * j, binT[:, 18 + fc:18 + fc + 1], 1.0, ktb[fc, :, T0:T0 + 1024])
                    tick()
            for p in range(2):
                slot = next_piece()
                tm_piece(c, slot, 0, 512,
                         lambda tt, T0=T0, p=p: vb[T0 + tt * 128:T0 + (tt + 1) * 128, 512 * p:512 * p + 512],
                         512 * p, tick)
            slot = next_piece()
            for g in range(2):
                fm_group(c, slot, 128 * g, bka[:, g:g + 1], 1.0, kta[g, :, T0:T0 + 1024])
                tick()
            tm_piece(c, slot, 256, 128, lambda tt, T0=T0: va[T0 + tt * 128:T0 + (tt + 1) * 128, :], 1024, tick)
            if own:
                for (c0, dst, b0) in ((0, qta, 0), (1280, qtb, 10)):
                    for p in range(2):
                        slot = next_piece()
                        for j in range(4):
                            fc = 4 * p + j
                            fm_group(c, slot, 128 * j, binT8[:, b0 + fc:b0 + fc + 1], 0.125,
                                     dst[fc, :, T0 - HALO:T0 - HALO + 1024])
                            tick()
            drain(nxt)
        conv_drain()
        S.barrier()
    else:
        conv_drain()
        S.barrier()
    A.reset(p12_mark)

    if 2 in phases:
        qT = [[A.alloc(f"qT{i}_{h}", [128, NOWN], BF16) for h in range(2)] for i in range(2)]
        kT = [A.alloc(f"kT{i}", [128, NTOK], BF16) for i in range(2)]
        vS = [A.alloc(f"vS{i}", [128, 48, 130], BF16) for i in range(2)]
        bias2 = [A.alloc(f"bias2_{i}", [128, 512], F32) for i in range(2)]
        s32 = [A.alloc(f"s32_{i}", [128, 512], F32) for i in range(3)]
        pT = [A.alloc(f"pT{i}", [128, 512], BF16) for i in range(5)]
        ost = [A.alloc(f"ost{i}", [128, 130], F32) for i in range(8)]

        for i in range(2):
            S.op("dve", MS(vS[i][:, :, 0:1], 1.0), writes=[f"vS{i}"])
            S.op("dve", MS(vS[i][:, :, 129:130], 1.0), writes=[f"vS{i}"])
            S.op("dve", MS(qT[i][0][64:128, :], 0.0), writes=[f"qT{i}"])
            S.op("dve", MS(qT[i][1][0:64, :], 0.0), writes=[f"qT{i}"])

        passes = [("A", 1, 0), ("B", 1, 1), ("B", 4, 2), ("B", 16, 3)]
        items = [(pi, hp) for pi in range(4) for hp in range(8)]
        if p2_items is not None:
            items = p2_items
        srot = Rot([0, 1, 2])
        s3rot = Rot([0, 1, 2])
        prot = Rot([0, 1, 2, 3, 4])
        orot = Rot(list(range(8)))
        obank = Rot([6, 7])
        oev = Rot(["dve", "act"])

        def p2_loads(it, pi, hp):
            kind, d, pidx = passes[pi]
            b = it % 2
            isA = kind == "A"
            nt = 48 // d
            qsrc = (qta if isA else qtb)[hp]
            for h in range(2):
                S.dma("sp", f"qT{b}", qT[b][h][64 * h:64 * h + 64, :], qsrc[64 * h:64 * h + 64, :], writes=[f"qT{b}"])
            S.dma("sp", f"kT{b}", kT[b][:], kta[hp // 4] if isA else ktb[hp], writes=[f"kT{b}"])
            for r in range(d):
                for j0 in range(0, nt, 8):
                    j1 = min(nt, j0 + 8)
                    if isA:
                        g = hp // 4
                        src = va[:, 64 * g:64 * g + 64].rearrange("(jt m) c -> m jt c", m=128)[:, j0:j1, :]
                        S.dma("sp", f"vS{b}", vS[b][:, j0:j1, 1:65], src, writes=[f"vS{b}"])
                    else:
                        src = vb[r::d, 128 * hp:128 * hp + 128].rearrange("(jt m) c -> m jt c", m=128)[:, j0:j1, :]
                        S.dma("sp", f"vS{b}", vS[b][:, r * nt + j0:r * nt + j1, 1:129], src, writes=[f"vS{b}"])

        def p2_compute(it, pi, hp):
            kind, d, pidx = passes[pi]
            b = it % 2
            isA = kind == "A"
            nt = 48 // d
            jh = 16 // d
            vbt = vbA if isA else vbB
            for h in range(2):
                S.op("dve", STT(bias2[b][:, 256 * h:256 * h + 256], stp[:], -SLOPES[2 * hp + h] * d, vbt[:],
                                ALU.mult, ALU.add), reads=["stp", "vbA", "vbB"], writes=[f"bias2_{b}"])

            def score(r, jt):
                n0 = 128 if jt == jh - 1 else 0
                n1 = 128 if jt == nt - 1 else 256
                sb = srot.next()
                ks = r + 128 * d * jt
                qs = r + d * (128 * jt + n0) - HALO
                nq = n1 - n0
                for h in range(2):
                    S.op("pe", MM(PS2[sb][:, 512 * h + n0:512 * h + n0 + nq],
                                  kT[b][:, ks:ks + 127 * d + 1:d],
                                  qT[b][h][:, qs:qs + (nq - 1) * d + 1:d], True, True),
                         reads=[f"kT{b}", f"qT{b}"], writes=[f"ps2_{sb}"])
                s3 = s3rot.next()
                p = prot.next()
                pv = PS2[sb].rearrange("p (h n) -> p h n", h=2)[:, :, n0:n1]
                bvw = bias2[b][:, :].rearrange("p (h n) -> p h n", h=2)[:, :, n0:n1]
                sv = s32[s3][:, :].rearrange("p (h n) -> p h n", h=2)[:, :, n0:n1]
                ptv = pT[p][:, :].rearrange("p (h n) -> p h n", h=2)[:, :, n0:n1]
                S.op("dve", TT(sv, pv, bvw, ALU.add), reads=[f"ps2_{sb}", f"bias2_{b}"], writes=[f"s32_{s3}"])
                col = hbt if jt < jh else zcol
                S.op("act", ACTF(ptv, sv, AF.Exp, bias=col[:, 0:1], scale=1.0),
                     reads=[f"s32_{s3}", "hbt", "zcol"], writes=[f"pT{p}"])
                return p

            def pv_q(r, jq, p_prev, p_cur):
                ob = obank.next()
                for h in range(2):
                    rc0 = 0 if (isA or h == 0) else 65
                    S.op("pe", MM(PB[ob][:, 65 * h:65 * h + 65],
                                  pT[p_prev][:, 256 * h + 128:256 * h + 256],
                                  vS[b][:, r * nt + jq - 1, rc0:rc0 + 65], True, False, skip=True),
                         reads=[f"pT{p_prev}", f"vS{b}"], writes=[f"ps{ob}"])
                    S.op("pe", MM(PB[ob][:, 65 * h:65 * h + 65],
                                  pT[p_cur][:, 256 * h:256 * h + 128],
                                  vS[b][:, r * nt + jq, rc0:rc0 + 65], False, True, skip=True),
                         reads=[f"pT{p_cur}", f"vS{b}"], writes=[f"ps{ob}"])
                o = orot.next()
                if oev.next() == "dve":
                    S.op("dve", CP(ost[o][:, :], PB[ob][:, 0:130]), reads=[f"ps{ob}"], writes=[f"ost{o}"])
                else:
                    S.op("act", ACP(ost[o][:, :], PB[ob][:, 0:130]), reads=[f"ps{ob}"], writes=[f"ost{o}"])
                t0 = r + 128 * d * jq - HALO
                S.dma("pool", f"ost{o}", opart[pidx, t0:t0 + 127 * d + 1:d, hp, :], ost[o][:, :],
                      reads=[f"ost{o}"], writes=[f"ostd{o}"])

            sc_list, pv_list = [], []
            for r in range(d):
                for jt in range(jh - 1, nt):
                    sc_list.append((r, jt))
                    if jt >= jh:
                        pv_list.append((r, jt, len(sc_list) - 2, len(sc_list) - 1))
            LA = 2
            slot_of = {}
            si = 0
            for (r, jq, ip, ic) in pv_list:
                while si <= min(ic + LA, len(sc_list) - 1):
                    slot_of[si] = score(*sc_list[si])
                    si += 1
                pv_q(r, jq, slot_of[ip], slot_of[ic])

        if items:
            p2_loads(0, *items[0])
        for it, (pi, hp) in enumerate(items):
            if it + 1 < len(items):
                p2_loads(it + 1, *items[it + 1])
            p2_compute(it, pi, hp)
        S.barrier()
    A.reset(base_mark)

    if 3 in phases:
        x1 = A.alloc("x1", [128, 4, 2048], F32)
        opA = A.alloc("opA", [128, 8 * 130], F32)
        opB = A.alloc("opB", [128, 3, 8 * 130], F32)
        junk = A.alloc("junk3", [128, 2048], BF16)
        mixb = [A.alloc(f"mixb{i}", [128, 2048], BF16) for i in range(2)]
        h2b = [A.alloc(f"h2b{i}", [128, 2048], BF16) for i in range(2)]
        mixT = A.alloc("mixT", [128, 16, 512], BF16)
        h2T = A.alloc("h2T", [128, 16, 512], BF16)
        uT = A.alloc("uT", [128, 16, 512], BF16)
        r32 = [A.alloc(f"r32_{i}", [128, 512], F32) for i in range(2)]
        xres = [A.alloc(f"xres{i}", [128, 512], F32) for i in range(3)]
        wsl = [A.alloc(f"wsl3_{i}", [128, 16, 512], BF16) for i in range(3)]
        goab = A.alloc("goab", [128, 2048], F32)
        gml = A.alloc("gml", [128, 2048], F32)
        gfi = A.alloc("gfi", [128, 2048], F32)
        esink = A.alloc("esink", [128, 16], F32)
        dA = A.alloc("dA", [128, 16], F32)
        dB = A.alloc("dB", [128, 16], F32)
        ss3 = A.alloc("ss3", [128, 4], F32)
        rs3 = A.alloc("rs3", [128, 4], F32)

        S.dma("sp", "misc3", goab[:], g_oab.partition_broadcast(128), writes=["goab", "misc3"])
        S.dma("sp", "misc3", gml[:], g_mlp.partition_broadcast(128), writes=["gml", "misc3"])
        S.dma("sp", "misc3", gfi[:], g_fin.partition_broadcast(128), writes=["gfi", "misc3"])
        S.dma("sp", "misc3", esink[:], sinks.partition_broadcast(128), writes=["esink", "misc3"])
        S.op("act", ACTF(esink[:], esink[:], AF.Exp), reads=["esink", "misc3"], writes=["esink"])
        trot = Rot([0, 1])
        mrot = Rot([2, 3, 4, 5, 6, 7])
        evrot = Rot(["dve", "act"])
        wrot = Rot([0, 1, 2])
        rrot = Rot([0, 1])
        xrot = Rot([0, 1, 2])
        a3 = opA[:, :].rearrange("p (h c) -> p h c", h=16)
        b3 = opB[:, 0, :].rearrange("p (a c) -> p a c", a=8)
        dB3 = dB[:, :].rearrange("p (a t) -> p a t", a=8)
        b3o = b3[:, :, 1:129].rearrange("p a (t c) -> p a t c", t=2)

        def p3_prologue(c):
            tok0 = 512 * c
            for tt in range(4):
                r0 = tok0 + 128 * tt
                s = tt % 2
                S.dma("sp", "opA", opA[:, :], opart[0, r0:r0 + 128].rearrange("t h c -> t (h c)"), writes=["opA"])
                S.dma("sp", "opB", opB[:, :, :], opart[1:4, r0:r0 + 128].rearrange("p t h c -> t p (h c)"),
                      writes=["opB"])
                yield
                S.op("dve", TT(dA[:, :], a3[:, :, 0], esink[:, :], ALU.add), reads=["opA", "esink"], writes=["dA"])
                S.op("dve", RCP(dA[:, :], dA[:, :]), reads=["dA"], writes=["dA"])
                S.op("dve", TT(a3[:, :, 1:65], a3[:, :, 1:65],
                               dA[:, :].unsqueeze(2).to_broadcast([128, 16, 64]), ALU.mult),
                     reads=["opA", "dA"], writes=["opA"])
                S.op("act", ACTF(junk[:, 0:1024].rearrange("p (h c) -> p h c", h=16), a3[:, :, 1:65],
                                 AF.Square, accum=ss3[:, 0:1]), reads=["opA"], writes=["ss3a"])
                S.op("dve", TT(opB[:, 0, :], opB[:, 0, :], opB[:, 1, :], ALU.add), reads=["opB"], writes=["opB"])
                S.op("dve", TT(opB[:, 0, :], opB[:, 0, :], opB[:, 2, :], ALU.add), reads=["opB"], writes=["opB"])
                S.op("dve", RCP(dB3, b3[:, :, 0::129]), reads=["opB"], writes=["dB"])
                S.op("dve", TT(b3o, b3o, dB3.unsqueeze(3).to_broadcast([128, 8, 2, 64]), ALU.mult),
                     reads=["opB", "dB"], writes=["opB"])
                S.op("act", ACTF(junk[:, 1024:2048].rearrange("p (a c) -> p a c", a=8), b3[:, :, 1:129],
                                 AF.Square, accum=ss3[:, 1:2]), reads=["opB"], writes=["ss3b"])
                yield
                rstd_ops(ss3[:, 0:2], rs3[:, 0:2], 1024, ["ss3a", "ss3b"], ["rs3ab"])
                S.op("dve", STT(mixb[s][:, 0:1024].rearrange("p (h c) -> p h c", h=16), a3[:, :, 1:65],
                                rs3[:, 0:1], goab[:, 0:1024].rearrange("p (h c) -> p h c", h=16),
                                ALU.mult, ALU.mult), reads=["opA", "rs3ab", "goab", "misc3"], writes=[f"mixb{s}"])
                S.op("dve", STT(mixb[s][:, 1024:2048].rearrange("p (a c) -> p a c", a=8), b3[:, :, 1:129],
                                rs3[:, 1:2], goab[:, 1024:2048].rearrange("p (a c) -> p a c", a=8),
                                ALU.mult, ALU.mult), reads=["opB", "rs3ab", "goab", "misc3"], writes=[f"mixb{s}"])
                yield
                transposes(mixb[s], f"mixb{s}", mixT, lambda g, tt=tt: f"mixT{tt}_{g}", tt * 128, trot, evrot)
                yield

        plan3 = []
        for c in range(n_chunks):
            for cg in range(4):
                plan3.append((wout_s, "wout", 0, 512 * cg))
            for q in range(4):
                for gp in range(4):
                    plan3.append((w1_s, "w1", 0, 2048 * q + 512 * gp))
                for cg in range(4):
                    plan3.append((w2_s, "w2", 2048 * q, 512 * cg))
        pi3 = [0]

        def wl3(i):
            if i < len(plan3):
                scr, nm, r0, c0 = plan3[i]
                load_wpiece(wsl, i % 3, scr, nm, r0, c0, 512)

        def next_piece3():
            i = pi3[0]
            pi3[0] += 1
            wl3(i + 2)
            return i % 3

        wl3(0)
        wl3(1)
        drain(p3_prologue(0))
        for c in range(n_chunks):
            tok0 = 512 * c
            for cg in range(4):
                slot = next_piece3()
                for tt in range(4):
                    bk = mrot.next()
                    xr = xrot.next()
                    r0 = tok0 + 128 * tt
                    S.dma("sp", f"xres{xr}", xres[xr][:, :], xc[HALO + r0:HALO + r0 + 128, 512 * cg:512 * cg + 512],
                          writes=[f"xres{xr}"])
                    for kc in range(16):
                        S.op("pe", MM(PB[bk][:, :], mixT[:, kc, tt * 128:(tt + 1) * 128], wsl[slot][:, kc, :],
                                      kc == 0, kc == 15),
                             reads=[f"w{slot}", f"mixT{tt}_{kc // 4}"], writes=[f"ps{bk}"])
                    S.op("dve", TT(x1[:, tt, 512 * cg:512 * cg + 512], PB[bk][:, :], xres[xr][:, :], ALU.add),
                         reads=[f"ps{bk}", f"xres{xr}"], writes=[f"x1_{tt}"])
            for tt in range(4):
                s = tt % 2
                S.op("act", ACTF(junk[:], x1[:, tt, :], AF.Square, accum=ss3[:, 2:3]),
                     reads=[f"x1_{tt}"], writes=["ss3c"])
                rstd_ops(ss3[:, 2:3], rs3[:, 2:3], D, ["ss3c"], ["rs3c"])
                S.op("dve", STT(h2b[s][:], x1[:, tt, :], rs3[:, 2:3], gml[:], ALU.mult, ALU.mult),
                     reads=[f"x1_{tt}", "rs3c", "gml", "misc3"], writes=[f"h2b{s}"])
                transposes(h2b[s], f"h2b{s}", h2T, lambda g, tt=tt: f"h2T{tt}_{g}", tt * 128, trot, evrot)
            nxt = p3_prologue(c + 1) if c + 1 < n_chunks else None
            for q in range(4):
                for gp in range(4):
                    slot = next_piece3()
                    for j in range(4):
                        bk = mrot.next()
                        f = 4 * gp + j
                        for kc in range(16):
                            S.op("pe", MM(PB[bk][:, :], wsl[slot][:, kc, 128 * j:128 * j + 128], h2T[:, kc, :],
                                          kc == 0, kc == 15),
                                 reads=[f"w{slot}"] + [f"h2T{t}_{kc // 4}" for t in range(4)],
                                 writes=[f"ps{bk}"])
                        rr = rrot.next()
                        S.op("act", ACTF(r32[rr][:, :], PB[bk][:, :], AF.Relu), reads=[f"ps{bk}"],
                             writes=[f"r32_{rr}"])
                        S.op("pool", TT(uT[:, f, :], r32[rr][:, :], r32[rr][:, :], ALU.mult),
                             reads=[f"r32_{rr}"], writes=[f"uT{f}"])
                    if nxt is not None:
                        next(nxt, None)
                for cg in range(4):
                    slot = next_piece3()
                    for tt in range(4):
                        bk = mrot.next()
                        for f in range(16):
                            S.op("pe", MM(PB[bk][:, :], uT[:, f, tt * 128:(tt + 1) * 128], wsl[slot][:, f, :],
                                          f == 0, f == 15),
                                 reads=[f"w{slot}", f"uT{f}"], writes=[f"ps{bk}"])
                        xv = x1[:, tt, 512 * cg:512 * cg + 512]
                        S.op("dve", TT(xv, PB[bk][:, :], xv, ALU.add), reads=[f"ps{bk}", f"x1_{tt}"],
                             writes=[f"x1_{tt}"])
                    if nxt is not None:
                        next(nxt, None)
            drain(nxt)
            for tt in range(4):
                S.op("act", ACTF(junk[:], x1[:, tt, :], AF.Square, accum=ss3[:, 3:4]),
                     reads=[f"x1_{tt}"], writes=["ss3d"])
                rstd_ops(ss3[:, 3:4], rs3[:, 3:4], D, ["ss3d"], ["rs3d"])
                S.op("dve", STT(x1[:, tt, :], x1[:, tt, :], rs3[:, 3:4], gfi[:], ALU.mult, ALU.mult),
                     reads=[f"x1_{tt}", "rs3d", "gfi", "misc3"], writes=[f"x1_{tt}"])
                r0 = tok0 + 128 * tt
                S.dma("pool", f"outst{tt}", out[r0:r0 + 128, :], x1[:, tt, :], reads=[f"x1_{tt}"],
                      writes=[f"outd{tt}"])
        S.barrier()

    finals = [c for c in S.chan if c.startswith(("outst", "ost", "qst", "vst", "cvst"))]
    S.emit(final_wait_chans=finals)
    return nc, S


_CACHE = {}


def _core_inputs(x, hf_first, b, hf):
    xc = np.zeros((NTOK, D), np.float32)
    if hf == 0:
        xc[HALO:] = x[b, 0:NOWN]
    else:
        xc[:] = x[b, NOWN - HALO:2 * NOWN]
    return xc


def kernel(x, g_attn, w_in, b_in, sinks_a, g_out_a, g_out_b, w_out, g_mlp, w_1, w_2, g_final):
    x = np.asarray(x, np.float32)
    if "nc" not in _CACHE:
        _CACHE["nc"] = build()[0]
    nc = _CACHE["nc"]
    shared = {
        "w_in": np.ascontiguousarray(np.asarray(w_in, np.float32)[0]),
        "w_out": np.ascontiguousarray(np.asarray(w_out, np.float32)[0]),
        "w_1": np.ascontiguousarray(np.asarray(w_1, np.float32)[0]),
        "w_2": np.ascontiguousarray(np.asarray(w_2, np.float32)[0]),
        "g_attn": np.ascontiguousarray(np.asarray(g_attn, np.float32)[0]),
        "b_in": np.ascontiguousarray(np.asarray(b_in, np.float32)[0]),
        "sinks": np.ascontiguousarray(np.asarray(sinks_a, np.float32)[0]),
        "g_oab": np.concatenate([np.asarray(g_out_a, np.float32)[0], np.asarray(g_out_b, np.float32)[0]]),
        "g_mlp": np.ascontiguousarray(np.asarray(g_mlp, np.float32)[0]),
        "g_fin": np.ascontiguousarray(np.asarray(g_final, np.float32)),
    }
    in_maps = []
    for core in range(N_CORES):
        b, hf = core // 2, core % 2
        m = dict(shared)
        m["xc"] = _core_inputs(x, None, b, hf)
        m["hb"] = np.full((128, 1), NEG if hf == 0 else 0.0, np.float32)
        in_maps.append(m)
    res = run_bass_kernel_spmd(nc, in_maps, core_ids=list(range(N_CORES)))
    outp = np.empty((4, 2 * NOWN, D), np.float32)
    for core in range(N_CORES):
        b, hf = core // 2, core % 2
        outp[b, hf * NOWN:(hf + 1) * NOWN] = res.results[core]["out"]
    return outp
```

```python
import numpy as np
import concourse.bass as bass
import concourse.mybir as mybir
from concourse.bass_utils import run_bass_kernel_spmd

F32 = mybir.dt.float32
BF16 = mybir.dt.bfloat16
I32 = mybir.dt.int32
AF = mybir.ActivationFunctionType
ALU = mybir.AluOpType

D = 2048
NOWN = 4096
HALO = 2048
NTOK = 6144
DIN = 4352
DFF = 8192
EPS = 1e-5
NEG = -30000.0
SLOPES = [2.0 ** (-8.0 * (h + 1) / 16) for h in range(16)]
N_CORES = 8
P2_STAGE = 9


class Sched:
    ENGS = ("pe", "act", "dve", "pool", "sp")

    def __init__(self, nc):
        self.nc = nc
        self.ops = []
        self.last_w = {}
        self.readers = {}
        self.chan = {}
        self.chan_last = {}
        self.eng_last = {}
        self.pending = {}

    def _add(self, eng, fn, reads, writes, dma_chan=None):
        idx = len(self.ops)
        deps = set()
        for r in reads:
            if r in self.last_w:
                deps.add((self.last_w[r], "raw"))
        for w in writes:
            if w in self.last_w:
                deps.add((self.last_w[w], "waw"))
            for rd in self.readers.get(w, {}).values():
                deps.add((rd, "war"))
        if eng in self.pending:
            for j in self.pending.pop(eng):
                deps.add((j, "raw"))
        op = dict(eng=eng, fn=fn, deps=deps, dma=dma_chan, sig=False)
        if dma_chan is not None:
            self.chan[dma_chan] = self.chan.get(dma_chan, 0) + 16
            op["chan_val"] = self.chan[dma_chan]
            self.chan_last[dma_chan] = idx
        self.eng_last[eng] = idx
        self.ops.append(op)
        rkey = eng if dma_chan is None else ("dma", dma_chan)
        for r in reads:
            self.readers.setdefault(r, {})[rkey] = idx
        for w in writes:
            self.last_w[w] = idx
            self.readers[w] = {}
        return idx

    def op(self, eng, fn, reads=(), writes=()):
        return self._add(eng, fn, tuple(reads), tuple(writes))

    def dma(self, eng, chan, out, in_, reads=(), writes=(), **kw):
        def fn(e, out=out, in_=in_, kw=kw):
            return e.dma_start(out=out, in_=in_, **kw)
        return self._add(eng, fn, tuple(reads), tuple(writes), dma_chan=chan)

    def barrier(self, skip_engs=(), skip_chan_prefix=None):
        deps = [v for k, v in self.eng_last.items() if k not in skip_engs]
        deps += [v for k, v in self.chan_last.items()
                 if not (skip_chan_prefix and k.startswith(skip_chan_prefix))]
        for e in self.ENGS:
            if e in skip_engs:
                continue
            self.pending[e] = list(set(self.pending.get(e, []) + deps))

    def emit(self, final_wait_chans=()):
        from contextlib import ExitStack
        nc = self.nc
        ops = self.ops
        for op in ops:
            waits = []
            for (j, kind) in op["deps"]:
                J = ops[j]
                if J["dma"] is not None:
                    waits.append(("chan", J["dma"], J["chan_val"]))
                elif J["eng"] == op["eng"]:
                    if op["eng"] == "pe":
                        continue
                    J["sig"] = True
                    waits.append(("eng", J["eng"], j))
                else:
                    J["sig"] = True
                    waits.append(("eng", J["eng"], j))
            op["waits"] = waits
        cnt = {e: 0 for e in self.ENGS}
        for op in ops:
            if op["sig"] and op["dma"] is None:
                cnt[op["eng"]] += 1
                op["sigval"] = cnt[op["eng"]]
        per_eng = {e: [] for e in self.ENGS}
        waited = {e: {} for e in self.ENGS}
        for op in ops:
            w2 = {}
            for w in op["waits"]:
                if w[0] == "chan":
                    key, val = ("chan", w[1]), w[2]
                else:
                    key, val = ("eng", w[1]), ops[w[2]]["sigval"]
                if waited[op["eng"]].get(key, 0) >= val:
                    continue
                w2[key] = max(w2.get(key, 0), val)
            for k, v in w2.items():
                waited[op["eng"]][k] = v
            op["w2"] = w2
            per_eng[op["eng"]].append(op)
        self.stats = dict(n_ops=len(ops), n_sem=len(self.chan) + 5,
                          per_eng={e: len(v) for e, v in per_eng.items()}, sig=dict(cnt))
        with ExitStack() as st:
            sems = {}
            for e in self.ENGS:
                sems[("eng", e)] = st.enter_context(nc.semaphore("s_" + e))
            for c in self.chan:
                sems[("chan", c)] = st.enter_context(nc.semaphore("c_" + str(c)))
            block = st.enter_context(nc.Block())

            def run(engobj, lst):
                for op in lst:
                    for k, v in op["w2"].items():
                        engobj.wait_ge(sems[k], v)
                    ins = op["fn"](engobj)
                    if op["dma"] is not None:
                        ins.then_inc(sems[("chan", op["dma"])], 16)
                    elif op["sig"]:
                        ins.then_inc(sems[("eng", op["eng"])], 1)

            @block.tensor
            def _(e):
                run(e, per_eng["pe"])

            @block.scalar
            def _(e):
                run(e, per_eng["act"])

            @block.vector
            def _(e):
                run(e, per_eng["dve"])

            @block.gpsimd
            def _(e):
                run(e, per_eng["pool"])

            @block.sync
            def _(e):
                run(e, per_eng["sp"])
                for c in final_wait_chans:
                    e.wait_ge(sems[("chan", c)], self.chan[c])


class Arena:
    def __init__(self, nc):
        self.nc = nc
        self.lo = ((nc.SBUF_PARTITION_SIZE_BYTES - nc.sbuf_bytes_remaining + 63) // 64) * 64
        self.hi = nc.SBUF_PARTITION_SIZE_BYTES
        self.cur = self.lo
        self.n = 0

    def alloc(self, name, shape, dt):
        nbytes = int(np.prod(shape[1:])) * (4 if dt in (F32, I32) else 2)
        nbytes = ((nbytes + 63) // 64) * 64
        off = self.cur
        assert off + nbytes <= self.hi, f"SBUF overflow at {name}: {off + nbytes} > {self.hi}"
        self.cur += nbytes
        self.n += 1
        return self.nc.alloc_sbuf_tensor_at(f"{name}_{self.n}", list(shape), dt, offset=off)

    def mark(self):
        return self.cur

    def reset(self, m):
        self.cur = m


class Rot:
    def __init__(self, items):
        self.items = list(items)
        self.i = 0

    def next(self):
        v = self.items[self.i % len(self.items)]
        self.i += 1
        return v


def MM(o, l, r, start, stop, skip=False):
    return lambda e: e.matmul(o, lhsT=l, rhs=r, start=start, stop=stop, skip_group_check=skip)


def ACTF(o, i, func, bias=None, scale=1.0, accum=None):
    def f(e):
        kw = {}
        if bias is not None:
            kw["bias"] = bias
        if accum is not None:
            kw["accum_out"] = accum
        return e.activation(out=o, in_=i, func=func, scale=scale, **kw)
    return f


def TT(o, a, b, op):
    return lambda e: e.tensor_tensor(out=o, in0=a, in1=b, op=op)


def STT(o, a, s, b, op0, op1):
    return lambda e: e.scalar_tensor_tensor(out=o, in0=a, scalar=s, in1=b, op0=op0, op1=op1)


def TS(o, a, s1, s2, op0, op1=None):
    if op1 is None:
        return lambda e: e.tensor_scalar(out=o, in0=a, scalar1=s1, scalar2=None, op0=op0)
    return lambda e: e.tensor_scalar(out=o, in0=a, scalar1=s1, scalar2=s2, op0=op0, op1=op1)


def CP(o, i):
    return lambda e: e.tensor_copy(out=o, in_=i)


def ACP(o, i):
    return lambda e: e.copy(out=o, in_=i)


def RCP(o, i):
    return lambda e: e.reciprocal(out=o, in_=i)


def MS(ap, v):
    return lambda e: e.memset(ap, v)


def build(phases=(0, 1, 2, 3), dbg=False, n_chunks=8, p2_items=None):
    nc = bass.Bass("TRN2", target_bir_lowering=False)

    def dram(name, shape, dt, kind):
        return nc.dram_tensor(name, list(shape), dt, kind=kind).ap()

    IN, OUT, INT = "ExternalInput", "ExternalOutput", "Internal"
    SCR = OUT if dbg else INT
    P = set(phases)
    xc = dram("xc", [NTOK, D], F32, IN) if P & {1, 3} else None
    if 0 in P:
        w_in = dram("w_in", [D, DIN], F32, IN)
        w_out = dram("w_out", [D, D], F32, IN)
        w_1 = dram("w_1", [D, DFF], F32, IN)
        w_2 = dram("w_2", [DFF, D], F32, IN)
    if 1 in P:
        g_attn = dram("g_attn", [D], F32, IN)
        b_in = dram("b_in", [DIN], F32, IN)
    if 3 in P:
        sinks = dram("sinks", [16], F32, IN)
        g_oab = dram("g_oab", [2048], F32, IN)
        g_mlp = dram("g_mlp", [D], F32, IN)
        g_fin = dram("g_fin", [D], F32, IN)
    hb = dram("hb", [128, 1], F32, IN)
    out = dram("out", [NOWN, D], F32, OUT)

    wout_s = dram("wout_s", [D, D], BF16, INT)
    w1_s = dram("w1_s", [D, DFF], BF16, INT)
    w2_s = dram("w2_s", [DFF, D], BF16, INT)
    qta = dram("qta", [8, 128, NOWN], BF16, SCR)
    qtb = dram("qtb", [8, 128, NOWN], BF16, SCR)
    ktb = dram("ktb", [8, 128, NTOK], BF16, SCR)
    kta = dram("kta", [2, 128, NTOK], BF16, SCR)
    va = dram("va", [NTOK, 128], BF16, SCR)
    vb = dram("vb", [NTOK, 1024], BF16, SCR)
    opart = dram("opart", [4, NOWN, 8, 130], F32, SCR)

    S = Sched(nc)
    A = Arena(nc)
    PBALL = nc.alloc_psum_tensor("pball", [128, 4096], F32)
    PB = [PBALL[:, 512 * i:512 * i + 512] for i in range(8)]
    PS2 = [PBALL[:, 1024 * i:1024 * i + 1024] for i in range(3)]

    def pbf(i):
        return PB[i].bitcast(BF16)

    ident = A.alloc("ident", [128, 128], BF16)
    zcol = A.alloc("zcol", [128, 1], F32)
    hbt = A.alloc("hbt", [128, 1], F32)
    epsc = A.alloc("epsc", [128, 1], F32)
    S.op("pool", MS(ident[:], 0.0), writes=["ident"])
    S.op("pool", lambda e: e.affine_select(out=ident[:], in_=ident[:], pattern=[[-1, 128]],
                                           compare_op=ALU.not_equal, fill=1.0, base=0,
                                           channel_multiplier=1),
         reads=["ident"], writes=["ident"])
    S.op("pool", MS(zcol[:], 0.0), writes=["zcol"])
    S.op("pool", MS(epsc[:], EPS), writes=["epsc"])
    S.dma("sp", "misc", hbt[:], hb, writes=["hbt", "misc"])
    base_mark = A.mark()

    stp_i = A.alloc("stp_i", [128, 256], I32)
    stp = A.alloc("stp", [128, 256], F32)
    vbA = A.alloc("vbA", [128, 256], F32)
    vbB = A.alloc("vbB", [128, 256], F32)
    S.op("pool", lambda e: e.iota(stp_i[:], pattern=[[1, 256]], base=0, channel_multiplier=-1),
         writes=["stp_i"])
    S.op("pool", CP(stp[:], stp_i[:]), reads=["stp_i"], writes=["stp"])
    for (t, ms, nm) in ((vbA, 127, "vbA"), (vbB, 128, "vbB")):
        S.op("pool", MS(t[:], 0.0), writes=[nm])
        S.op("pool", (lambda t: (lambda e: e.affine_select(
            out=t[:], in_=t[:], pattern=[[1, 256]], compare_op=ALU.is_ge, fill=NEG, base=0,
            channel_multiplier=-1)))(t), reads=[nm], writes=[nm])
        S.op("pool", (lambda t, ms: (lambda e: e.affine_select(
            out=t[:], in_=t[:], pattern=[[-1, 256]], compare_op=ALU.is_ge, fill=NEG, base=ms,
            channel_multiplier=1)))(t, ms), reads=[nm], writes=[nm])
    p12_mark = A.mark()

    cvd = []
    if 0 in P:
        for i in range(4):
            cvd.append(("wout", wout_s[512 * i:512 * i + 512, :], w_out[512 * i:512 * i + 512, :]))
        for q in range(4):
            for i in range(4):
                cvd.append((f"w1q{q}", w1_s[512 * i:512 * i + 512, 2048 * q:2048 * q + 2048],
                            w_1[512 * i:512 * i + 512, 2048 * q:2048 * q + 2048]))
            for i in range(4):
                r0 = 2048 * q + 512 * i
                cvd.append((f"w2q{q}", w2_s[r0:r0 + 512, :], w_2[r0:r0 + 512, :]))
    cvs = dict(idx=0)

    def CV(group):
        return [f"cv_{group}_{c}" for c in range(4)]

    def pump_conv(n=1):
        for _ in range(n):
            k = cvs["idx"]
            if k >= len(cvd):
                return
            grp, dst, src = cvd[k]
            S.dma("pool", f"cvd{k % 4}", dst, src, writes=[f"cv_{grp}_{k % 4}"])
            cvs["idx"] += 1

    def conv_drain():
        while cvs["idx"] < len(cvd):
            pump_conv()

    def rstd_ops(ssap, rsap, n, reads, writes):
        S.op("act", ACTF(rsap, ssap, AF.Sqrt, bias=epsc[:, 0:1], scale=1.0 / n), reads=list(reads) + ["epsc"],
             writes=writes)
        S.op("dve", RCP(rsap, rsap), reads=writes, writes=writes)

    def transposes(src_bf, src_res, dstT, dst_res_fn, tcol, trot, evrot):
        for g in range(4):
            bk = trot.next()
            for j in range(4):
                kc = 4 * g + j
                o = pbf(bk)[:, j * 128:(j + 1) * 128]
                i_ = src_bf[:, kc * 128:(kc + 1) * 128]
                S.op("pe", (lambda o, i_: (lambda e: e.transpose(out=o, in_=i_, identity=ident[:])))(o, i_),
                     reads=[src_res, "ident"], writes=[f"ps{bk}"])
            ev = evrot.next()
            o = dstT[:, 4 * g:4 * g + 4, tcol:tcol + 128]
            i_ = pbf(bk)[:, 0:512].rearrange("p (a b) -> p a b", a=4)
            if ev == "dve":
                S.op("dve", CP(o, i_), reads=[f"ps{bk}"], writes=[dst_res_fn(g)])
            else:
                S.op("act", ACP(o, i_), reads=[f"ps{bk}"], writes=[dst_res_fn(g)])

    def load_wpiece(wsl, slot, scr, wname, r0, c0, ncols, dcol0=0):
        if wname == "win":
            eng, deps = "pool", []
        else:
            eng, deps = "sp", CV(wname)
        S.dma(eng, f"w{slot}", wsl[slot][:, :, dcol0:dcol0 + ncols],
              scr[r0:r0 + 2048, c0:c0 + ncols].rearrange("(k p) c -> p k c", p=128),
              reads=deps, writes=[f"w{slot}"])

    def drain(gen):
        if gen is not None:
            for _ in gen:
                pass

    if 1 in phases:
        hT = [A.alloc(f"hT{i}", [128, 16, 1024], BF16) for i in range(2)]
        xs = [A.alloc(f"xs{i}", [128, 2048], F32) for i in range(3)]
        hbf = [A.alloc(f"hbf{i}", [128, 2048], BF16) for i in range(3)]
        gat = A.alloc("gat", [128, 2048], F32)
        ss = A.alloc("ss", [128, 3], F32)
        rs = A.alloc("rs", [128, 3], F32)
        binT = A.alloc("binT", [128, 34], F32)
        binT8 = A.alloc("binT8", [128, 34], F32)
        bka = A.alloc("bka", [128, 2], F32)
        bv = A.alloc("bv", [128, 1152], F32)
        wsl = [A.alloc(f"wsl{i}", [128, 16, 512], BF16) for i in range(3)]
        qst = [A.alloc(f"qst{i}", [128, 1024], BF16) for i in range(3)]
        vst = [A.alloc(f"vst{i}", [128, 512], BF16) for i in range(4)]

        S.dma("sp", "misc", gat[:], g_attn.partition_broadcast(128), writes=["gat", "misc"])
        for c0 in range(0, 34, 6):
            c1 = min(34, c0 + 6)
            S.dma("sp", "misc", binT[:, c0:c1], b_in.rearrange("(c p) -> p c", p=128)[:, c0:c1],
                  writes=["binT", "misc"], allow_slow_non_contiguous=True)
        for g in range(2):
            for hh in range(2):
                S.dma("sp", "misc", bka[64 * hh:64 * hh + 64, g:g + 1],
                      b_in[1024 + 64 * g:1024 + 64 * g + 64].rearrange("(p o) -> p o", o=1),
                      writes=["bka", "misc"])
        S.dma("sp", "misc", bv[:, 0:1024], b_in[3328:4352].partition_broadcast(128), writes=["bv", "misc"])
        S.dma("sp", "misc", bv[:, 1024:1152], b_in[1152:1280].partition_broadcast(128), writes=["bv", "misc"])
        S.op("dve", TS(binT8[:], binT[:], 0.125, None, ALU.mult), reads=["binT", "misc"], writes=["binT8"])
        trot = Rot([0, 1])
        mrot = Rot([2, 3, 4, 5, 6, 7])
        evrot = Rot(["dve", "act"])
        wrot = Rot([0, 1, 2])
        qrot = Rot([0, 1, 2])
        vrot = Rot([0, 1, 2, 3])
        NCH1 = 6

        def p1_prologue(c):
            T0 = 1024 * c
            hb_ = c % 2

            def L(tt):
                s = tt % 3
                S.dma("sp", f"xs{s}", xs[s][:], xc[T0 + tt * 128:T0 + (tt + 1) * 128, :], writes=[f"xs{s}"])

            def N(tt):
                s = tt % 3
                S.op("act", ACTF(hbf[s][:], xs[s][:], AF.Square, accum=ss[:, s:s + 1]),
                     reads=[f"xs{s}"], writes=[f"ss{s}", f"hbf{s}"])
                rstd_ops(ss[:, s:s + 1], rs[:, s:s + 1], D, [f"ss{s}"], [f"rs{s}"])
                S.op("dve", STT(hbf[s][:], xs[s][:], rs[:, s:s + 1], gat[:], ALU.mult, ALU.mult),
                     reads=[f"xs{s}", f"rs{s}", "gat", "misc"], writes=[f"hbf{s}"])

            def T(tt):
                s = tt % 3
                transposes(hbf[s], f"hbf{s}", hT[hb_], lambda g, tt=tt: f"hT{hb_}_{tt}_{g}", tt * 128, trot, evrot)

            for step in range(8 + 4):
                if step < 8:
                    L(step)
                    yield
                if 0 <= step - 2 < 8:
                    N(step - 2)
                    yield
                if 0 <= step - 4 < 8:
                    T(step - 4)
                    yield

        def fm_group(c, slot, wc0, bias_col, scale, dstap):
            hb_ = c % 2
            q = qrot.next()
            for tq in range(2):
                bk = mrot.next()
                for kc in range(16):
                    S.op("pe", MM(PB[bk][:, :], wsl[slot][:, kc, wc0:wc0 + 128],
                                  hT[hb_][:, kc, tq * 512:(tq + 1) * 512], kc == 0, kc == 15),
                         reads=[f"w{slot}"] + [f"hT{hb_}_{4 * tq + t}_{kc // 4}" for t in range(4)],
                         writes=[f"ps{bk}"])
                S.op("act", ACTF(qst[q][:, tq * 512:(tq + 1) * 512], PB[bk][:, :], AF.Identity,
                                 bias=bias_col, scale=scale),
                     reads=[f"ps{bk}", "binT8", "misc"], writes=[f"qst{q}"])
            S.dma("sp", f"qst{q}", dstap, qst[q][:], reads=[f"qst{q}"], writes=[f"qstd{q}"])

        def tm_piece(c, slot, wc0, n, dstap_fn, bcol0, tick):
            hb_ = c % 2
            for tt in range(8):
                bk = mrot.next()
                v = vrot.next()
                for kc in range(16):
                    S.op("pe", MM(PB[bk][:, 0:n], hT[hb_][:, kc, tt * 128:(tt + 1) * 128],
                                  wsl[slot][:, kc, wc0:wc0 + n], kc == 0, kc == 15),
                         reads=[f"w{slot}", f"hT{hb_}_{tt}_{kc // 4}"], writes=[f"ps{bk}"])
                S.op("dve", TT(vst[v][:, 0:n], PB[bk][:, 0:n], bv[:, bcol0:bcol0 + n], ALU.add),
                     reads=[f"ps{bk}", "bv", "misc"], writes=[f"vst{v}"])
                S.dma("sp", f"vst{v}", dstap_fn(tt), vst[v][:, 0:n], reads=[f"vst{v}"], writes=[f"vstd{v}"])
                if tt % 2 == 1:
                    tick()

        plan = []

        def mk_simple(c0):
            return lambda slot: load_wpiece(wsl, slot, w_in, "win", 0, c0, 512)

        def mk_kava():
            def f(slot):
                for g in range(2):
                    for hh in range(2):
                        load_wpiece(wsl, slot, w_in, "win", 0, 1024 + 64 * g, 64, dcol0=128 * g + 64 * hh)
                load_wpiece(wsl, slot, w_in, "win", 0, 1152, 128, dcol0=256)
            return f

        for c in range(NCH1):
            plan += [mk_simple(2304), mk_simple(2816), mk_simple(3328), mk_simple(3840), mk_kava()]
            if c >= 2:
                plan += [mk_simple(0), mk_simple(512), mk_simple(1280), mk_simple(1792)]
        pi_ = [0]

        def wl(i):
            if i < len(plan):
                plan[i](i % 3)

        def next_piece():
            i = pi_[0]
            pi_[0] += 1
            wl(i + 2)
            return i % 3

        gt = [0]
        wl(0)
        wl(1)
        drain(p1_prologue(0))
        for c in range(NCH1):
            T0 = 1024 * c
            own = c >= 2
            nxt = p1_prologue(c + 1) if c + 1 < NCH1 else None
            tk = [0]

            def tick():
                tk[0] += 1
                gt[0] += 1
                if gt[0] % 12 == 0:
                    pump_conv(1)
                if nxt is not None:
                    next(nxt, None)

            for p in range(2):
                slot = next_piece()
                for j in range(4):
                    fc = 4 * p + j
                    fm_group(c, slot, 128 * j, binT[:, 18 + fc:18 + fc + 1], 1.0, ktb[fc, :, T0:T0 + 1024])
                    tick()
            for p in range(2):
                slot = next_piece()
                tm_piece(c, slot, 0, 512,
                         lambda tt, T0=T0, p=p: vb[T0 + tt * 128:T0 + (tt + 1) * 128, 512 * p:512 * p + 512],
                         512 * p, tick)
            slot = next_piece()
            for g in range(2):
                fm_group(c, slot, 128 * g, bka[:, g:g + 1], 1.0, kta[g, :, T0:T0 + 1024])
                tick()
            tm_piece(c, slot, 256, 128, lambda tt, T0=T0: va[T0 + tt * 128:T0 + (tt + 1) * 128, :], 1024, tick)
            if own:
                for (c0, dst, b0) in ((0, qta, 0), (1280, qtb, 10)):
                    for p in range(2):
                        slot = next_piece()
                        for j in range(4):
                            fc = 4 * p + j
                            fm_group(c, slot, 128 * j, binT8[:, b0 + fc:b0 + fc + 1], 0.125,
                                     dst[fc, :, T0 - HALO:T0 - HALO + 1024])
                            tick()
            drain(nxt)
        S.barrier(skip_chan_prefix="cvd")
    A.reset(p12_mark)

    if 2 in phases:
        qT = [[A.alloc(f"qT{i}_{h}", [128, NOWN], BF16) for h in range(2)] for i in range(2)]
        kT = [A.alloc(f"kT{i}", [128, NTOK], BF16) for i in range(2)]
        vS = [A.alloc(f"vS{i}", [128, 48, 130], BF16) for i in range(2)]
        bias2 = [A.alloc(f"bias2_{i}", [128, 512], F32) for i in range(2)]
        s32 = [A.alloc(f"s32_{i}", [128, 512], F32) for i in range(3)]
        pT = [A.alloc(f"pT{i}", [128, 512], BF16) for i in range(5)]
        ost = [A.alloc(f"ost{i}", [128, 130], F32) for i in range(8)]

        for i in range(2):
            S.op("dve", MS(vS[i][:, :, 0:1], 1.0), writes=[f"vS{i}"])
            S.op("dve", MS(vS[i][:, :, 129:130], 1.0), writes=[f"vS{i}"])
            S.op("dve", MS(qT[i][0][64:128, :], 0.0), writes=[f"qT{i}"])
            S.op("dve", MS(qT[i][1][0:64, :], 0.0), writes=[f"qT{i}"])

        passes = [("A", 1, 0), ("B", 1, 1), ("B", 4, 2), ("B", 16, 3)]
        items = [(pi, hp) for pi in range(4) for hp in range(8)]
        if p2_items is not None:
            items = p2_items
        srot = Rot([0, 1, 2])
        s3rot = Rot([0, 1, 2])
        prot = Rot([0, 1, 2, 3, 4])
        orot = Rot(list(range(8)))
        obank = Rot([6, 7])
        oev = Rot(["dve", "act"])

        def p2_loads(it, pi, hp):
            kind, d, pidx = passes[pi]
            b = it % 2
            isA = kind == "A"
            nt = 48 // d
            qsrc = (qta if isA else qtb)[hp]
            for h in range(2):
                S.dma("sp", f"qT{b}", qT[b][h][64 * h:64 * h + 64, :], qsrc[64 * h:64 * h + 64, :], writes=[f"qT{b}"])
            S.dma("sp", f"kT{b}", kT[b][:], kta[hp // 4] if isA else ktb[hp], writes=[f"kT{b}"])
            for r in range(d):
                for j0 in range(0, nt, 8):
                    j1 = min(nt, j0 + 8)
                    if isA:
                        g = hp // 4
                        src = va[:, 64 * g:64 * g + 64].rearrange("(jt m) c -> m jt c", m=128)[:, j0:j1, :]
                        S.dma("sp", f"vS{b}", vS[b][:, j0:j1, 1:65], src, writes=[f"vS{b}"])
                    else:
                        src = vb[r::d, 128 * hp:128 * hp + 128].rearrange("(jt m) c -> m jt c", m=128)[:, j0:j1, :]
                        S.dma("sp", f"vS{b}", vS[b][:, r * nt + j0:r * nt + j1, 1:129], src, writes=[f"vS{b}"])

        def p2_compute(it, pi, hp):
            kind, d, pidx = passes[pi]
            b = it % 2
            isA = kind == "A"
            nt = 48 // d
            jh = 16 // d
            vbt = vbA if isA else vbB
            for h in range(2):
                S.op("dve", STT(bias2[b][:, 256 * h:256 * h + 256], stp[:], -SLOPES[2 * hp + h] * d, vbt[:],
                                ALU.mult, ALU.add), reads=["stp", "vbA", "vbB"], writes=[f"bias2_{b}"])

            def score(r, jt):
                n0 = 128 if jt == jh - 1 else 0
                n1 = 128 if jt == nt - 1 else 256
                sb = srot.next()
                ks = r + 128 * d * jt
                qs = r + d * (128 * jt + n0) - HALO
                nq = n1 - n0
                for h in range(2):
                    S.op("pe", MM(PS2[sb][:, 512 * h + n0:512 * h + n0 + nq],
                                  kT[b][:, ks:ks + 127 * d + 1:d],
                                  qT[b][h][:, qs:qs + (nq - 1) * d + 1:d], True, True),
                         reads=[f"kT{b}", f"qT{b}"], writes=[f"ps2_{sb}"])
                s3 = s3rot.next()
                p = prot.next()
                pv = PS2[sb].rearrange("p (h n) -> p h n", h=2)[:, :, n0:n1]
                bvw = bias2[b][:, :].rearrange("p (h n) -> p h n", h=2)[:, :, n0:n1]
                sv = s32[s3][:, :].rearrange("p (h n) -> p h n", h=2)[:, :, n0:n1]
                ptv = pT[p][:, :].rearrange("p (h n) -> p h n", h=2)[:, :, n0:n1]
                S.op("dve", TT(sv, pv, bvw, ALU.add), reads=[f"ps2_{sb}", f"bias2_{b}"], writes=[f"s32_{s3}"])
                col = hbt if jt < jh else zcol
                S.op("act", ACTF(ptv, sv, AF.Exp, bias=col[:, 0:1], scale=1.0),
                     reads=[f"s32_{s3}", "hbt", "zcol"], writes=[f"pT{p}"])
                return p

            def pv_q(r, jq, p_prev, p_cur):
                ob = obank.next()
                for h in range(2):
                    rc0 = 0 if (isA or h == 0) else 65
                    S.op("pe", MM(PB[ob][:, 65 * h:65 * h + 65],
                                  pT[p_prev][:, 256 * h + 128:256 * h + 256],
                                  vS[b][:, r * nt + jq - 1, rc0:rc0 + 65], True, False, skip=True),
                         reads=[f"pT{p_prev}", f"vS{b}"], writes=[f"ps{ob}"])
                    S.op("pe", MM(PB[ob][:, 65 * h:65 * h + 65],
                                  pT[p_cur][:, 256 * h:256 * h + 128],
                                  vS[b][:, r * nt + jq, rc0:rc0 + 65], False, True, skip=True),
                         reads=[f"pT{p_cur}", f"vS{b}"], writes=[f"ps{ob}"])
                o = orot.next()
                if oev.next() == "dve":
                    S.op("dve", CP(ost[o][:, :], PB[ob][:, 0:130]), reads=[f"ps{ob}"], writes=[f"ost{o}"])
                else:
                    S.op("act", ACP(ost[o][:, :], PB[ob][:, 0:130]), reads=[f"ps{ob}"], writes=[f"ost{o}"])
                t0 = r + 128 * d * jq - HALO
                S.dma("pool", f"ost{o}", opart[pidx, t0:t0 + 127 * d + 1:d, hp, :], ost[o][:, :],
                      reads=[f"ost{o}"], writes=[f"ostd{o}"])

            sc_list, pv_list = [], []
            for r in range(d):
                for jt in range(jh - 1, nt):
                    sc_list.append((r, jt))
                    if jt >= jh:
                        pv_list.append((r, jt, len(sc_list) - 2, len(sc_list) - 1))
            LA = 2
            slot_of = {}
            si = 0
            for (r, jq, ip, ic) in pv_list:
                while si <= min(ic + LA, len(sc_list) - 1):
                    slot_of[si] = score(*sc_list[si])
                    si += 1
                pv_q(r, jq, slot_of[ip], slot_of[ic])

        if items:
            p2_loads(0, *items[0])
        for it, (pi, hp) in enumerate(items):
            if it + 1 < len(items):
                p2_loads(it + 1, *items[it + 1])
            pump_conv(2)
            p2_compute(it, pi, hp)
        conv_drain()
        S.barrier(skip_chan_prefix="cvd")
    else:
        conv_drain()
    A.reset(base_mark)

    if 3 in phases:
        x1 = A.alloc("x1", [128, 4, 2048], F32)
        opA = A.alloc("opA", [128, 8 * 130], F32)
        opB = A.alloc("opB", [128, 3, 8 * 130], F32)
        junk = A.alloc("junk3", [128, 2048], BF16)
        mixb = [A.alloc(f"mixb{i}", [128, 2048], BF16) for i in range(2)]
        h2b = [A.alloc(f"h2b{i}", [128, 2048], BF16) for i in range(2)]
        mixT = A.alloc("mixT", [128, 16, 512], BF16)
        h2T = A.alloc("h2T", [128, 16, 512], BF16)
        uT = A.alloc("uT", [128, 16, 512], BF16)
        r32 = [A.alloc(f"r32_{i}", [128, 512], F32) for i in range(2)]
        xres = [A.alloc(f"xres{i}", [128, 512], F32) for i in range(3)]
        wsl = [A.alloc(f"wsl3_{i}", [128, 16, 512], BF16) for i in range(3)]
        goab = A.alloc("goab", [128, 2048], F32)
        gml = A.alloc("gml", [128, 2048], F32)
        gfi = A.alloc("gfi", [128, 2048], F32)
        esink = A.alloc("esink", [128, 16], F32)
        dA = A.alloc("dA", [128, 16], F32)
        dB = A.alloc("dB", [128, 16], F32)
        ss3 = A.alloc("ss3", [128, 4], F32)
        rs3 = A.alloc("rs3", [128, 4], F32)

        S.dma("sp", "misc3", goab[:], g_oab.partition_broadcast(128), writes=["goab", "misc3"])
        S.dma("sp", "misc3", gml[:], g_mlp.partition_broadcast(128), writes=["gml", "misc3"])
        S.dma("sp", "misc3", gfi[:], g_fin.partition_broadcast(128), writes=["gfi", "misc3"])
        S.dma("sp", "misc3", esink[:], sinks.partition_broadcast(128), writes=["esink", "misc3"])
        S.op("act", ACTF(esink[:], esink[:], AF.Exp), reads=["esink", "misc3"], writes=["esink"])
        trot = Rot([0, 1])
        mrot = Rot([2, 3, 4, 5, 6, 7])
        evrot = Rot(["dve", "act"])
        wrot = Rot([0, 1, 2])
        rrot = Rot([0, 1])
        xrot = Rot([0, 1, 2])
        a3 = opA[:, :].rearrange("p (h c) -> p h c", h=16)
        b3 = opB[:, 0, :].rearrange("p (a c) -> p a c", a=8)
        dB3 = dB[:, :].rearrange("p (a t) -> p a t", a=8)
        b3o = b3[:, :, 1:129].rearrange("p a (t c) -> p a t c", t=2)

        def p3_prologue(c):
            tok0 = 512 * c
            for tt in range(4):
                r0 = tok0 + 128 * tt
                s = tt % 2
                S.dma("sp", "opA", opA[:, :], opart[0, r0:r0 + 128].rearrange("t h c -> t (h c)"), writes=["opA"])
                S.dma("sp", "opB", opB[:, :, :], opart[1:4, r0:r0 + 128].rearrange("p t h c -> t p (h c)"),
                      writes=["opB"])
                yield
                S.op("dve", TT(dA[:, :], a3[:, :, 0], esink[:, :], ALU.add), reads=["opA", "esink"], writes=["dA"])
                S.op("dve", RCP(dA[:, :], dA[:, :]), reads=["dA"], writes=["dA"])
                S.op("dve", TT(a3[:, :, 1:65], a3[:, :, 1:65],
                               dA[:, :].unsqueeze(2).to_broadcast([128, 16, 64]), ALU.mult),
                     reads=["opA", "dA"], writes=["opA"])
                S.op("act", ACTF(junk[:, 0:1024].rearrange("p (h c) -> p h c", h=16), a3[:, :, 1:65],
                                 AF.Square, accum=ss3[:, 0:1]), reads=["opA"], writes=["ss3a"])
                S.op("dve", TT(opB[:, 0, :], opB[:, 0, :], opB[:, 1, :], ALU.add), reads=["opB"], writes=["opB"])
                S.op("dve", TT(opB[:, 0, :], opB[:, 0, :], opB[:, 2, :], ALU.add), reads=["opB"], writes=["opB"])
                S.op("dve", RCP(dB3, b3[:, :, 0::129]), reads=["opB"], writes=["dB"])
                S.op("dve", TT(b3o, b3o, dB3.unsqueeze(3).to_broadcast([128, 8, 2, 64]), ALU.mult),
                     reads=["opB", "dB"], writes=["opB"])
                S.op("act", ACTF(junk[:, 1024:2048].rearrange("p (a c) -> p a c", a=8), b3[:, :, 1:129],
                                 AF.Square, accum=ss3[:, 1:2]), reads=["opB"], writes=["ss3b"])
                yield
                rstd_ops(ss3[:, 0:2], rs3[:, 0:2], 1024, ["ss3a", "ss3b"], ["rs3ab"])
                S.op("dve", STT(mixb[s][:, 0:1024].rearrange("p (h c) -> p h c", h=16), a3[:, :, 1:65],
                                rs3[:, 0:1], goab[:, 0:1024].rearrange("p (h c) -> p h c", h=16),
                                ALU.mult, ALU.mult), reads=["opA", "rs3ab", "goab", "misc3"], writes=[f"mixb{s}"])
                S.op("dve", STT(mixb[s][:, 1024:2048].rearrange("p (a c) -> p a c", a=8), b3[:, :, 1:129],
                                rs3[:, 1:2], goab[:, 1024:2048].rearrange("p (a c) -> p a c", a=8),
                                ALU.mult, ALU.mult), reads=["opB", "rs3ab", "goab", "misc3"], writes=[f"mixb{s}"])
                yield
                transposes(mixb[s], f"mixb{s}", mixT, lambda g, tt=tt: f"mixT{tt}_{g}", tt * 128, trot, evrot)
                yield

        plan3 = []
        for c in range(n_chunks):
            for cg in range(4):
                plan3.append((wout_s, "wout", 0, 512 * cg))
            for q in range(4):
                for gp in range(4):
                    plan3.append((w1_s, f"w1q{q}", 0, 2048 * q + 512 * gp))
                for cg in range(4):
                    plan3.append((w2_s, f"w2q{q}", 2048 * q, 512 * cg))
        pi3 = [0]

        def wl3(i):
            if i < len(plan3):
                scr, nm, r0, c0 = plan3[i]
                load_wpiece(wsl, i % 3, scr, nm, r0, c0, 512)

        def next_piece3():
            i = pi3[0]
            pi3[0] += 1
            wl3(i + 2)
            return i % 3

        wl3(0)
        wl3(1)
        drain(p3_prologue(0))
        for c in range(n_chunks):
            tok0 = 512 * c
            for cg in range(4):
                slot = next_piece3()
                for tt in range(4):
                    bk = mrot.next()
                    xr = xrot.next()
                    r0 = tok0 + 128 * tt
                    S.dma("sp", f"xres{xr}", xres[xr][:, :], xc[HALO + r0:HALO + r0 + 128, 512 * cg:512 * cg + 512],
                          writes=[f"xres{xr}"])
                    for kc in range(16):
                        S.op("pe", MM(PB[bk][:, :], mixT[:, kc, tt * 128:(tt + 1) * 128], wsl[slot][:, kc, :],
                                      kc == 0, kc == 15),
                             reads=[f"w{slot}", f"mixT{tt}_{kc // 4}"], writes=[f"ps{bk}"])
                    S.op("dve", TT(x1[:, tt, 512 * cg:512 * cg + 512], PB[bk][:, :], xres[xr][:, :], ALU.add),
                         reads=[f"ps{bk}", f"xres{xr}"], writes=[f"x1_{tt}"])
            for tt in range(4):
                s = tt % 2
                S.op("act", ACTF(junk[:], x1[:, tt, :], AF.Square, accum=ss3[:, 2:3]),
                     reads=[f"x1_{tt}"], writes=["ss3c"])
                rstd_ops(ss3[:, 2:3], rs3[:, 2:3], D, ["ss3c"], ["rs3c"])
                S.op("dve", STT(h2b[s][:], x1[:, tt, :], rs3[:, 2:3], gml[:], ALU.mult, ALU.mult),
                     reads=[f"x1_{tt}", "rs3c", "gml", "misc3"], writes=[f"h2b{s}"])
                transposes(h2b[s], f"h2b{s}", h2T, lambda g, tt=tt: f"h2T{tt}_{g}", tt * 128, trot, evrot)
            nxt = p3_prologue(c + 1) if c + 1 < n_chunks else None
            for q in range(4):
                for gp in range(4):
                    slot = next_piece3()
                    for j in range(4):
                        bk = mrot.next()
                        f = 4 * gp + j
                        for kc in range(16):
                            S.op("pe", MM(PB[bk][:, :], wsl[slot][:, kc, 128 * j:128 * j + 128], h2T[:, kc, :],
                                          kc == 0, kc == 15),
                                 reads=[f"w{slot}"] + [f"h2T{t}_{kc // 4}" for t in range(4)],
                                 writes=[f"ps{bk}"])
                        rr = rrot.next()
                        S.op("act", ACTF(r32[rr][:, :], PB[bk][:, :], AF.Relu), reads=[f"ps{bk}"],
                             writes=[f"r32_{rr}"])
                        S.op("pool", TT(uT[:, f, :], r32[rr][:, :], r32[rr][:, :], ALU.mult),
                             reads=[f"r32_{rr}"], writes=[f"uT{f}"])
                    if nxt is not None:
                        next(nxt, None)
                for cg in range(4):
                    slot = next_piece3()
                    for tt in range(4):
                        bk = mrot.next()
                        for f in range(16):
                            S.op("pe", MM(PB[bk][:, :], uT[:, f, tt * 128:(tt + 1) * 128], wsl[slot][:, f, :],
                                          f == 0, f == 15),
                                 reads=[f"w{slot}", f"uT{f}"], writes=[f"ps{bk}"])
                        xv = x1[:, tt, 512 * cg:512 * cg + 512]
                        S.op("dve", TT(xv, PB[bk][:, :], xv, ALU.add), reads=[f"ps{bk}", f"x1_{tt}"],
                             writes=[f"x1_{tt}"])
                    if nxt is not None:
                        next(nxt, None)
            drain(nxt)
            for tt in range(4):
                S.op("act", ACTF(junk[:], x1[:, tt, :], AF.Square, accum=ss3[:, 3:4]),
                     reads=[f"x1_{tt}"], writes=["ss3d"])
                rstd_ops(ss3[:, 3:4], rs3[:, 3:4], D, ["ss3d"], ["rs3d"])
                S.op("dve", STT(x1[:, tt, :], x1[:, tt, :], rs3[:, 3:4], gfi[:], ALU.mult, ALU.mult),
                     reads=[f"x1_{tt}", "rs3d", "gfi", "misc3"], writes=[f"x1_{tt}"])
                r0 = tok0 + 128 * tt
                S.dma("pool", f"outst{tt}", out[r0:r0 + 128, :], x1[:, tt, :], reads=[f"x1_{tt}"],
                      writes=[f"outd{tt}"])
        S.barrier()

    finals = [c for c in S.chan if c.startswith(("outst", "ost", "qst", "vst", "cvd"))]
    S.emit(final_wait_chans=finals)
    return nc, S


_CACHE = {}


def _core_inputs(x, hf_first, b, hf):
    xc = np.zeros((NTOK, D), np.float32)
    if hf == 0:
        xc[HALO:] = x[b, 0:NOWN]
    else:
        xc[:] = x[b, NOWN - HALO:2 * NOWN]
    return xc


def kernel(x, g_attn, w_in, b_in, sinks_a, g_out_a, g_out_b, w_out, g_mlp, w_1, w_2, g_final):
    x = np.asarray(x, np.float32)
    if "nc" not in _CACHE:
        _CACHE["nc"] = build()[0]
    nc = _CACHE["nc"]
    shared = {
        "w_in": np.ascontiguousarray(np.asarray(w_in, np.float32)[0]),
        "w_out": np.ascontiguousarray(np.asarray(w_out, np.float32)[0]),
        "w_1": np.ascontiguousarray(np.asarray(w_1, np.float32)[0]),
        "w_2": np.ascontiguousarray(np.asarray(w_2, np.float32)[0]),
        "g_attn": np.ascontiguousarray(np.asarray(g_attn, np.float32)[0]),
        "b_in": np.ascontiguousarray(np.asarray(b_in, np.float32)[0]),
        "sinks": np.ascontiguousarray(np.asarray(sinks_a, np.float32)[0]),
        "g_oab": np.concatenate([np.asarray(g_out_a, np.float32)[0], np.asarray(g_out_b, np.float32)[0]]),
        "g_mlp": np.ascontiguousarray(np.asarray(g_mlp, np.float32)[0]),
        "g_fin": np.ascontiguousarray(np.asarray(g_final, np.float32)),
    }
    in_maps = []
    for core in range(N_CORES):
        b, hf = core // 2, core % 2
        m = dict(shared)
        m["xc"] = _core_inputs(x, None, b, hf)
        m["hb"] = np.full((128, 1), NEG if hf == 0 else 0.0, np.float32)
        in_maps.append(m)
    res = run_bass_kernel_spmd(nc, in_maps, core_ids=list(range(N_CORES)))
    outp = np.empty((4, 2 * NOWN, D), np.float32)
    for core in range(N_CORES):
        b, hf = core // 2, core % 2
        outp[b, hf * NOWN:(hf + 1) * NOWN] = res.results[core]["out"]
    return outp
```
